# Optimizing a Trainium2 kernel written in Bass

```python
import math
import jax, jax.numpy as jnp
from jax import lax
import numpy as np


D_MODEL = 2048
BATCH = 4
SEQ = 4096
DEPTH = 2

CHUNK = 64
EPS = 1e-6
D_FF = ((8 * D_MODEL) // 3 + 255) // 256 * 256

POOL_WINDOWS = (2, 4, 8, 16)
POOL_WIDTH = D_MODEL // 4
POOL_GROUP = POOL_WIDTH // len(POOL_WINDOWS)
SB_HEAD_DIM = 128
SB_HEADS = (D_MODEL - POOL_WIDTH) // SB_HEAD_DIM
SB_WIDTH = SB_HEADS * SB_HEAD_DIM
SB_BLOCK = 128
EVEN_IN = POOL_WIDTH + 3 * SB_WIDTH
EVEN_MIX = POOL_WIDTH + SB_WIDTH

SSM_WIDTH = D_MODEL // 2
SSM_GROUP = 16
SSM_GROUPS = SSM_WIDTH // SSM_GROUP
SSM_STATE = 64
SGU_WIDTH = D_MODEL // 2
SGU_HEADS = 8
SGU_HEAD_DIM = SGU_WIDTH // SGU_HEADS
SGU_LEN = 128
ODD_IN = SSM_WIDTH + 2 * SGU_WIDTH
ODD_MIX = SSM_WIDTH + SGU_WIDTH

N_EVEN = (DEPTH + 1) // 2
N_ODD = DEPTH // 2

kernel_name = 'hybrid_pool_stickbreak_s5_sgu_macaron'


def rmsnorm(x, g):
    xf = x.astype(jnp.float32)
    y = xf * lax.rsqrt(jnp.mean(xf * xf, axis=-1, keepdims=True) + EPS)
    return (y * g.astype(jnp.float32)).astype(x.dtype)


def swiglu(x, w_gate, w_up, w_down):
    return (jax.nn.silu(x @ w_gate) * (x @ w_up)) @ w_down


def pool_mixer(u, w_group, scale):
    Bsz, L, _ = u.shape
    uf = u.astype(jnp.float32)
    cs = jnp.pad(jnp.cumsum(uf, axis=1), ((0, 0), (1, 0), (0, 0)))
    count = jnp.arange(1, L + 1, dtype=jnp.float32)[None, :, None]
    means = []
    for gi, w in enumerate(POOL_WINDOWS):
        c = cs[..., gi * POOL_GROUP:(gi + 1) * POOL_GROUP]
        lo = jnp.pad(c[:, :L + 1 - w], ((0, 0), (w - 1, 0), (0, 0)))
        means.append((c[:, 1:] - lo) / jnp.minimum(count, float(w)))
    pooled = jnp.concatenate(means, axis=-1) - uf
    pooled = pooled.reshape(Bsz, L, len(POOL_WINDOWS), POOL_GROUP)
    y = jnp.einsum('blgc,gcd->blgd', pooled, w_group).reshape(Bsz, L, POOL_WIDTH)
    return (y * scale).astype(u.dtype)


def stick_breaking(q, k, v):
    L = q.shape[2]
    scale = SB_HEAD_DIM ** -0.5
    outs = []
    for i in range(L // SB_BLOCK):
        q0 = i * SB_BLOCK
        kend = q0 + SB_BLOCK
        qb = q[:, :, q0:kend]
        kb = k[:, :, :kend]
        vb = v[:, :, :kend]
        z = jnp.einsum('bhqd,bhkd->bhqk', qb, kb).astype(jnp.float32) * scale
        t_pos = q0 + jnp.arange(SB_BLOCK)[:, None]
        s_pos = jnp.arange(kend)[None, :]
        mask = s_pos < t_pos
        log_keep = jnp.where(mask, jax.nn.log_sigmoid(-z), 0.0)
        later = lax.cumsum(log_keep, axis=3, reverse=True) - log_keep
        w = jnp.where(mask, jnp.exp(jax.nn.log_sigmoid(z) + later), 0.0)
        outs.append(jnp.einsum('bhqk,bhkd->bhqd', w.astype(vb.dtype), vb))
    return jnp.concatenate(outs, axis=2)


def even_mixer(h, w_in, pool_w, pool_scale, w_out):
    Bsz, L, _ = h.shape
    p = h @ w_in
    u_pool = p[..., :POOL_WIDTH]
    qkv = p[..., POOL_WIDTH:].reshape(Bsz, L, 3, SB_HEADS, SB_HEAD_DIM)
    q = jnp.transpose(qkv[:, :, 0], (0, 2, 1, 3))
    k = jnp.transpose(qkv[:, :, 1], (0, 2, 1, 3))
    v = jnp.transpose(qkv[:, :, 2], (0, 2, 1, 3))
    y_pool = pool_mixer(u_pool, pool_w, pool_scale)
    y_sb = jnp.transpose(stick_breaking(q, k, v), (0, 2, 1, 3)).reshape(Bsz, L, SB_WIDTH)
    return jnp.concatenate([y_pool, y_sb.astype(y_pool.dtype)], axis=-1) @ w_out


def s5_ssm(u, a_re, a_im, log_dt, b_re, b_im, c_re, c_im, d_skip, glu_w, glu_b):
    Bsz, L, _ = u.shape
    f32 = jnp.float32
    uf = u.astype(f32).reshape(Bsz, L, SSM_GROUPS, SSM_GROUP)
    a_re = a_re.astype(f32)
    a_im = a_im.astype(f32)
    b_re = b_re.astype(f32)
    b_im = b_im.astype(f32)
    dt = jnp.exp(log_dt.astype(f32))[:, None]
    mag = jnp.exp(a_re * dt)
    lam_re = mag * jnp.cos(a_im * dt)
    lam_im = mag * jnp.sin(a_im * dt)
    den = a_re * a_re + a_im * a_im
    nr = lam_re - 1.0
    f_re = (nr * a_re + lam_im * a_im) / den
    f_im = (lam_im * a_re - nr * a_im) / den
    bb_re = f_re[..., None] * b_re - f_im[..., None] * b_im
    bb_im = f_re[..., None] * b_im + f_im[..., None] * b_re
    bu_re = jnp.einsum('gnc,blgc->lbgn', bb_re, uf)
    bu_im = jnp.einsum('gnc,blgc->lbgn', bb_im, uf)
    lr = jnp.broadcast_to(lam_re[None, None], (L, 1, SSM_GROUPS, SSM_STATE))
    li = jnp.broadcast_to(lam_im[None, None], (L, 1, SSM_GROUPS, SSM_STATE))

    def combine(left, right):
        ar1, ai1, br1, bi1 = left
        ar2, ai2, br2, bi2 = right
        return (ar2 * ar1 - ai2 * ai1, ar2 * ai1 + ai2 * ar1,
                ar2 * br1 - ai2 * bi1 + br2, ar2 * bi1 + ai2 * br1 + bi2)

    _, _, s_re, s_im = lax.associative_scan(combine, (lr, li, bu_re, bu_im), axis=0)
    y = (jnp.einsum('gcn,lbgn->blgc', c_re.astype(f32), s_re)
         - jnp.einsum('gcn,lbgn->blgc', c_im.astype(f32), s_im))
    y = y.reshape(Bsz, L, SSM_WIDTH) + d_skip.astype(f32) * uf.reshape(Bsz, L, SSM_WIDTH)
    y = jax.nn.gelu(y)
    y = y * jax.nn.sigmoid(y @ glu_w.astype(f32) + glu_b.astype(f32))
    return y.astype(u.dtype)


def spatial_gating(z, norm_g, w_s, b_s):
    Bsz, L, _ = z.shape
    z = jax.nn.gelu(z)
    u = z[..., :SGU_WIDTH]
    v = rmsnorm(z[..., SGU_WIDTH:], norm_g)
    v = v.reshape(Bsz, L // SGU_LEN, SGU_LEN, SGU_HEADS, SGU_HEAD_DIM)
    w_causal = jnp.tril(w_s)
    mixed = jnp.einsum('hts,bnshd->bnthd', w_causal, v) + jnp.transpose(b_s)[None, None, :, :, None]
    u = u.reshape(Bsz, L // SGU_LEN, SGU_LEN, SGU_HEADS, SGU_HEAD_DIM)
    return (u * mixed).reshape(Bsz, L, SGU_WIDTH)


def odd_mixer(h, w_in, a_re, a_im, log_dt, b_re, b_im, c_re, c_im, d_skip, glu_w, glu_b,
              sgu_norm_g, sgu_w, sgu_b, w_out):
    p = h @ w_in
    y_ssm = s5_ssm(p[..., :SSM_WIDTH], a_re, a_im, log_dt, b_re, b_im, c_re, c_im,
                   d_skip, glu_w, glu_b)
    y_sgu = spatial_gating(p[..., SSM_WIDTH:], sgu_norm_g, sgu_w, sgu_b)
    return jnp.concatenate([y_ssm, y_sgu.astype(y_ssm.dtype)], axis=-1) @ w_out


def setup_inputs(seed: int = 0) -> dict:
    key = jax.random.key(seed)
    ks = jax.random.split(key, 24)
    nrm = jax.random.normal
    f32 = jnp.float32
    G, N, C = SSM_GROUPS, SSM_STATE, SSM_GROUP
    x = nrm(ks[0], (BATCH, SEQ, D_MODEL), f32)
    norm_g = 1.0 + 0.02 * nrm(ks[1], (DEPTH, 6, D_MODEL), f32)
    ffn_w_gate = nrm(ks[2], (DEPTH, 2, D_MODEL, D_FF), f32) * D_MODEL ** -0.5
    ffn_w_up = nrm(ks[3], (DEPTH, 2, D_MODEL, D_FF), f32) * D_MODEL ** -0.5
    ffn_w_down = nrm(ks[4], (DEPTH, 2, D_FF, D_MODEL), f32) * D_FF ** -0.5
    ev_w_in = nrm(ks[5], (N_EVEN, D_MODEL, EVEN_IN), f32) * D_MODEL ** -0.5
    ev_pool_w = nrm(ks[6], (N_EVEN, len(POOL_WINDOWS), POOL_GROUP, POOL_GROUP), f32) * POOL_GROUP ** -0.5
    ev_pool_scale = 1.0 + 0.02 * nrm(ks[7], (N_EVEN, POOL_WIDTH), f32)
    ev_w_out = nrm(ks[8], (N_EVEN, EVEN_MIX, D_MODEL), f32) * EVEN_MIX ** -0.5
    od_w_in = nrm(ks[9], (N_ODD, D_MODEL, ODD_IN), f32) * D_MODEL ** -0.5
    od_ssm_a_re = -0.5 + 0.01 * nrm(ks[10], (N_ODD, G, N), f32)
    od_ssm_a_im = (math.pi * jnp.arange(N, dtype=f32))[None, None, :] + 0.01 * nrm(ks[11], (N_ODD, G, N), f32)
    od_ssm_log_dt = jax.random.uniform(ks[12], (N_ODD, G), f32, math.log(1e-3), math.log(1e-1))
    od_ssm_b_re = nrm(ks[13], (N_ODD, G, N, C), f32) * (2.0 * C) ** -0.5
    od_ssm_b_im = nrm(ks[14], (N_ODD, G, N, C), f32) * (2.0 * C) ** -0.5
    od_ssm_c_re = nrm(ks[15], (N_ODD, G, C, N), f32) * (2.0 * N) ** -0.5
    od_ssm_c_im = nrm(ks[16], (N_ODD, G, C, N), f32) * (2.0 * N) ** -0.5
    od_ssm_d = nrm(ks[17], (N_ODD, SSM_WIDTH), f32)
    od_glu_w = nrm(ks[18], (N_ODD, SSM_WIDTH, SSM_WIDTH), f32) * SSM_WIDTH ** -0.5
    od_glu_b = 0.01 * nrm(ks[19], (N_ODD, SSM_WIDTH), f32)
    od_sgu_norm_g = 1.0 + 0.02 * nrm(ks[20], (N_ODD, SGU_WIDTH), f32)
    od_sgu_w = nrm(ks[21], (N_ODD, SGU_HEADS, SGU_LEN, SGU_LEN), f32) * SGU_LEN ** -0.5
    od_sgu_b = 1.0 + 0.02 * nrm(ks[22], (N_ODD, SGU_HEADS, SGU_LEN), f32)
    od_w_out = nrm(ks[23], (N_ODD, ODD_MIX, D_MODEL), f32) * ODD_MIX ** -0.5
    return {'x': x, 'norm_g': norm_g, 'ffn_w_gate': ffn_w_gate, 'ffn_w_up': ffn_w_up,
            'ffn_w_down': ffn_w_down, 'ev_w_in': ev_w_in, 'ev_pool_w': ev_pool_w,
            'ev_pool_scale': ev_pool_scale, 'ev_w_out': ev_w_out, 'od_w_in': od_w_in,
            'od_ssm_a_re': od_ssm_a_re, 'od_ssm_a_im': od_ssm_a_im, 'od_ssm_log_dt': od_ssm_log_dt,
            'od_ssm_b_re': od_ssm_b_re, 'od_ssm_b_im': od_ssm_b_im, 'od_ssm_c_re': od_ssm_c_re,
            'od_ssm_c_im': od_ssm_c_im, 'od_ssm_d': od_ssm_d, 'od_glu_w': od_glu_w,
            'od_glu_b': od_glu_b, 'od_sgu_norm_g': od_sgu_norm_g, 'od_sgu_w': od_sgu_w,
            'od_sgu_b': od_sgu_b, 'od_w_out': od_w_out}


def reference(x, norm_g, ffn_w_gate, ffn_w_up, ffn_w_down, ev_w_in, ev_pool_w, ev_pool_scale,
              ev_w_out, od_w_in, od_ssm_a_re, od_ssm_a_im, od_ssm_log_dt, od_ssm_b_re, od_ssm_b_im,
              od_ssm_c_re, od_ssm_c_im, od_ssm_d, od_glu_w, od_glu_b, od_sgu_norm_g, od_sgu_w,
              od_sgu_b, od_w_out):
    for i in range(DEPTH):
        g = norm_g[i]
        f = swiglu(rmsnorm(x, g[0]), ffn_w_gate[i, 0], ffn_w_up[i, 0], ffn_w_down[i, 0])
        x = x + 0.5 * rmsnorm(f, g[1])
        h = rmsnorm(x, g[2])
        if i % 2 == 0:
            j = i // 2
            m = even_mixer(h, ev_w_in[j], ev_pool_w[j], ev_pool_scale[j], ev_w_out[j])
        else:
            j = i // 2
            m = odd_mixer(h, od_w_in[j], od_ssm_a_re[j], od_ssm_a_im[j], od_ssm_log_dt[j],
                          od_ssm_b_re[j], od_ssm_b_im[j], od_ssm_c_re[j], od_ssm_c_im[j],
                          od_ssm_d[j], od_glu_w[j], od_glu_b[j], od_sgu_norm_g[j],
                          od_sgu_w[j], od_sgu_b[j], od_w_out[j])
        x = x + rmsnorm(m.astype(x.dtype), g[3])
        f = swiglu(rmsnorm(x, g[4]), ffn_w_gate[i, 1], ffn_w_up[i, 1], ffn_w_down[i, 1])
        x = x + 0.5 * rmsnorm(f, g[5])
    return x
```

```python
import math
import ml_dtypes
from concourse.bass_utils import run_bass_kernel_spmd
import numpy as np
from contextlib import ExitStack
import concourse.bass as bass
import concourse.mybir as mybir

F32 = mybir.dt.float32
BF16 = mybir.dt.bfloat16
ALU = mybir.AluOpType
AF = mybir.ActivationFunctionType
AX = mybir.AxisListType

ENGS = ["pe", "dve", "act", "pool", "sp"]
NRING = 16
RING_N = {"sp": 24, "act": 16, "pool": 16}


class T:
    def __init__(self, h, name):
        self.h = h
        self.name = name
        self.w = {}
        self.r = {}

    def __getitem__(self, idx):
        return self.h[idx]


class Prog:
    def __init__(self, nc, es):
        self.nc = nc
        self.es = es
        self.q = {e: [] for e in ENGS}
        self.esem = {e: es.enter_context(nc.semaphore(f"s_{e}")) for e in ENGS}
        self.ecnt = {e: 0 for e in ENGS}
        self.seen = {e: {} for e in ENGS}
        self.ring = {}
        self.ringcnt = {}
        self.ringpos = {}
        for qn in ["sp", "act", "pool"]:
            self.ring[qn] = [es.enter_context(nc.semaphore(f"d_{qn}{i}")) for i in range(RING_N[qn])]
            self.ringcnt[qn] = [0] * RING_N[qn]
            self.ringpos[qn] = 0
        self.nuniq = 0
        self.out_events = []

    def sb(self, shape, dt, name=None):
        self.nuniq += 1
        name = f"{name or 't'}_{self.nuniq}"
        h = self.es.enter_context(self.nc.sbuf_tensor(name, list(shape), dt))
        return T(h, name)

    def ps(self, shape, dt=F32, name=None):
        self.nuniq += 1
        name = f"{name or 'p'}_{self.nuniq}"
        h = self.es.enter_context(self.nc.psum_tensor(name, list(shape), dt))
        return T(h, name)

    def dram(self, name, shape, dt, kind):
        h = self.nc.dram_tensor(name, list(shape), dt, kind=kind)
        return T(h.ap() if hasattr(h, "ap") else h, name)

    def _collect(self, eng, reads, writes):
        waits = []
        for t in list(reads) + list(writes):
            for sem, (val, src) in t.w.items():
                waits.append((sem, val, src))
        for t in writes:
            for sem, (val, src) in t.r.items():
                waits.append((sem, val, src))
        need = {}
        seen = self.seen[eng]
        for (sem, val, src) in waits:
            if eng == "pe" and src == "pe":
                continue
            if seen.get(sem, 0) >= val:
                continue
            if need.get(sem, (0,))[0] < val:
                need[sem] = (val,)
        out = []
        for sem, (val,) in need.items():
            seen[sem] = val
            out.append((sem, val))
        return out

    def _commit(self, ev, reads, writes):
        sem, val, src = ev
        for t in reads:
            t.r[sem] = (val, src)
        for t in writes:
            t.w[sem] = (val, src)

    def op(self, eng, fn, reads=(), writes=()):
        waits = self._collect(eng, reads, writes)
        self.ecnt[eng] += 1
        ev = (self.esem[eng], self.ecnt[eng], eng)
        self.q[eng].append(("op", waits, [fn], (self.esem[eng], 1)))
        self._commit(ev, reads, writes)
        return ev

    def group(self, eng, fns, reads=(), writes=()):
        waits = self._collect(eng, reads, writes)
        self.ecnt[eng] += 1
        ev = (self.esem[eng], self.ecnt[eng], eng)
        self.q[eng].append(("op", waits, list(fns), (self.esem[eng], 1)))
        self._commit(ev, reads, writes)
        return ev

    def dma(self, qn, out_t, out_ap, in_t, in_ap, is_output=False, **kw):
        reads = [in_t]
        writes = [out_t]
        waits = self._collect(qn, reads, writes)
        pos = self.ringpos[qn]
        self.ringpos[qn] = (pos + 1) % RING_N[qn]
        sem = self.ring[qn][pos]
        prev = self.ringcnt[qn][pos]
        if prev > 0 and self.seen[qn].get(sem, 0) < prev:
            waits.append((sem, prev))
            self.seen[qn][sem] = prev
        self.ringcnt[qn][pos] = prev + 16
        ev = (sem, prev + 16, "dma")

        def fn(e, out_ap=out_ap, in_ap=in_ap, kw=kw):
            return e.dma_start(out=out_ap, in_=in_ap, **kw)

        self.q[qn].append(("op", waits, [fn], (sem, 16)))
        self._commit(ev, reads, writes)
        if is_output:
            self.out_events.append(ev)
        return ev

    def collective(self, kind, out_t, out_ap, in_t, in_ap, groups):
        qn = "pool"
        waits = self._collect(qn, [in_t], [out_t])
        pos = self.ringpos[qn]
        self.ringpos[qn] = (pos + 1) % RING_N[qn]
        sem = self.ring[qn][pos]
        prev = self.ringcnt[qn][pos]
        if prev > 0 and self.seen[qn].get(sem, 0) < prev:
            waits.append((sem, prev))
            self.seen[qn][sem] = prev
        self.ringcnt[qn][pos] = prev + 16
        ev = (sem, prev + 16, "dma")

        def fn(e):
            return e.collective_compute(kind, ALU.bypass, replica_groups=groups, ins=[in_ap], outs=[out_ap])

        self.q[qn].append(("op", waits, [fn], (sem, 16)))
        self._commit(ev, [in_t], [out_t])
        return ev

    def barrier(self):
        targets = [(self.esem[e], self.ecnt[e]) for e in ENGS if self.ecnt[e] > 0]
        for qn in self.ring:
            for i in range(RING_N[qn]):
                if self.ringcnt[qn][i] > 0:
                    targets.append((self.ring[qn][i], self.ringcnt[qn][i]))
        for e in ENGS:
            waits = [(s, v) for (s, v) in targets if self.seen[e].get(s, 0) < v]
            for s, v in waits:
                self.seen[e][s] = v
            self.q[e].append(("wait", waits, [], None))

    def finish(self):
        need = {}
        for (sem, val, _) in self.out_events:
            need[sem] = max(need.get(sem, 0), val)
        self.q["sp"].append(("wait", list(need.items()), [], None))

    def emit(self):
        nc = self.nc
        eobj = {"pe": "tensor", "dve": "vector", "act": "scalar", "pool": "gpsimd", "sp": "sync"}
        with nc.Block() as block:
            for en in ENGS:
                items = self.q[en]

                def body(e, items=items):
                    for (_, waits, fns, inc) in items:
                        for (sem, val) in waits:
                            e.wait_ge(sem, val)
                        last = None
                        for f in fns:
                            last = f(e)
                        if inc is not None:
                            last.then_inc(inc[0], inc[1])

                getattr(block, eobj[en])(body)


D = 2048
KC = 16
FF = 5632
FT = 44
EPS = 1e-6


class Common:
    def __init__(self, P):
        self.P = P
        self.ones = P.sb([128, 128], BF16, "ones")
        self.eps = P.sb([128, 1], F32, "eps")
        P.op("pool", lambda e: e.memset(self.ones[:], 1.0), writes=[self.ones])
        P.op("pool", lambda e: e.memset(self.eps[:], EPS), writes=[self.eps])
        self.banks = [P.ps([128, 512], F32, f"bank{i}") for i in range(8)]
        self.xt = [P.sb([128, 512], F32, "xt") for _ in range(4)]
        self.ft = [P.sb([128, 512], F32, "ftile") for _ in range(3)]
        self.sq = [P.sb([128, 512], BF16, "sq") for _ in range(4)]
        self.tmp = [P.sb([128, 512], F32, "tmp") for _ in range(2)]
        self.rstd = [P.sb([128, 512], F32, "rstd") for _ in range(2)]
        self.rstdA = [P.sb([128, 512], F32, "rstdA") for _ in range(2)]
        self.sqS = [P.sb([128, 512], BF16, "sqS") for _ in range(6)]
        self.hb = [P.sb([128, 512], BF16, "hb") for _ in range(3)]
        self.cnt = {}

    def rot(self, lst, key):
        i = self.cnt.get(key, 0)
        self.cnt[key] = i + 1
        return lst[i % len(lst)]


def emit_rstd(P, C, stats_bank, rstd_t, n=D):
    tmp = C.rot(C.tmp, "tmp")
    P.op("act", lambda e: e.activation(out=tmp[:], in_=stats_bank[:], func=AF.Sqrt,
                                       bias=C.eps[:, 0:1], scale=1.0 / n),
         reads=[stats_bank, C.eps], writes=[tmp])
    P.op("dve", lambda e: e.reciprocal(out=rstd_t[:], in_=tmp[:]), reads=[tmp], writes=[rstd_t])


def emit_ffn(P, C, res, x_in, x_out, wg, wu, wd, g, jpre, jpost, ntok, h_out=None, jnext=None):
    h = res["h"]
    act = res["act"]
    wgs, wus, wds = res["wgs"], res["wus"], res["wds"]
    f_scr = res["f_scr"]
    bk = C.banks
    nblk = ntok // 1024
    for blk in range(nblk):
        for half in range(2):
            hg = blk * 2 + half
            tsl = slice(hg * 512, (hg + 1) * 512)
            hsl = slice(half * 512, (half + 1) * 512)
            st = bk[6 + half]
            for kc in range(KC):
                xt = C.rot(C.xt, "xt")
                P.dma("sp", xt, xt[:], x_in[(kc, hg)], x_in[(kc, hg)].h[kc, :, tsl])
                sq = C.rot(C.sq, "sq")
                P.op("act", lambda e, sq=sq, xt=xt: e.activation(out=sq[:], in_=xt[:], func=AF.Square),
                     reads=[xt], writes=[sq])
                P.op("pe", lambda e, sq=sq, kc=kc, st=st: e.matmul(st[:], lhsT=C.ones[:], rhs=sq[:],
                                                                     start=(kc == 0), stop=(kc == KC - 1)),
                     reads=[sq, C.ones], writes=[st])
            rstd = C.rstd[half]
            emit_rstd(P, C, st, rstd)
            for kc in range(KC):
                xt = C.rot(C.xt, "xt")
                P.dma("sp", xt, xt[:], x_in[(kc, hg)], x_in[(kc, hg)].h[kc, :, tsl])
                P.op("dve", lambda e, xt=xt, kc=kc, rstd=rstd, hsl=hsl: e.scalar_tensor_tensor(
                    out=h[:, kc, hsl], in0=xt[:], scalar=g[:, jpre * 16 + kc: jpre * 16 + kc + 1],
                    in1=rstd[:], op0=ALU.mult, op1=ALU.mult),
                    reads=[xt, rstd, g], writes=[h])
        for ft in range(FT):
            wgt = C.rot(wgs, "wg")
            wut = C.rot(wus, "wu")
            P.dma("pool", wgt, wgt[:], wg, wg.h[ft])
            P.dma("pool", wut, wut[:], wu, wu.h[ft])
            for half in range(2):
                hsl = slice(half * 512, (half + 1) * 512)
                pg = bk[0 + half]
                pu = bk[2 + half]
                P.group("pe", [
                    (lambda e, kc=kc, pg=pg, wgt=wgt, hsl=hsl: e.matmul(
                        pg[:], lhsT=wgt[:, kc, :], rhs=h[:, kc, hsl], start=(kc == 0), stop=(kc == KC - 1)))
                    for kc in range(KC)], reads=[wgt, h], writes=[pg])
                P.group("pe", [
                    (lambda e, kc=kc, pu=pu, wut=wut, hsl=hsl: e.matmul(
                        pu[:], lhsT=wut[:, kc, :], rhs=h[:, kc, hsl], start=(kc == 0), stop=(kc == KC - 1)))
                    for kc in range(KC)], reads=[wut, h], writes=[pu])
                sl = C.rot(C.hb, "hb")
                P.op("act", lambda e, sl=sl, pg=pg: e.activation(out=sl[:], in_=pg[:], func=AF.Silu),
                     reads=[pg], writes=[sl])
                P.op("dve", lambda e, sl=sl, pu=pu, ft=ft, hsl=hsl: e.tensor_tensor(
                    out=act[:, ft, hsl], in0=pu[:], in1=sl[:], op=ALU.mult),
                    reads=[pu, sl], writes=[act])
        pend = None
        for m in range(KC):
            wdt = C.rot(wds, "wd")
            for q in range(4):
                P.dma("pool", wdt, wdt[:, q * 11:(q + 1) * 11, :], wd, wd.h[m, :, q * 11:(q + 1) * 11, :])
            for half in range(2):
                hg = blk * 2 + half
                tsl = slice(hg * 512, (hg + 1) * 512)
                hsl = slice(half * 512, (half + 1) * 512)
                pd = bk[4 + half]
                st = bk[6 + half]
                P.group("pe", [
                    (lambda e, fc=fc, pd=pd, wdt=wdt, hsl=hsl: e.matmul(
                        pd[:], lhsT=wdt[:, fc, :], rhs=act[:, fc, hsl], start=(fc == 0), stop=(fc == FT - 1)))
                    for fc in range(FT)], reads=[wdt, act], writes=[pd])
                ftile = C.rot(C.ft, "ft")
                P.op("act", lambda e, ftile=ftile, pd=pd: e.copy(out=ftile[:], in_=pd[:]),
                     reads=[pd], writes=[ftile])
                sq = C.rot(C.sq, "sq")
                P.op("act", lambda e, sq=sq, ftile=ftile: e.activation(out=sq[:], in_=ftile[:], func=AF.Square),
                     reads=[ftile], writes=[sq])
                if pend is not None:
                    pend()
                pend = (lambda sq=sq, m=m, st=st: P.op("pe", lambda e: e.matmul(
                    st[:], lhsT=C.ones[:], rhs=sq[:], start=(m == 0), stop=(m == KC - 1)),
                    reads=[sq, C.ones], writes=[st]))
                P.dma("act", f_scr[(m, hg)], f_scr[(m, hg)].h[m, :, tsl], ftile, ftile[:])
        if pend is not None:
            pend()
        for half in range(2):
            hg = blk * 2 + half
            tsl = slice(hg * 512, (hg + 1) * 512)
            st = bk[6 + half]
            rstd = C.rstd[half]
            emit_rstd(P, C, st, rstd)
            st2 = bk[4 + half]
            for kc in range(KC):
                ftile = C.rot(C.ft, "ft")
                xt = C.rot(C.xt, "xt")
                P.dma("sp", ftile, ftile[:], f_scr[(kc, hg)], f_scr[(kc, hg)].h[kc, :, tsl])
                P.dma("sp", xt, xt[:], x_in[(kc, hg)], x_in[(kc, hg)].h[kc, :, tsl])
                P.op("dve", lambda e, ftile=ftile, kc=kc, rstd=rstd: e.scalar_tensor_tensor(
                    out=ftile[:], in0=ftile[:], scalar=g[:, jpost * 16 + kc: jpost * 16 + kc + 1],
                    in1=rstd[:], op0=ALU.mult, op1=ALU.mult),
                    reads=[rstd, g], writes=[ftile])
                P.op("dve", lambda e, ftile=ftile, xt=xt: e.scalar_tensor_tensor(
                    out=xt[:], in0=ftile[:], scalar=0.5, in1=xt[:], op0=ALU.mult, op1=ALU.add),
                    reads=[ftile], writes=[xt])
                P.dma("act", x_out[(kc, hg)], x_out[(kc, hg)].h[kc, :, tsl], xt, xt[:],
                      is_output=x_out.get("is_output", False))
                if h_out is not None:
                    sq = C.rot(C.sq, "sq")
                    P.op("act", lambda e, sq=sq, xt=xt: e.activation(out=sq[:], in_=xt[:], func=AF.Square),
                         reads=[xt], writes=[sq])
                    P.op("pe", lambda e, sq=sq, kc=kc, st2=st2: e.matmul(
                        st2[:], lhsT=C.ones[:], rhs=sq[:], start=(kc == 0), stop=(kc == KC - 1)),
                        reads=[sq, C.ones], writes=[st2])
            if h_out is not None:
                emit_rstd(P, C, st2, rstd)
                for kc in range(KC):
                    xt = C.rot(C.xt, "xt")
                    P.dma("sp", xt, xt[:], x_out[(kc, hg)], x_out[(kc, hg)].h[kc, :, tsl])
                    hb = C.rot(C.hb, "hb")
                    P.op("dve", lambda e, xt=xt, kc=kc, rstd=rstd, hb=hb: e.scalar_tensor_tensor(
                        out=hb[:], in0=xt[:], scalar=g[:, jnext * 16 + kc: jnext * 16 + kc + 1],
                        in1=rstd[:], op0=ALU.mult, op1=ALU.mult),
                        reads=[xt, rstd, g], writes=[hb])
                    P.dma("act", h_out[(kc, hg)], h_out[(kc, hg)].h[kc, :, tsl], hb, hb[:],
                          is_output=h_out.get("is_output", False))


def regions(P, name, shape, dt, kind, nk, nh):
    base = P.dram(name, shape, dt, kind)
    d = {}
    for k in range(nk):
        for hh in range(nh):
            d[(k, hh)] = T(base.h, f"{name}_{k}_{hh}")
    d["is_output"] = (kind == "ExternalOutput")
    d["base"] = base
    return d


def ffn_resources(P, ntok, scr_name="f_scr", f_scr=None):
    res = {}
    res["h"] = P.sb([128, KC, 1024], BF16, "h")
    res["act"] = P.sb([128, FT, 1024], BF16, "act")
    res["wgs"] = [P.sb([128, KC, 128], BF16, "wg") for _ in range(3)]
    res["wus"] = [P.sb([128, KC, 128], BF16, "wu") for _ in range(3)]
    res["wds"] = [P.sb([128, FT, 128], BF16, "wd") for _ in range(2)]
    res["f_scr"] = f_scr if f_scr is not None else regions(P, scr_name, [KC, 128, ntok], F32, "Internal", KC, ntok // 512)
    return res


def emit_outproj(P, C, res, ym, wo, x_in, x_out, g, jpost, ntok):
    f_scr = res["f_scr"]
    wgs = res["wgs"]
    bk = C.banks
    for c in range(ntok // 512):
        tsl = slice(c * 512, (c + 1) * 512)
        yt = ym(c)
        st = bk[6 + (c % 2)]
        pend = None
        for m in range(KC):
            wt = C.rot(wgs, "wg")
            P.dma("pool", wt, wt[:], wo, wo.h[m])
            pd = bk[4 + (m % 2)]
            P.group("pe", [(lambda e, kc=kc, pd=pd, wt=wt, yt=yt: e.matmul(
                pd[:], lhsT=wt[:, kc, :], rhs=yt[:, kc, :], start=(kc == 0), stop=(kc == KC - 1)))
                for kc in range(KC)], reads=[wt, yt], writes=[pd])
            ftile = C.rot(C.ft, "ft")
            P.op("act", lambda e, ftile=ftile, pd=pd: e.copy(out=ftile[:], in_=pd[:]), reads=[pd], writes=[ftile])
            sq = C.rot(C.sq, "sq")
            P.op("act", lambda e, sq=sq, ftile=ftile: e.activation(out=sq[:], in_=ftile[:], func=AF.Square),
                 reads=[ftile], writes=[sq])
            if pend is not None:
                pend()
            pend = (lambda sq=sq, m=m, st=st: P.op("pe", lambda e: e.matmul(
                st[:], lhsT=C.ones[:], rhs=sq[:], start=(m == 0), stop=(m == KC - 1)),
                reads=[sq, C.ones], writes=[st]))
            P.dma("act", f_scr[(m, c)], f_scr[(m, c)].h[m, :, tsl], ftile, ftile[:])
        if pend is not None:
            pend()
        rstd = C.rstd[c % 2]
        emit_rstd(P, C, st, rstd)
        for kc in range(KC):
            ftile = C.rot(C.ft, "ft")
            xt = C.rot(C.xt, "xt")
            P.dma("sp", ftile, ftile[:], f_scr[(kc, c)], f_scr[(kc, c)].h[kc, :, tsl])
            P.dma("sp", xt, xt[:], x_in[(kc, c)], x_in[(kc, c)].h[kc, :, tsl])
            P.op("dve", lambda e, ftile=ftile, kc=kc, rstd=rstd: e.scalar_tensor_tensor(
                out=ftile[:], in0=ftile[:], scalar=g[:, jpost * 16 + kc: jpost * 16 + kc + 1],
                in1=rstd[:], op0=ALU.mult, op1=ALU.mult), reads=[rstd, g], writes=[ftile])
            P.op("dve", lambda e, ftile=ftile, xt=xt: e.tensor_tensor(
                out=xt[:], in0=ftile[:], in1=xt[:], op=ALU.add), reads=[ftile], writes=[xt])
            P.dma("act", x_out[(kc, c)], x_out[(kc, c)].h[kc, :, tsl], xt, xt[:],
                  is_output=x_out.get("is_output", False))


FLUSH = "FLUSH"


def emit_ffn_pipelined(P, C, res, x_in, x_out, wg, wu, wd, g, jpre, jpost, ntok, h_out=None, jnext=None):
    h = res["h"]
    act = res["act"]
    wgs, wus, wds = res["wgs"], res["wus"], res["wds"]
    f_scr = res["f_scr"]
    bk = C.banks
    nblk = ntok // 1024
    rstdA = C.rstdA

    def stats_mm(st, sq, first, last):
        return lambda: P.op("pe", lambda e: e.matmul(st[:], lhsT=C.ones[:], rhs=sq[:], start=first, stop=last),
                            reads=[sq, C.ones], writes=[st])

    def stageA(blk):
        for half in range(2):
            hg = blk * 2 + half
            tsl = slice(hg * 512, (hg + 1) * 512)
            hsl = slice(half * 512, (half + 1) * 512)
            st = bk[0 + half]
            for kc in range(KC):
                xt = C.rot(C.xt, "xt")
                P.dma("sp", xt, xt[:], x_in[(kc, hg)], x_in[(kc, hg)].h[kc, :, tsl])
                sq = C.rot(C.sqS, "sqS")
                P.op("act", lambda e, sq=sq, xt=xt: e.activation(out=sq[:], in_=xt[:], func=AF.Square),
                     reads=[xt], writes=[sq])
                yield stats_mm(st, sq, kc == 0, kc == KC - 1)
            yield FLUSH
            rstd = rstdA[half]
            emit_rstd(P, C, st, rstd)
            for kc in range(KC):
                xt = C.rot(C.xt, "xt")
                P.dma("sp", xt, xt[:], x_in[(kc, hg)], x_in[(kc, hg)].h[kc, :, tsl])
                P.op("dve", lambda e, xt=xt, kc=kc, rstd=rstd, hsl=hsl: e.scalar_tensor_tensor(
                    out=h[:, kc, hsl], in0=xt[:], scalar=g[:, jpre * 16 + kc: jpre * 16 + kc + 1],
                    in1=rstd[:], op0=ALU.mult, op1=ALU.mult),
                    reads=[xt, rstd, g], writes=[h])
                yield None

    def stageB(blk):
        for ft in range(FT):
            wgt = C.rot(wgs, "wg")
            wut = C.rot(wus, "wu")
            P.dma("pool", wgt, wgt[:], wg, wg.h[ft])
            P.dma("pool", wut, wut[:], wu, wu.h[ft])
            for half in range(2):
                hsl = slice(half * 512, (half + 1) * 512)
                pg = bk[0 + half]
                pu = bk[2 + half]
                P.group("pe", [
                    (lambda e, kc=kc, pg=pg, wgt=wgt, hsl=hsl: e.matmul(
                        pg[:], lhsT=wgt[:, kc, :], rhs=h[:, kc, hsl], start=(kc == 0), stop=(kc == KC - 1)))
                    for kc in range(KC)], reads=[wgt, h], writes=[pg])
                P.group("pe", [
                    (lambda e, kc=kc, pu=pu, wut=wut, hsl=hsl: e.matmul(
                        pu[:], lhsT=wut[:, kc, :], rhs=h[:, kc, hsl], start=(kc == 0), stop=(kc == KC - 1)))
                    for kc in range(KC)], reads=[wut, h], writes=[pu])
                sl = C.rot(C.hb, "hb")
                P.op("act", lambda e, sl=sl, pg=pg: e.activation(out=sl[:], in_=pg[:], func=AF.Silu),
                     reads=[pg], writes=[sl])
                P.op("dve", lambda e, sl=sl, pu=pu, ft=ft, hsl=hsl: e.tensor_tensor(
                    out=act[:, ft, hsl], in0=pu[:], in1=sl[:], op=ALU.mult),
                    reads=[pu, sl], writes=[act])
            yield None

    def stageC(blk):
        pend = None
        for m in range(KC):
            wdt = C.rot(wds, "wd")
            for q in range(4):
                P.dma("pool", wdt, wdt[:, q * 11:(q + 1) * 11, :], wd, wd.h[m, :, q * 11:(q + 1) * 11, :])
            for half in range(2):
                hg = blk * 2 + half
                tsl = slice(hg * 512, (hg + 1) * 512)
                hsl = slice(half * 512, (half + 1) * 512)
                pd = bk[4 + half]
                st = bk[6 + half]
                P.group("pe", [
                    (lambda e, fc=fc, pd=pd, wdt=wdt, hsl=hsl: e.matmul(
                        pd[:], lhsT=wdt[:, fc, :], rhs=act[:, fc, hsl], start=(fc == 0), stop=(fc == FT - 1)))
                    for fc in range(FT)], reads=[wdt, act], writes=[pd])
                ftile = C.rot(C.ft, "ft")
                P.op("act", lambda e, ftile=ftile, pd=pd: e.copy(out=ftile[:], in_=pd[:]),
                     reads=[pd], writes=[ftile])
                sq = C.rot(C.sq, "sq")
                P.op("act", lambda e, sq=sq, ftile=ftile: e.activation(out=sq[:], in_=ftile[:], func=AF.Square),
                     reads=[ftile], writes=[sq])
                if pend is not None:
                    pend()
                pend = stats_mm(st, sq, m == 0, m == KC - 1)
                P.dma("act", f_scr[(m, hg)], f_scr[(m, hg)].h[m, :, tsl], ftile, ftile[:])
                yield None
        pend()
        yield None

    def stageDE(blk):
        for half in range(2):
            hg = blk * 2 + half
            tsl = slice(hg * 512, (hg + 1) * 512)
            st = bk[6 + half]
            rstd = C.rstd[half]
            emit_rstd(P, C, st, rstd)
            st2 = bk[4 + half]
            for kc in range(KC):
                ftile = C.rot(C.ft, "ft")
                xt = C.rot(C.xt, "xt")
                P.dma("sp", ftile, ftile[:], f_scr[(kc, hg)], f_scr[(kc, hg)].h[kc, :, tsl])
                P.dma("sp", xt, xt[:], x_in[(kc, hg)], x_in[(kc, hg)].h[kc, :, tsl])
                P.op("dve", lambda e, ftile=ftile, kc=kc, rstd=rstd: e.scalar_tensor_tensor(
                    out=ftile[:], in0=ftile[:], scalar=g[:, jpost * 16 + kc: jpost * 16 + kc + 1],
                    in1=rstd[:], op0=ALU.mult, op1=ALU.mult),
                    reads=[rstd, g], writes=[ftile])
                P.op("dve", lambda e, ftile=ftile, xt=xt: e.scalar_tensor_tensor(
                    out=xt[:], in0=ftile[:], scalar=0.5, in1=xt[:], op0=ALU.mult, op1=ALU.add),
                    reads=[ftile], writes=[xt])
                P.dma("act", x_out[(kc, hg)], x_out[(kc, hg)].h[kc, :, tsl], xt, xt[:],
                      is_output=x_out.get("is_output", False))
                if h_out is not None:
                    sq = C.rot(C.sqS, "sqS")
                    P.op("act", lambda e, sq=sq, xt=xt: e.activation(out=sq[:], in_=xt[:], func=AF.Square),
                         reads=[xt], writes=[sq])
                    yield stats_mm(st2, sq, kc == 0, kc == KC - 1)
                else:
                    yield None
            if h_out is not None:
                yield FLUSH
                emit_rstd(P, C, st2, rstd)
                for kc in range(KC):
                    xt = C.rot(C.xt, "xt")
                    P.dma("sp", xt, xt[:], x_out[(kc, hg)], x_out[(kc, hg)].h[kc, :, tsl])
                    hb = C.rot(C.hb, "hb")
                    P.op("dve", lambda e, xt=xt, kc=kc, rstd=rstd, hb=hb: e.scalar_tensor_tensor(
                        out=hb[:], in0=xt[:], scalar=g[:, jnext * 16 + kc: jnext * 16 + kc + 1],
                        in1=rstd[:], op0=ALU.mult, op1=ALU.mult),
                        reads=[xt, rstd, g], writes=[hb])
                    P.dma("act", h_out[(kc, hg)], h_out[(kc, hg)].h[kc, :, tsl], hb, hb[:],
                          is_output=h_out.get("is_output", False))
                    yield None

    def run(main, side, per_iter):
        deferred = []
        side_done = side is None

        def side_step():
            nonlocal side_done
            try:
                r = next(side)
            except StopIteration:
                side_done = True
                return
            if r is FLUSH:
                for t in deferred:
                    t()
                deferred.clear()
            elif r is not None:
                deferred.append(r)

        if main is None:
            while not side_done:
                side_step()
                for t in deferred:
                    t()
                deferred.clear()
            return
        for _ in main:
            for t in deferred:
                t()
            deferred.clear()
            if not side_done:
                for _ in range(per_iter):
                    if side_done:
                        break
                    side_step()
        while not side_done:
            side_step()
            for t in deferred:
                t()
            deferred.clear()
        for t in deferred:
            t()
        deferred.clear()

    run(None, stageA(0), 0)
    for blk in range(nblk):
        run(stageB(blk), stageDE(blk - 1) if blk > 0 else None, 2)
        run(stageC(blk), stageA(blk + 1) if blk + 1 < nblk else None, 2)
    run(None, stageDE(nblk - 1), 0)


L = 4096
KC = 16
SCALE = 128 ** -0.5
POOL_WINDOWS = (2, 4, 8, 16)


def emit_proj_fm(P, psb, hT, wt, outs, evac):
    pass


def emit_emix(P, hT, wq, wk, wv, wp, pw_d, ps_d, cst_d, ymT, selw_d, tile_of=lambda j: j, is_output=True):
    nc = P.nc
    base_es = P.es
    cst = P.sb([128, 768], F32, "cst")
    P.dma("sp", cst, cst[:], cst_d, cst_d.h[:, :])
    ident_bf = P.sb([128, 128], BF16, "identbf")
    mask_f = cst
    mask_bf = P.sb([128, 128], BF16, "maskbf")
    P.op("dve", lambda e: e.tensor_copy(out=ident_bf[:], in_=cst[:, 0:128]), reads=[cst], writes=[ident_bf])
    P.op("dve", lambda e: e.tensor_copy(out=mask_bf[:], in_=cst[:, 128:256]), reads=[cst], writes=[mask_bf])
    one_c = P.sb([128, 1], F32, "onec")
    P.op("pool", lambda e: e.memset(one_c[:], 1.0), writes=[one_c])
    pscale = P.sb([128, 2], F32, "pscale")
    P.dma("sp", pscale, pscale[:], ps_d, ps_d.h[:, :])
    selw = P.sb([128, 2, 21], F32, "selw")
    P.dma("sp", selw, selw[:], selw_d, selw_d.h[:, :, :])
    hch = [P.sb([128, KC, 256], BF16, "hch") for _ in range(2)]
    pbank = [P.ps([128, 512], F32, "pbank") for _ in range(2)]
    cnt = {}

    def rot(lst, key):
        i = cnt.get(key, 0)
        cnt[key] = i + 1
        return lst[i % len(lst)]

    def load_h(c):
        t = rot(hch, "hch")
        for kq in range(4):
            P.dma("sp", t, t[:, kq * 4:(kq + 1) * 4, :], hT[(0, c)],
                  hT[(0, c)].h[kq * 4:(kq + 1) * 4, :, c * 256:(c + 1) * 256].rearrange("k p t -> p k t"))
        return t

    with ExitStack() as es:
        P.es = es
        wps = [P.sb([128, KC, 128], BF16, "wp") for _ in range(2)]
        pws = [P.sb([128, 128], BF16, "pw") for _ in range(2)]
        for gi in range(2):
            P.dma("pool", wps[gi], wps[gi][:], wp, wp.h[gi])
            P.dma("pool", pws[gi], pws[gi][:], pw_d, pw_d.h[gi])
        u = [P.sb([128, 16 + L], F32, "upool") for _ in range(2)]
        sa = P.sb([128, 16 + L], F32, "sa")
        sbb = P.sb([128, 16 + L], F32, "sbb")
        pooled = P.sb([128, L], BF16, "pooled")
        yp = [P.sb([128, 512], BF16, "yp") for _ in range(2)]
        for t in u + [sa, sbb]:
            P.op("pool", lambda e, t=t: e.memset(t[:, 0:16], 0.0), writes=[t])
        for c in range(16):
            ht = load_h(c)
            for gi in range(2):
                bk = rot(pbank, "pb")
                P.group("pe", [(lambda e, kc=kc, bk=bk, gi=gi, ht=ht: e.matmul(
                    bk[:, 0:256], lhsT=wps[gi][:, kc, :], rhs=ht[:, kc, :], start=(kc == 0), stop=(kc == KC - 1)))
                    for kc in range(KC)], reads=[wps[gi], ht], writes=[bk])
                P.op("act", lambda e, bk=bk, gi=gi, c=c: e.copy(out=u[gi][:, 16 + c * 256:16 + (c + 1) * 256],
                                                                 in_=bk[:, 0:256]), reads=[bk], writes=[u[gi]])
        sc = P.sb([128, L], F32, "sc")
        acc = P.sb([128, L], F32, "acc")
        for gi in range(2):
            src = u[gi]
            bufs = [sa, sbb]
            for k in range(4):
                sh = 1 << k
                dst = bufs[k % 2]
                P.op("dve", lambda e, dst=dst, src=src, sh=sh: e.tensor_tensor(
                    out=dst[:, 16:16 + L], in0=src[:, 16:16 + L], in1=src[:, 16 - sh:16 + L - sh], op=ALU.add),
                    reads=[src], writes=[dst])
                src = dst
                if k == 0:
                    P.op("dve", lambda e, dst=dst, gi=gi: e.tensor_scalar(
                        out=acc[:], in0=dst[:, 16:16 + L], scalar1=selw[:, gi, 0:1], scalar2=None, op0=ALU.mult),
                        reads=[dst, selw], writes=[acc])
                else:
                    P.op("dve", lambda e, dst=dst, gi=gi, k=k: e.scalar_tensor_tensor(
                        out=acc[:], in0=dst[:, 16:16 + L], scalar=selw[:, gi, k:k + 1], in1=acc[:],
                        op0=ALU.mult, op1=ALU.add), reads=[dst, selw], writes=[acc])
            P.op("dve", lambda e, gi=gi: e.tensor_scalar(
                out=sc[:], in0=acc[:], scalar1=selw[:, gi, 4:5], scalar2=None, op0=ALU.mult),
                reads=[acc, selw], writes=[sc])
            P.op("dve", lambda e, gi=gi: e.tensor_tensor(
                out=sc[:, 0:16], in0=acc[:, 0:16], in1=selw[:, gi, 5:21], op=ALU.mult),
                reads=[acc, selw], writes=[sc])
            P.op("dve", lambda e, gi=gi: e.tensor_tensor(
                out=pooled[:], in0=sc[:], in1=u[gi][:, 16:16 + L], op=ALU.subtract),
                reads=[sc, u[gi]], writes=[pooled])
            for c in range(8):
                bk = rot(pbank, "pb")
                P.op("pe", lambda e, bk=bk, gi=gi, c=c: e.matmul(
                    bk[:], lhsT=pws[gi][:], rhs=pooled[:, c * 512:(c + 1) * 512], start=True, stop=True),
                    reads=[pws[gi], pooled], writes=[bk])
                y = rot(yp, "yp")
                P.op("act", lambda e, y=y, bk=bk, gi=gi: e.activation(
                    out=y[:], in_=bk[:], func=AF.Copy, scale=pscale[:, gi:gi + 1]),
                    reads=[bk, pscale], writes=[y])
                P.dma("act", ymT[(tile_of(gi), c)], ymT[(tile_of(gi), c)].h[tile_of(gi), :, c * 512:(c + 1) * 512], y, y[:], is_output=is_output)
        P.barrier()
    for hp in range(3):
        with ExitStack() as es:
            P.es = es
            wqs = [P.sb([128, KC, 128], BF16, "wq") for _ in range(2)]
            wks = [P.sb([128, KC, 128], BF16, "wk") for _ in range(2)]
            wvs = P.sb([128, KC, 256], BF16, "wv")
            for j in range(2):
                P.dma("pool", wqs[j], wqs[j][:], wq, wq.h[hp * 2 + j])
                P.dma("pool", wks[j], wks[j][:], wk, wk.h[hp * 2 + j])
            P.dma("pool", wvs, wvs[:], wv, wv.h[hp])
            qT = [P.sb([128, L], BF16, "qT") for _ in range(2)]
            kT = [P.sb([128, L], BF16, "kT") for _ in range(2)]
            v = P.sb([128, 32, 256], BF16, "v")
            for c in range(16):
                ht = load_h(c)
                for j in range(2):
                    for (ws, dst) in ((wqs[j], qT[j]), (wks[j], kT[j])):
                        bk = rot(pbank, "pb")
                        P.group("pe", [(lambda e, kc=kc, bk=bk, ws=ws, ht=ht: e.matmul(
                            bk[:, 0:256], lhsT=ws[:, kc, :], rhs=ht[:, kc, :], start=(kc == 0), stop=(kc == KC - 1)))
                            for kc in range(KC)], reads=[ws, ht], writes=[bk])
                        P.op("act", lambda e, bk=bk, dst=dst, c=c: e.copy(
                            out=dst[:, c * 256:(c + 1) * 256], in_=bk[:, 0:256]), reads=[bk], writes=[dst])
                for tb in range(2):
                    bk = rot(pbank, "pb")
                    P.group("pe", [(lambda e, kc=kc, bk=bk, tb=tb, ht=ht: e.matmul(
                        bk[:, 0:256], lhsT=ht[:, kc, tb * 128:(tb + 1) * 128], rhs=wvs[:, kc, :],
                        start=(kc == 0), stop=(kc == KC - 1))) for kc in range(KC)],
                        reads=[wvs, ht], writes=[bk])
                    P.op("dve", lambda e, bk=bk, c=c, tb=tb: e.tensor_copy(
                        out=v[:, c * 2 + tb, :], in_=bk[:, 0:256]), reads=[bk], writes=[v])
            Pps = [P.sb([128, L + 1], F32, "Pp") for _ in range(2)]
            for Pp in Pps:
                P.op("pool", lambda e, Pp=Pp: e.memset(Pp[:, 0:1], 0.0), writes=[Pp])
            Eb = [P.sb([128, L], F32, "E") for _ in range(2)]
            wb = [P.sb([128, L], BF16, "w") for _ in range(2)]
            et = [P.sb([128, 512], F32, "et") for _ in range(2)]
            lt = [P.sb([128, 512], F32, "lt") for _ in range(2)]
            negT = [P.sb([128, 1], F32, "negT") for _ in range(2)]
            wTt = [P.sb([128, 512], BF16, "wT") for _ in range(3)]
            yT = [P.sb([128, 128], BF16, "yT") for _ in range(2)]
            zbank = [P.ps([128, 512], F32, "zbank") for _ in range(2)]
            tbank = [P.ps([128, 512], BF16, "tbank") for _ in range(2)]
            obank = [P.ps([128, 128], F32, "obank") for _ in range(2)]
            ones512 = cst
            def p1(i, j):
                    jg = 2 + hp * 2 + j
                    Pp = Pps[j]
                    nk = (i + 1) * 128
                    nch = (nk + 511) // 512
                    E = rot(Eb, "E")
                    w = rot(wb, "w")
                    for c in range(nch):
                        k0 = c * 512
                        kn = min(512, nk - k0)
                        zb = rot(zbank, "zb")
                        P.op("pe", lambda e, zb=zb, j=j, i=i, k0=k0, kn=kn: e.matmul(
                            zb[:, 0:kn], lhsT=qT[j][:, i * 128:(i + 1) * 128], rhs=kT[j][:, k0:k0 + kn],
                            start=True, stop=True), reads=[qT[j], kT[j]], writes=[zb])
                        e_t = rot(et, "et")
                        l_t = rot(lt, "lt")
                        P.op("act", lambda e, e_t=e_t, zb=zb, kn=kn: e.activation(
                            out=e_t[:, 0:kn], in_=zb[:, 0:kn], func=AF.Exp, scale=SCALE), reads=[zb], writes=[e_t])
                        P.op("act", lambda e, e_t=e_t, l_t=l_t, kn=kn: e.activation(
                            out=l_t[:, 0:kn], in_=e_t[:, 0:kn], func=AF.Ln, bias=one_c[:, 0:1], scale=1.0),
                            reads=[e_t, one_c], writes=[l_t])
                        if c == nch - 1:
                            P.op("pool", lambda e, l_t=l_t, kn=kn: e.tensor_tensor(
                                out=l_t[:, kn - 128:kn], in0=l_t[:, kn - 128:kn], in1=cst[:, 128:256], op=ALU.mult),
                                reads=[mask_f], writes=[l_t])
                        init = 0.0 if c == 0 else Pp[:, k0:k0 + 1]
                        P.op("dve", lambda e, l_t=l_t, k0=k0, kn=kn, init=init, Pp=Pp: e.tensor_tensor_scan(
                            out=Pp[:, 1 + k0:1 + k0 + kn], data0=cst[:, 256:256 + kn], data1=l_t[:, 0:kn],
                            initial=init, op0=ALU.mult, op1=ALU.add), reads=[l_t, ones512], writes=[Pp])
                        P.op("dve", lambda e, E=E, zb=zb, k0=k0, kn=kn, Pp=Pp: e.scalar_tensor_tensor(
                            out=E[:, k0:k0 + kn], in0=zb[:, 0:kn], scalar=SCALE, in1=Pp[:, k0:k0 + kn],
                            op0=ALU.mult, op1=ALU.add), reads=[zb, Pp], writes=[E])
                    nt = rot(negT, "negT")
                    P.op("dve", lambda e, nt=nt, nk=nk, Pp=Pp: e.tensor_scalar(
                        out=nt[:], in0=Pp[:, nk:nk + 1], scalar1=-1.0, scalar2=None, op0=ALU.mult),
                        reads=[Pp], writes=[nt])
                    return dict(i=i, j=j, jg=jg, nk=nk, nch=nch, E=E, w=w, nt=nt)

            def p2(st_):
                    i, j, jg, nk, nch, E, w, nt = (st_[k_] for k_ in ('i', 'j', 'jg', 'nk', 'nch', 'E', 'w', 'nt'))
                    for c in range(nch):
                        k0 = c * 512
                        kn = min(512, nk - k0)
                        P.op("act", lambda e, w=w, E=E, nt=nt, k0=k0, kn=kn: e.activation(
                            out=w[:, k0:k0 + kn], in_=E[:, k0:k0 + kn], func=AF.Exp, bias=nt[:, 0:1], scale=1.0),
                            reads=[E, nt], writes=[w])
                    P.op("pool", lambda e, w=w, nk=nk: e.tensor_tensor(
                        out=w[:, nk - 128:nk], in0=w[:, nk - 128:nk], in1=mask_bf[:], op=ALU.mult),
                        reads=[mask_bf], writes=[w])
                    ob = rot(obank, "ob")
                    groups = list(range(0, i + 1, 4))

                    def emit_T(s0):
                        ns = min(4, i + 1 - s0)
                        tb_ = rot(tbank, "tb")
                        P.group("pe", [(lambda e, tb_=tb_, w=w, s0=s0, q=q: e.transpose(
                            out=tb_[:, q * 128:(q + 1) * 128], in_=w[:, (s0 + q) * 128:(s0 + q + 1) * 128],
                            identity=ident_bf[:])) for q in range(ns)], reads=[w, ident_bf], writes=[tb_])
                        wT = rot(wTt, "wT")
                        P.op("dve", lambda e, wT=wT, tb_=tb_, ns=ns: e.tensor_copy(
                            out=wT[:, 0:ns * 128], in_=tb_[:, 0:ns * 128]), reads=[tb_], writes=[wT])
                        return (s0, ns, wT)

                    def emit_PV(g_):
                        s0, ns, wT = g_
                        P.group("pe", [(lambda e, ob=ob, wT=wT, s0=s0, q=q, j=j, i=i: e.matmul(
                            ob[:], lhsT=v[:, s0 + q, j * 128:(j + 1) * 128], rhs=wT[:, q * 128:(q + 1) * 128],
                            start=(s0 + q == 0), stop=(s0 + q == i))) for q in range(ns)],
                            reads=[v, wT], writes=[ob])

                    pg_ = None
                    for s0 in groups:
                        g_ = emit_T(s0)
                        if pg_ is not None:
                            emit_PV(pg_)
                        pg_ = g_
                    emit_PV(pg_)
                    y = rot(yT, "yT")
                    P.op("act", lambda e, y=y, ob=ob: e.copy(out=y[:], in_=ob[:]), reads=[ob], writes=[y])
                    P.dma("act", ymT[(tile_of(jg), i)], ymT[(tile_of(jg), i)].h[tile_of(jg), :, i * 128:(i + 1) * 128], y, y[:], is_output=is_output)

            pend_ = None
            for i in range(32):
                for j in range(2):
                    st_ = p1(i, j)
                    if pend_ is not None:
                        p2(pend_)
                    pend_ = st_
            p2(pend_)
            P.barrier()
    P.es = base_es

import math

L = 4096
KC = 16
NGP = 16
PI = math.pi
MAGIC = 12582912.0
TWO_PI_S = 2 * math.pi * (1 - 2e-6)


def emit_ossm(P, hT, wu_d, prm_d, bsm_d, ct_d, dd_d, cst_d, ysT, ys_ap=None, is_output=True):
    cnt = {}

    def rot(lst, key):
        i = cnt.get(key, 0)
        cnt[key] = i + 1
        return lst[i % len(lst)]

    cst = P.sb([128, 1152], F32, "cst")
    P.dma("sp", cst, cst[:], cst_d, cst_d.h[:, :])
    ident = cst
    prm = P.sb([128, 3, 16], F32, "prm")
    P.dma("sp", prm, prm[:], prm_d, prm_d.h[:, :, :])
    bsm = P.sb([128, 2, 16, 32], F32, "bsm")
    P.dma("sp", bsm, bsm[:], bsm_d, bsm_d.h[:, :, :, :])
    ctf = P.sb([128, 2, 16, 32], F32, "ctf")
    P.dma("sp", ctf, ctf[:], ct_d, ct_d.h[:, :, :, :])
    ddf = P.sb([32, 16, 32], F32, "ddf")
    P.dma("sp", ddf, ddf[:], dd_d, dd_d.h[:, :, :])
    wus = P.sb([128, NGP, KC, 32], BF16, "wus")
    for gp in range(NGP):
        P.dma("pool", wus, wus[:, gp], wu_d, wu_d.h[gp])
    negpi = P.sb([128, 1], F32, "negpi")
    P.op("pool", lambda e: e.memset(negpi[:], -PI), writes=[negpi])

    def small(name):
        return P.sb([128, 16], F32, name)

    dt, adt, th, dec, sa, ca, sn, cs = [small(n) for n in ["dt", "adt", "th", "dec", "sa", "ca", "sn", "cs"]]
    lre, lim, nr, den, f_re, f_im, tA, tB = [small(n) for n in ["lre", "lim", "nr", "den", "fre", "fim", "tA", "tB"]]
    a_re = lambda: prm[:, 0, :]
    a_im = lambda: prm[:, 1, :]
    P.op("act", lambda e: e.activation(out=dt[:], in_=prm[:, 2, :], func=AF.Exp), reads=[prm], writes=[dt])
    P.op("dve", lambda e: e.tensor_tensor(out=adt[:], in0=a_re(), in1=dt[:], op=ALU.mult), reads=[prm, dt], writes=[adt])
    P.op("dve", lambda e: e.tensor_tensor(out=th[:], in0=a_im(), in1=dt[:], op=ALU.mult), reads=[prm, dt], writes=[th])
    P.op("act", lambda e: e.activation(out=dec[:], in_=adt[:], func=AF.Exp), reads=[adt], writes=[dec])
    thn = small("thn")
    P.op("dve", lambda e: e.tensor_scalar(out=thn[:], in0=th[:], scalar1=1.0 / (2 * PI), scalar2=None, op0=ALU.mult),
         reads=[th], writes=[thn])

    def emit_sincos(src, dst_sin, dst_cos, mk):
        for (off, dst) in ((0.0, dst_sin), (0.25, dst_cos)):
            a2 = mk()
            k_ = mk()
            P.op("dve", lambda e, a2=a2, off=off: e.tensor_scalar(out=a2[:], in0=src[:], scalar1=off, scalar2=None, op0=ALU.add),
                 reads=[src], writes=[a2])
            P.op("dve", lambda e, a2=a2, k_=k_: e.tensor_scalar(out=k_[:], in0=a2[:], scalar1=MAGIC, scalar2=MAGIC,
                                                                 op0=ALU.add, op1=ALU.subtract), reads=[a2], writes=[k_])
            P.op("dve", lambda e, a2=a2, k_=k_: e.tensor_tensor(out=a2[:], in0=a2[:], in1=k_[:], op=ALU.subtract),
                 reads=[k_], writes=[a2])
            P.op("act", lambda e, a2=a2, dst=dst: e.activation(out=dst[:], in_=a2[:], func=AF.Sin, scale=TWO_PI_S),
                 reads=[a2], writes=[dst])

    emit_sincos(thn, sn, cs, lambda: small("sc_tmp"))
    P.op("dve", lambda e: e.tensor_tensor(out=lre[:], in0=dec[:], in1=cs[:], op=ALU.mult), reads=[dec, cs], writes=[lre])
    P.op("dve", lambda e: e.tensor_tensor(out=lim[:], in0=dec[:], in1=sn[:], op=ALU.mult), reads=[dec, sn], writes=[lim])
    P.op("dve", lambda e: e.tensor_scalar(out=nr[:], in0=lre[:], scalar1=-1.0, scalar2=None, op0=ALU.add), reads=[lre], writes=[nr])
    P.op("dve", lambda e: e.tensor_tensor(out=tA[:], in0=a_re(), in1=a_re(), op=ALU.mult), reads=[prm], writes=[tA])
    P.op("dve", lambda e: e.tensor_tensor(out=tB[:], in0=a_im(), in1=a_im(), op=ALU.mult), reads=[prm], writes=[tB])
    P.op("dve", lambda e: e.tensor_tensor(out=den[:], in0=tA[:], in1=tB[:], op=ALU.add), reads=[tA, tB], writes=[den])
    P.op("dve", lambda e: e.reciprocal(out=den[:], in_=den[:]), reads=[], writes=[den])
    P.op("dve", lambda e: e.tensor_tensor(out=tA[:], in0=nr[:], in1=a_re(), op=ALU.mult), reads=[nr, prm], writes=[tA])
    P.op("dve", lambda e: e.tensor_tensor(out=tB[:], in0=lim[:], in1=a_im(), op=ALU.mult), reads=[lim, prm], writes=[tB])
    P.op("dve", lambda e: e.tensor_tensor(out=tA[:], in0=tA[:], in1=tB[:], op=ALU.add), reads=[tB], writes=[tA])
    P.op("dve", lambda e: e.tensor_tensor(out=f_re[:], in0=tA[:], in1=den[:], op=ALU.mult), reads=[tA, den], writes=[f_re])
    P.op("dve", lambda e: e.tensor_tensor(out=tA[:], in0=lim[:], in1=a_re(), op=ALU.mult), reads=[lim, prm], writes=[tA])
    P.op("dve", lambda e: e.tensor_tensor(out=tB[:], in0=nr[:], in1=a_im(), op=ALU.mult), reads=[nr, prm], writes=[tB])
    P.op("dve", lambda e: e.tensor_tensor(out=tA[:], in0=tA[:], in1=tB[:], op=ALU.subtract), reads=[tB], writes=[tA])
    P.op("dve", lambda e: e.tensor_tensor(out=f_im[:], in0=tA[:], in1=den[:], op=ALU.mult), reads=[tA, den], writes=[f_im])

    BT = [P.sb([32, NGP, 128], BF16, "BTre"), P.sb([32, NGP, 128], BF16, "BTim")]
    xt_ = [P.sb([128, 32], F32, "xt_") for _ in range(2)]
    xo = [P.sb([128, 32], F32, "xo") for _ in range(2)]
    tps = [P.ps([32, 128], F32, "tps") for _ in range(1)]
    for gp in range(NGP):
        for ri in range(2):
            t_ = rot(xt_, "xt_")
            o_ = rot(xo, "xo")
            if ri == 0:
                P.op("dve", lambda e, t_=t_, gp=gp: e.tensor_scalar(
                    out=t_[:], in0=bsm[:, 1, gp, :], scalar1=f_im[:, gp:gp + 1], scalar2=None, op0=ALU.mult),
                    reads=[bsm, f_im], writes=[t_])
                P.op("dve", lambda e, t_=t_, o_=o_, gp=gp: e.scalar_tensor_tensor(
                    out=o_[:], in0=bsm[:, 0, gp, :], scalar=f_re[:, gp:gp + 1], in1=t_[:],
                    op0=ALU.mult, op1=ALU.subtract), reads=[bsm, f_re, t_], writes=[o_])
            else:
                P.op("dve", lambda e, t_=t_, gp=gp: e.tensor_scalar(
                    out=t_[:], in0=bsm[:, 0, gp, :], scalar1=f_im[:, gp:gp + 1], scalar2=None, op0=ALU.mult),
                    reads=[bsm, f_im], writes=[t_])
                P.op("dve", lambda e, t_=t_, o_=o_, gp=gp: e.scalar_tensor_tensor(
                    out=o_[:], in0=bsm[:, 1, gp, :], scalar=f_re[:, gp:gp + 1], in1=t_[:],
                    op0=ALU.mult, op1=ALU.add), reads=[bsm, f_re, t_], writes=[o_])
            tp = rot(tps, "tps")
            P.op("pe", lambda e, tp=tp, o_=o_: e.transpose(out=tp[:], in_=o_[:], identity=cst[:, 0:128]),
                 reads=[o_, cst], writes=[tp])
            P.op("act", lambda e, tp=tp, ri=ri, gp=gp: e.copy(out=BT[ri][:, gp, :], in_=tp[:]),
                 reads=[tp], writes=[BT[ri]])
    ctb = [P.sb([128, NGP, 32], BF16, "ctre"), P.sb([128, NGP, 32], BF16, "ctimn")]
    P.op("dve", lambda e: e.tensor_copy(out=ctb[0][:], in_=ctf[:, 0]), reads=[ctf], writes=[ctb[0]])
    P.op("dve", lambda e: e.tensor_scalar(out=ctb[1][:], in0=ctf[:, 1], scalar1=-1.0, scalar2=None, op0=ALU.mult),
         reads=[ctf], writes=[ctb[1]])
    ddb = P.sb([32, NGP, 32], BF16, "ddb")
    P.op("dve", lambda e: e.tensor_copy(out=ddb[:], in_=ddf[:]), reads=[ddf], writes=[ddb])

    cosT = [P.sb([128, 512], F32, "cosT") for _ in range(NGP)]
    sinT = [P.sb([128, 512], F32, "sinT") for _ in range(NGP)]
    decT = [P.sb([128, 512], F32, "decT") for _ in range(2)]
    ang = [P.sb([128, 512], F32, "ang") for _ in range(2)]
    arg = [P.sb([128, 512], F32, "arg") for _ in range(4)]
    for gp in range(NGP):
        a_ = rot(ang, "ang")
        P.op("dve", lambda e, a_=a_, gp=gp: e.tensor_scalar(
            out=a_[:], in0=cst[:, 128:640], scalar1=thn[:, gp:gp + 1], scalar2=None, op0=ALU.mult),
            reads=[cst, thn], writes=[a_])
        emit_sincos(a_, sinT[gp], cosT[gp], lambda: rot(arg, "arg"))

    carry = [[P.sb([128, 1], F32, f"car{ri}") for ri in range(2)] for _ in range(NGP)]
    for gp in range(NGP):
        for ri in range(2):
            P.op("pool", lambda e, gp=gp, ri=ri: e.memset(carry[gp][ri][:], 0.0), writes=[carry[gp][ri]])
    hch = [P.sb([128, KC, 512], BF16, "hch") for _ in range(2)]
    pu_b = [P.ps([32, 512], F32, "pu") for _ in range(2)]
    pr_b = [P.ps([128, 512], F32, "pr") for _ in range(2)]
    pi_b = [P.ps([128, 512], F32, "pi") for _ in range(2)]
    py_b = [P.ps([32, 512], F32, "py") for _ in range(1)]
    ugs = [P.sb([32, 512], BF16, "ug") for _ in range(2)]
    tt = [P.sb([128, 512], F32, "tt") for _ in range(8)]
    mm = [P.sb([128, 512], F32, "mm") for _ in range(4)]
    ww = [P.sb([128, 512], F32, "ww") for _ in range(4)]
    sf = [P.sb([128, 512], F32, "sf") for _ in range(4)]
    sbf = [P.sb([128, 512], BF16, "sbf") for _ in range(4)]
    ybs = [P.sb([32, 512], BF16, "yb") for _ in range(2)]
    pend_ = None
    for tc in range(L // 512):
        ht = rot(hch, "hch")
        for kq in range(4):
            P.dma("sp", ht, ht[:, kq * 4:(kq + 1) * 4, :], hT,
                  hT.h[kq * 4:(kq + 1) * 4, :, tc * 512:(tc + 1) * 512].rearrange("k p t -> p k t"))
        def p1(tc, gp, ht):
                pu = rot(pu_b, "pu")
                P.group("pe", [(lambda e, kc=kc, pu=pu, gp=gp, ht=ht: e.matmul(
                    pu[:], lhsT=wus[:, gp, kc, :], rhs=ht[:, kc, :], start=(kc == 0), stop=(kc == KC - 1)))
                    for kc in range(KC)], reads=[wus, ht], writes=[pu])
                ug = rot(ugs, "ug")
                P.op("act", lambda e, ug=ug, pu=pu: e.copy(out=ug[:], in_=pu[:]), reads=[pu], writes=[ug])
                pr = rot(pr_b, "pr")
                pi_ = rot(pi_b, "pi")
                P.op("pe", lambda e, pr=pr, ug=ug, gp=gp: e.matmul(pr[:], lhsT=BT[0][:, gp, :], rhs=ug[:], start=True, stop=True),
                     reads=[BT[0], ug], writes=[pr])
                P.op("pe", lambda e, pi_=pi_, ug=ug, gp=gp: e.matmul(pi_[:], lhsT=BT[1][:, gp, :], rhs=ug[:], start=True, stop=True),
                     reads=[BT[1], ug], writes=[pi_])
                cT, sT, dT = cosT[gp], sinT[gp], rot(decT, "decT")
                P.op("act", lambda e, gp=gp, dT=dT: e.activation(
                    out=dT[:], in_=cst[:, 640:1152], func=AF.Copy, scale=dec[:, gp:gp + 1]),
                    reads=[cst, dec], writes=[dT])
                t1, t2, t3, t4 = [rot(tt, "tt") for _ in range(4)]
                P.op("dve", lambda e, t1=t1, pr=pr, cT=cT: e.tensor_tensor(out=t1[:], in0=pr[:], in1=cT[:], op=ALU.mult),
                     reads=[pr, cT], writes=[t1])
                P.op("dve", lambda e, t2=t2, pi_=pi_, sT=sT: e.tensor_tensor(out=t2[:], in0=pi_[:], in1=sT[:], op=ALU.mult),
                     reads=[pi_, sT], writes=[t2])
                P.op("dve", lambda e, t3=t3, pi_=pi_, cT=cT: e.tensor_tensor(out=t3[:], in0=pi_[:], in1=cT[:], op=ALU.mult),
                     reads=[pi_, cT], writes=[t3])
                P.op("dve", lambda e, t4=t4, pr=pr, sT=sT: e.tensor_tensor(out=t4[:], in0=pr[:], in1=sT[:], op=ALU.mult),
                     reads=[pr, sT], writes=[t4])
                m_re, m_im = rot(mm, "mm"), rot(mm, "mm")
                P.op("pool", lambda e, m_re=m_re, t1=t1, t2=t2: e.tensor_tensor(out=m_re[:], in0=t1[:], in1=t2[:], op=ALU.add),
                     reads=[t1, t2], writes=[m_re])
                P.op("pool", lambda e, m_im=m_im, t3=t3, t4=t4: e.tensor_tensor(out=m_im[:], in0=t3[:], in1=t4[:], op=ALU.subtract),
                     reads=[t3, t4], writes=[m_im])
                w_re, w_im = rot(ww, "ww"), rot(ww, "ww")
                for (w_, m_, ri) in ((w_re, m_re, 0), (w_im, m_im, 1)):
                    P.op("dve", lambda e, w_=w_, m_=m_, ri=ri, gp=gp, dT=dT: e.tensor_tensor_scan(
                        out=w_[:], data0=dT[:], data1=m_[:], initial=carry[gp][ri][:, 0:1], op0=ALU.mult, op1=ALU.add),
                        reads=[dT, m_, carry[gp][ri]], writes=[w_])
                a1, a2, a3, a4 = [rot(tt, "tt") for _ in range(4)]
                P.op("pool", lambda e, a1=a1, w_re=w_re, cT=cT: e.tensor_tensor(out=a1[:], in0=w_re[:], in1=cT[:], op=ALU.mult),
                     reads=[w_re, cT], writes=[a1])
                P.op("pool", lambda e, a2=a2, w_im=w_im, sT=sT: e.tensor_tensor(out=a2[:], in0=w_im[:], in1=sT[:], op=ALU.mult),
                     reads=[w_im, sT], writes=[a2])
                P.op("pool", lambda e, a3=a3, w_re=w_re, sT=sT: e.tensor_tensor(out=a3[:], in0=w_re[:], in1=sT[:], op=ALU.mult),
                     reads=[w_re, sT], writes=[a3])
                P.op("pool", lambda e, a4=a4, w_im=w_im, cT=cT: e.tensor_tensor(out=a4[:], in0=w_im[:], in1=cT[:], op=ALU.mult),
                     reads=[w_im, cT], writes=[a4])
                s_re, s_im = rot(sf, "sf"), rot(sf, "sf")
                P.op("dve", lambda e, s_re=s_re, a1=a1, a2=a2: e.tensor_tensor(out=s_re[:], in0=a1[:], in1=a2[:], op=ALU.subtract),
                     reads=[a1, a2], writes=[s_re])
                P.op("dve", lambda e, s_im=s_im, a3=a3, a4=a4: e.tensor_tensor(out=s_im[:], in0=a3[:], in1=a4[:], op=ALU.add),
                     reads=[a3, a4], writes=[s_im])
                sb_re, sb_im = rot(sbf, "sbf"), rot(sbf, "sbf")
                for (sb_, s_, ri) in ((sb_re, s_re, 0), (sb_im, s_im, 1)):
                    P.op("act", lambda e, sb_=sb_, s_=s_: e.copy(out=sb_[:], in_=s_[:]), reads=[s_], writes=[sb_])
                    P.op("act", lambda e, s_=s_, gp=gp, ri=ri: e.copy(out=carry[gp][ri][:], in_=s_[:, 511:512]),
                         reads=[s_], writes=[carry[gp][ri]])
                return dict(tc=tc, gp=gp, ug=ug, sb_re=sb_re, sb_im=sb_im)

        def p2(st_):
                tc, gp, ug, sb_re, sb_im = (st_[k_] for k_ in ('tc', 'gp', 'ug', 'sb_re', 'sb_im'))
                py = rot(py_b, "py")
                P.group("pe", [
                    (lambda e, py=py, sb_re=sb_re, gp=gp: e.matmul(py[:], lhsT=ctb[0][:, gp, :], rhs=sb_re[:], start=True, stop=False)),
                    (lambda e, py=py, sb_im=sb_im, gp=gp: e.matmul(py[:], lhsT=ctb[1][:, gp, :], rhs=sb_im[:], start=False, stop=False)),
                    (lambda e, py=py, ug=ug, gp=gp: e.matmul(py[:], lhsT=ddb[:, gp, :], rhs=ug[:], start=False, stop=True)),
                ], reads=[ctb[0], ctb[1], ddb, sb_re, sb_im, ug], writes=[py])
                yb = rot(ybs, "yb")
                P.op("act", lambda e, yb=yb, py=py: e.activation(out=yb[:], in_=py[:], func=AF.Gelu), reads=[py], writes=[yb])
                P.dma("act", ysT[(gp, tc)], (ys_ap(gp, tc) if ys_ap else ysT[(gp, tc)].h[gp, :, tc * 512:(tc + 1) * 512]), yb, yb[:], is_output=is_output)

        for gp in range(NGP):
            st_ = p1(tc, gp, ht)
            if pend_ is not None:
                p2(pend_)
            pend_ = st_
    p2(pend_)


KC = 16
EPS = 1e-6


def emit_omix2(P, hT, yss, wzu_d, wzv_d, glw_d, glb_d, sgn_d, wsT_d, sgb_d, mle_d, ymx, ntok):
    cnt = {}

    def rot(lst, key):
        i = cnt.get(key, 0)
        cnt[key] = i + 1
        return lst[i % len(lst)]

    wzu = [P.sb([128, KC, 128], BF16, "wzu") for _ in range(8)]
    wzv = [P.sb([128, KC, 512], BF16, "wzv") for _ in range(2)]
    glw = [P.sb([128, 8, 128], BF16, "glw") for _ in range(8)]
    for m in range(8):
        P.dma("pool", wzu[m], wzu[m][:], wzu_d, wzu_d.h[m])
        P.dma("pool", glw[m], glw[m][:], glw_d, glw_d.h[m])
    for hf in range(2):
        for q in range(4):
            P.dma("pool", wzv[hf], wzv[hf][:, q * 4:(q + 1) * 4, :], wzv_d, wzv_d.h[hf, :, q * 4:(q + 1) * 4, :])
    glb = P.sb([128, 8], F32, "glb")
    P.dma("sp", glb, glb[:], glb_d, glb_d.h[:, :])
    sgn = P.sb([128, 1024], F32, "sgn")
    P.dma("sp", sgn, sgn[:], sgn_d, sgn_d.h[:, :])
    wsf = P.sb([128, 8, 128], F32, "wsf")
    P.dma("sp", wsf, wsf[:], wsT_d, wsT_d.h[:, :, :])
    sgb = P.sb([128, 8, 128], F32, "sgb")
    P.dma("sp", sgb, sgb[:], sgb_d, sgb_d.h[:, :, :])
    mle = P.sb([128, 128], F32, "mle")
    P.dma("sp", mle, mle[:], mle_d, mle_d.h[:, :])
    wsb = P.sb([128, 8, 128], BF16, "wsb")
    for hd in range(8):
        P.op("dve", lambda e, hd=hd: e.tensor_tensor(out=wsb[:, hd, :], in0=wsf[:, hd, :], in1=mle[:], op=ALU.mult),
             reads=[wsf, mle], writes=[wsb])
    epst = P.sb([128, 1], F32, "epst")
    P.op("pool", lambda e: e.memset(epst[:], EPS), writes=[epst])

    hch = [P.sb([128, KC, 512], BF16, "hch") for _ in range(2)]
    ych = [P.sb([128, 8, 512], BF16, "ych") for _ in range(2)]
    uT = P.sb([128, 8, 512], F32, "uT")
    vg = [P.sb([128, 1024], F32, "vg") for _ in range(2)]
    sqv = P.sb([128, 1024], F32, "sqv")
    ss = [P.sb([128, 1], F32, "ss") for _ in range(2)]
    rs = [P.sb([128, 1], F32, "rs") for _ in range(2)]
    vtm = [P.sb([128, 1024], BF16, "vtm") for _ in range(2)]
    sig = [P.sb([128, 512], F32, "sig") for _ in range(2)]
    tmp4 = [P.sb([128, 4, 128], F32, "tmp4") for _ in range(2)]
    yo = [P.sb([128, 512], BF16, "yo") for _ in range(3)]
    yo4 = [P.sb([128, 4, 512], BF16, "yo4") for _ in range(2)]
    bank = [P.ps([128, 512], F32, "bk") for _ in range(4)]
    bank4 = [P.ps([128, 4, 128], F32, "bk4") for _ in range(2)]

    for c in range(ntok // 512):
        tsl = slice(c * 512, (c + 1) * 512)
        ht = rot(hch, "hch")
        for kq in range(4):
            P.dma("sp", ht, ht[:, kq * 4:(kq + 1) * 4, :], hT,
                  hT.h[kq * 4:(kq + 1) * 4, :, tsl].rearrange("k p t -> p k t"))
        yt = rot(ych, "ych")
        for kq in range(2):
            P.dma("sp", yt, yt[:, kq * 4:(kq + 1) * 4, :], yss,
                  yss.h[kq * 4:(kq + 1) * 4, :, tsl].rearrange("k p t -> p k t"))
        for m in range(8):
            bk = rot(bank, "bk")
            P.group("pe", [(lambda e, kc=kc, bk=bk, m=m, yt=yt: e.matmul(
                bk[:], lhsT=glw[m][:, kc, :], rhs=yt[:, kc, :], start=(kc == 0), stop=(kc == 7)))
                for kc in range(8)], reads=[glw[m], yt], writes=[bk])
            sg = rot(sig, "sig")
            P.op("act", lambda e, sg=sg, bk=bk, m=m: e.activation(
                out=sg[:], in_=bk[:], func=AF.Sigmoid, bias=glb[:, m:m + 1], scale=1.0),
                reads=[bk, glb], writes=[sg])
            y_ = rot(yo, "yo")
            P.op("dve", lambda e, y_=y_, sg=sg, yt=yt, m=m: e.tensor_tensor(
                out=y_[:], in0=sg[:], in1=yt[:, m, :], op=ALU.mult), reads=[sg, yt], writes=[y_])
            P.dma("act", ymx[(m, c)], ymx[(m, c)].h[m, :, tsl], y_, y_[:])
        for m in range(8):
            bk = rot(bank, "bk")
            P.group("pe", [(lambda e, kc=kc, bk=bk, m=m, ht=ht: e.matmul(
                bk[:], lhsT=wzu[m][:, kc, :], rhs=ht[:, kc, :], start=(kc == 0), stop=(kc == KC - 1)))
                for kc in range(KC)], reads=[wzu[m], ht], writes=[bk])
            P.op("act", lambda e, bk=bk, m=m: e.activation(out=uT[:, m, :], in_=bk[:], func=AF.Gelu),
                 reads=[bk], writes=[uT])
        y4 = [rot(yo4, "yo4") for _ in range(2)]
        for tb in range(4):
            vg_ = rot(vg, "vg")
            for hf in range(2):
                bk = rot(bank, "bk")
                P.group("pe", [(lambda e, kc=kc, bk=bk, hf=hf, ht=ht, tb=tb: e.matmul(
                    bk[:], lhsT=ht[:, kc, tb * 128:(tb + 1) * 128], rhs=wzv[hf][:, kc, :],
                    start=(kc == 0), stop=(kc == KC - 1))) for kc in range(KC)],
                    reads=[wzv[hf], ht], writes=[bk])
                P.op("act", lambda e, bk=bk, vg_=vg_, hf=hf: e.activation(
                    out=vg_[:, hf * 512:(hf + 1) * 512], in_=bk[:], func=AF.Gelu), reads=[bk], writes=[vg_])
            ss_ = rot(ss, "ss")
            rs_ = rot(rs, "rs")
            P.op("dve", lambda e, vg_=vg_: e.tensor_tensor(out=sqv[:], in0=vg_[:], in1=vg_[:], op=ALU.mult),
                 reads=[vg_], writes=[sqv])
            P.op("dve", lambda e, ss_=ss_: e.reduce_sum(out=ss_[:], in_=sqv[:], axis=AX.X), reads=[sqv], writes=[ss_])
            P.op("act", lambda e, ss_=ss_, rs_=rs_: e.activation(
                out=rs_[:], in_=ss_[:], func=AF.Sqrt, bias=epst[:, 0:1], scale=1.0 / 1024),
                reads=[ss_, epst], writes=[rs_])
            P.op("dve", lambda e, rs_=rs_: e.reciprocal(out=rs_[:], in_=rs_[:]), reads=[], writes=[rs_])
            vt = rot(vtm, "vtm")
            P.op("dve", lambda e, vt=vt, vg_=vg_, rs_=rs_: e.scalar_tensor_tensor(
                out=vt[:], in0=vg_[:], scalar=rs_[:, 0:1], in1=sgn[:], op0=ALU.mult, op1=ALU.mult),
                reads=[vg_, rs_, sgn], writes=[vt])
            for j in range(2):
                b4 = rot(bank4, "bk4")
                P.group("pe", [(lambda e, b4=b4, vt=vt, j=j, q=q: e.matmul(
                    b4[:, q, :], lhsT=vt[:, (4 * j + q) * 128:(4 * j + q + 1) * 128], rhs=wsb[:, 4 * j + q, :],
                    start=True, stop=True)) for q in range(4)], reads=[vt, wsb], writes=[b4])
                t4 = rot(tmp4, "tmp4")
                P.op("dve", lambda e, t4=t4, b4=b4, j=j: e.tensor_tensor(
                    out=t4[:], in0=b4[:], in1=sgb[:, 4 * j:4 * j + 4, :], op=ALU.add), reads=[b4, sgb], writes=[t4])
                P.op("dve", lambda e, t4=t4, j=j, tb=tb, y4=y4: e.tensor_tensor(
                    out=y4[j][:, :, tb * 128:(tb + 1) * 128], in0=t4[:],
                    in1=uT[:, 4 * j:4 * j + 4, tb * 128:(tb + 1) * 128], op=ALU.mult),
                    reads=[t4, uT], writes=[y4[j]])
        for j in range(2):
            for q in range(4):
                m = 8 + 4 * j + q
                P.dma("act", ymx[(m, c)], ymx[(m, c)].h[m, :, tsl], y4[j], y4[j][:, q, :])

NCORE = 8
NT = 2048
BF = ml_dtypes.bfloat16


def _bass():
    return bass.Bass("TRN2", target_bir_lowering=False)


def _ffn_w_inputs(P, sfx):
    wg = P.dram("wg" + sfx, [FT, 128, KC, 128], F32, "ExternalInput")
    wu = P.dram("wu" + sfx, [FT, 128, KC, 128], F32, "ExternalInput")
    wd = P.dram("wd" + sfx, [KC, 128, FT, 128], F32, "ExternalInput")
    return wg, wu, wd


def _load_g(P, name):
    gd = P.dram(name, [128, 96], F32, "ExternalInput")
    g = P.sb([128, 96], F32, name)
    P.dma("sp", g, g[:], gd, gd.h[:, :])
    return g


LSEQ = 4096
NACT = 4


def _whole(P, name, shape, dt, kind):
    base = P.dram(name, shape, dt, kind)

    class _D(dict):
        def __missing__(self, k):
            return base
    d = _D()
    d["is_output"] = (kind == "ExternalOutput")
    d["base"] = base
    return d


def _ym_loader(P, res, ymd):
    def ym(c):
        h = res["h"]
        for kq in range(4):
            P.dma("sp", h, h[:, kq * 4:(kq + 1) * 4, 0:512], ymd,
                  ymd.h[kq * 4:(kq + 1) * 4, :, c * 512:(c + 1) * 512].rearrange("k p t -> p k t"))
        return _View(h)
    return ym


class _View:
    def __init__(self, t):
        self.t = t
        self.w = t.w
        self.r = t.r
        self.name = t.name

    def __getitem__(self, idx):
        a, b, c = idx
        assert c == slice(None)
        return self.t.h[a, b, 0:512]


STOP_AFTER = 99
SKIP = set()


def _on(k):
    return STOP_AFTER >= k and k not in SKIP


def build_FUSED():
    nc = _bass()
    NH = LSEQ // 512
    with ExitStack() as es:
        P = Prog(nc, es)
        base = P.es
        x_in = regions(P, "xT", [KC, 128, LSEQ], F32, "ExternalInput", KC, NH)
        xo = regions(P, "xoT", [KC, 128, LSEQ], F32, "ExternalOutput", KC, NH)
        xs = [regions(P, f"x{i}s", [KC, 128, LSEQ], F32, "Internal", KC, NH) for i in range(1, 6)]
        x1, x2, x3, x4, x5 = xs
        h1 = regions(P, "h1s", [KC, 128, LSEQ], BF16, "Internal", KC, NH)
        h2 = regions(P, "h2s", [KC, 128, LSEQ], BF16, "Internal", KC, NH)
        ymT = regions(P, "ymTs", [KC, 128, LSEQ], BF16, "Internal", KC, 32)
        ymx = regions(P, "ymxs", [KC, 128, LSEQ], BF16, "Internal", KC, NH)
        ysd = regions(P, "yss", [8, 128, LSEQ], BF16, "Internal", 32, NH)
        wf = [_ffn_w_inputs(P, str(i)) for i in range(4)]
        gd = [P.dram(f"g{i}", [128, 96], F32, "ExternalInput") for i in range(2)]
        em = []
        for hh in range(2):
            s = f"_{hh}"
            em.append(dict(
                wq=P.dram("wq" + s, [6, 128, KC, 128], F32, "ExternalInput"),
                wk=P.dram("wk" + s, [6, 128, KC, 128], F32, "ExternalInput"),
                wv=P.dram("wv" + s, [3, 128, KC, 256], F32, "ExternalInput"),
                wp=P.dram("wp" + s, [2, 128, KC, 128], F32, "ExternalInput"),
                pw=P.dram("pw" + s, [2, 128, 128], F32, "ExternalInput"),
                psc=P.dram("psc" + s, [128, 2], F32, "ExternalInput"),
                selw=P.dram("selw" + s, [128, 2, 21], F32, "ExternalInput")))
        cst = P.dram("cst", [128, 768], F32, "ExternalInput")
        wo0 = P.dram("wo0", [KC, 128, KC, 128], F32, "ExternalInput")
        wo1 = P.dram("wo1", [KC, 128, KC, 128], F32, "ExternalInput")
        om = []
        for hh in range(2):
            s = f"_{hh}"
            om.append(dict(
                wu=P.dram("swu" + s, [NGP, 128, KC, 32], F32, "ExternalInput"),
                prm=P.dram("prm" + s, [128, 3, 16], F32, "ExternalInput"),
                bsm=P.dram("bsm" + s, [128, 2, 16, 32], F32, "ExternalInput"),
                ct=P.dram("ct" + s, [128, 2, 16, 32], F32, "ExternalInput"),
                dd=P.dram("dd" + s, [32, 16, 32], F32, "ExternalInput")))
        cst2 = P.dram("cst2", [128, 1152], F32, "ExternalInput")
        o2 = dict(
            wzu=P.dram("wzu", [8, 128, KC, 128], F32, "ExternalInput"),
            wzv=P.dram("wzv", [2, 128, KC, 512], F32, "ExternalInput"),
            glw=P.dram("glw", [8, 128, 8, 128], F32, "ExternalInput"),
            glb=P.dram("glb", [128, 8], F32, "ExternalInput"),
            sgn=P.dram("sgn", [128, 1024], F32, "ExternalInput"),
            wsT=P.dram("wsT", [128, 8, 128], F32, "ExternalInput"),
            sgb=P.dram("sgb", [128, 8, 128], F32, "ExternalInput"),
            mle=P.dram("mle", [128, 128], F32, "ExternalInput"))
        fscr = [regions(P, f"fscr{i}", [KC, 128, LSEQ], F32, "Internal", KC, NH) for i in range(6)]

        def load_g(i):
            g = P.sb([128, 96], F32, f"g{i}")
            P.dma("sp", g, g[:], gd[i], gd[i].h[:, :])
            return g

        def ffn_res(fs):
            res = ffn_resources(P, LSEQ, scr_name=None, f_scr=fs)
            return res

        with ExitStack() as sc:
            P.es = sc
            C = Common(P)
            g0 = load_g(0)
            res = ffn_res(fscr[0])
            emit_ffn_pipelined(P, C, res, x_in, x1, wf[0][0], wf[0][1], wf[0][2], g0, 0, 1, LSEQ, h_out=h1, jnext=2)
            P.barrier()
        for hh in range(2 if _on(2) else 0):
            with ExitStack() as sc:
                P.es = sc
                e_ = em[hh]
                emit_emix(P, {(0, c): h1["base"] for c in range(16)}, e_["wq"], e_["wk"], e_["wv"], e_["wp"],
                          e_["pw"], e_["psc"], cst, ymT, e_["selw"],
                          tile_of=(lambda j, hh=hh: (2 * hh + j) if j < 2 else (4 + 6 * hh + j - 2)),
                          is_output=False)
                P.barrier()
        for _ in range(1 if _on(3) else 0):
          with ExitStack() as sc:
            P.es = sc
            C = Common(P)
            g0 = load_g(0)
            g1 = load_g(1)
            res = ffn_res(fscr[1])
            emit_outproj(P, C, res, _ym_loader(P, res, ymT["base"]), wo0, x1, x2, g0, 3, LSEQ)
            emit_ffn_pipelined(P, C, res, x2, x3, wf[1][0], wf[1][1], wf[1][2], g0, 4, 5, LSEQ)
            emit_ffn_pipelined(P, C, res, x3, x4, wf[2][0], wf[2][1], wf[2][2], g1, 0, 1, LSEQ, h_out=h2, jnext=2)
            P.barrier()
        for hh in range(2 if _on(4) else 0):
            with ExitStack() as sc:
                P.es = sc
                o_ = om[hh]

                def ys_ap(gp, tc, hh=hh):
                    gg = 16 * hh + gp
                    return ysd["base"].h[gg // 4, (gg % 4) * 32:(gg % 4) * 32 + 32, tc * 512:(tc + 1) * 512]
                ysT = {(gp, tc): ysd[(16 * hh + gp, tc)] for gp in range(NGP) for tc in range(NH)}
                emit_ossm(P, h2["base"], o_["wu"], o_["prm"], o_["bsm"], o_["ct"], o_["dd"], cst2, ysT,
                          ys_ap=ys_ap, is_output=False)
                P.barrier()
        for _ in range(1 if _on(5) else 0):
          with ExitStack() as sc:
            P.es = sc
            emit_omix2(P, h2["base"], ysd["base"], o2["wzu"], o2["wzv"], o2["glw"], o2["glb"], o2["sgn"],
                       o2["wsT"], o2["sgb"], o2["mle"], ymx, LSEQ)
            P.barrier()
        for _ in range(1 if _on(5) else 0):
          with ExitStack() as sc:
            P.es = sc
            C = Common(P)
            g1 = load_g(1)
            res = ffn_res(fscr[2])

            def ym(c):
                h = res["h"]
                for kc in range(KC):
                    P.dma("sp", h, h[:, kc, 0:512], ymx[(kc, c)], ymx[(kc, c)].h[kc, :, c * 512:(c + 1) * 512])
                return _View(h)
            emit_outproj(P, C, res, ym, wo1, x4, x5, g1, 3, LSEQ)
            emit_ffn_pipelined(P, C, res, x5, xo, wf[3][0], wf[3][1], wf[3][2], g1, 4, 5, LSEQ)
            P.barrier()
        P.es = base
        P.finish()
        P.emit()
    return nc


def _colT(cols):
    K = cols.shape[0] // 128
    n = cols.shape[1] // 128
    return np.ascontiguousarray(cols.reshape(K, 128, n, 128).transpose(2, 1, 0, 3))


def _ffn_layout(w_gate, w_up, w_down, sfx):
    return {"wg" + sfx: _colT(w_gate), "wu" + sfx: _colT(w_up),
            "wd" + sfx: np.ascontiguousarray(w_down.reshape(FT, 128, KC, 128).transpose(2, 1, 0, 3))}


def _g_layout(gn):
    return np.ascontiguousarray(gn.reshape(6, KC, 128).transpose(2, 0, 1).reshape(128, 96))


def _fm(a):
    return np.ascontiguousarray(a.T).reshape(a.shape[1] // 128, 128, a.shape[0])


def _emix_inputs(w_in, pool_w, pool_scale, hh):
    heads = range(6 * hh, 6 * hh + 6)
    qc = np.concatenate([w_in[:, 512 + h * 128: 512 + (h + 1) * 128] for h in heads], 1)
    kc = np.concatenate([w_in[:, 512 + 1536 + h * 128: 512 + 1536 + (h + 1) * 128] for h in heads], 1)
    vc = np.concatenate([w_in[:, 512 + 3072 + h * 128: 512 + 3072 + (h + 1) * 128] for h in heads], 1)
    pc = w_in[:, hh * 256:(hh + 1) * 256]
    wv = np.ascontiguousarray(vc.reshape(KC, 128, 3, 256).transpose(2, 1, 0, 3))
    pw = np.ascontiguousarray(pool_w[2 * hh:2 * hh + 2])
    psc = np.ascontiguousarray(pool_scale[hh * 256:(hh + 1) * 256].reshape(2, 128).T)
    selw = np.zeros((128, 2, 21), np.float32)
    for gi in range(2):
        k = 2 * hh + gi
        w = POOL_WINDOWS[k]
        selw[:, gi, k] = 1.0
        selw[:, gi, 4] = 1.0 / w
        selw[:, gi, 5:21] = (1.0 / np.minimum(np.arange(1, 17), w))[None, :]
    cst = np.zeros((128, 768), np.float32)
    cst[:, 0:128] = np.eye(128)
    cst[:, 128:256] = np.tril(np.ones((128, 128)), -1)
    cst[:, 256:768] = 1.0
    return {"wq": _colT(qc), "wk": _colT(kc), "wv": wv, "wp": _colT(pc), "pw": pw, "psc": psc,
            "selw": selw, "cst": cst}


def _ossm_inputs(w_in, a_re, a_im, log_dt, b_re, b_im, c_re, c_im, d, hh):
    G0 = 32 * hh
    wu = w_in[:, G0 * 16:(G0 + 32) * 16]
    wu = np.ascontiguousarray(wu.reshape(KC, 128, NGP, 32).transpose(2, 1, 0, 3))
    prm = np.zeros((128, 3, 16), np.float32)
    bsm = np.zeros((128, 2, 16, 32), np.float32)
    ct = np.zeros((128, 2, 16, 32), np.float32)
    dd = np.zeros((32, 16, 32), np.float32)
    for gp in range(NGP):
        for gl in range(2):
            g = G0 + 2 * gp + gl
            sl = slice(gl * 64, (gl + 1) * 64)
            prm[sl, 0, gp] = a_re[g]
            prm[sl, 1, gp] = a_im[g]
            prm[sl, 2, gp] = log_dt[g]
            bsm[sl, 0, gp, gl * 16:(gl + 1) * 16] = b_re[g]
            bsm[sl, 1, gp, gl * 16:(gl + 1) * 16] = b_im[g]
            ct[sl, 0, gp, gl * 16:(gl + 1) * 16] = c_re[g].T
            ct[sl, 1, gp, gl * 16:(gl + 1) * 16] = c_im[g].T
            idx = np.arange(16)
            dd[gl * 16 + idx, gp, gl * 16 + idx] = d[g * 16:(g + 1) * 16]
    c2 = np.zeros((128, 1152), np.float32)
    c2[:, 0:128] = np.eye(128)
    c2[:, 128:640] = np.arange(1, 513, dtype=np.float32)[None, :]
    c2[:, 640:1152] = 1.0
    return {"wu": wu, "prm": prm, "bsm": bsm, "ct": ct, "dd": dd, "cst2": c2}


def _omix2_inputs(w_in, glu_w, glu_b, sgu_norm_g, sgu_w, sgu_b):
    zu = w_in[:, 1024:2048]
    zv = w_in[:, 2048:3072]
    wzv = np.ascontiguousarray(zv.reshape(KC, 128, 2, 512).transpose(2, 1, 0, 3))
    glw = _colT(glu_w)
    glb = np.ascontiguousarray(glu_b.reshape(8, 128).T)
    sgn = np.ascontiguousarray(np.broadcast_to(sgu_norm_g[None, :], (128, 1024)))
    wsT = np.ascontiguousarray(sgu_w.transpose(2, 0, 1))
    sgb = np.ascontiguousarray(np.broadcast_to(sgu_b[None, :, :], (128, 8, 128)))
    mle = np.triu(np.ones((128, 128), np.float32))
    return {"wzu": _colT(zu), "wzv": wzv, "glw": glw, "glb": glb, "sgn": sgn, "wsT": wsT, "sgb": sgb, "mle": mle}


_PROGS = {}


def _make_inputs(x, norm_g, ffn_w_gate, ffn_w_up, ffn_w_down, ev_w_in, ev_pool_w, ev_pool_scale, ev_w_out,
           od_w_in, od_ssm_a_re, od_ssm_a_im, od_ssm_log_dt, od_ssm_b_re, od_ssm_b_im, od_ssm_c_re,
           od_ssm_c_im, od_ssm_d, od_glu_w, od_glu_b, od_sgu_norm_g, od_sgu_w, od_sgu_b, od_w_out):
    f = lambda a: np.asarray(a, dtype=np.float32)
    x = f(x)
    norm_g, ffn_w_gate, ffn_w_up, ffn_w_down = f(norm_g), f(ffn_w_gate), f(ffn_w_up), f(ffn_w_down)
    B, Lq, Dm = x.shape
    shared = {}
    for i, (l, j) in enumerate(((0, 0), (0, 1), (1, 0), (1, 1))):
        shared.update(_ffn_layout(ffn_w_gate[l, j], ffn_w_up[l, j], ffn_w_down[l, j], str(i)))
    shared["g0"] = _g_layout(norm_g[0])
    shared["g1"] = _g_layout(norm_g[1])
    for hh in range(2):
        e_ = _emix_inputs(f(ev_w_in[0]), f(ev_pool_w[0]), f(ev_pool_scale[0]), hh)
        shared["cst"] = e_.pop("cst")
        for k, v in e_.items():
            shared[f"{k}_{hh}"] = v
        o_ = _ossm_inputs(f(od_w_in[0]), f(od_ssm_a_re[0]), f(od_ssm_a_im[0]), f(od_ssm_log_dt[0]),
                          f(od_ssm_b_re[0]), f(od_ssm_b_im[0]), f(od_ssm_c_re[0]), f(od_ssm_c_im[0]),
                          f(od_ssm_d[0]), hh)
        shared["cst2"] = o_.pop("cst2")
        shared[f"swu_{hh}"] = o_.pop("wu")
        for k, v in o_.items():
            shared[f"{k}_{hh}"] = v
    shared["wo0"] = _colT(f(ev_w_out[0]))
    shared["wo1"] = _colT(f(od_w_out[0]))
    shared.update(_omix2_inputs(f(od_w_in[0]), f(od_glu_w[0]), f(od_glu_b[0]), f(od_sgu_norm_g[0]),
                                f(od_sgu_w[0]), f(od_sgu_b[0])))
    return [dict(shared, xT=_fm(x[b])) for b in range(B)]


def kernel(**inputs):
    x = np.asarray(inputs["x"])
    B, Lq, Dm = x.shape
    ims = _make_inputs(**inputs)
    if "F" not in _PROGS:
        _PROGS["F"] = build_FUSED()
    r = run_bass_kernel_spmd(_PROGS["F"], ims, core_ids=list(range(NACT)))
    out = np.stack([r.results[b]["xoT"].reshape(Dm, Lq).T for b in range(B)], axis=0)
    return np.ascontiguousarray(out).astype(np.float32)
```

```python
import math
import ml_dtypes
from concourse.bass_utils import run_bass_kernel_spmd
import numpy as np
from contextlib import ExitStack
import concourse.bass as bass
import concourse.mybir as mybir

F32 = mybir.dt.float32
BF16 = mybir.dt.bfloat16
ALU = mybir.AluOpType
AF = mybir.ActivationFunctionType
AX = mybir.AxisListType

ENGS = ["pe", "dve", "act", "pool", "sp"]
NRING = 16
RING_N = {"sp": 24, "act": 16, "pool": 16}


class T:
    def __init__(self, h, name):
        self.h = h
        self.name = name
        self.w = {}
        self.r = {}

    def __getitem__(self, idx):
        return self.h[idx]


class Prog:
    def __init__(self, nc, es):
        self.nc = nc
        self.es = es
        self.q = {e: [] for e in ENGS}
        self.esem = {e: es.enter_context(nc.semaphore(f"s_{e}")) for e in ENGS}
        self.ecnt = {e: 0 for e in ENGS}
        self.seen = {e: {} for e in ENGS}
        self.ring = {}
        self.ringcnt = {}
        self.ringpos = {}
        for qn in ["sp", "act", "pool"]:
            self.ring[qn] = [es.enter_context(nc.semaphore(f"d_{qn}{i}")) for i in range(RING_N[qn])]
            self.ringcnt[qn] = [0] * RING_N[qn]
            self.ringpos[qn] = 0
        self.nuniq = 0
        self.out_events = []

    def sb(self, shape, dt, name=None):
        self.nuniq += 1
        name = f"{name or 't'}_{self.nuniq}"
        h = self.es.enter_context(self.nc.sbuf_tensor(name, list(shape), dt))
        return T(h, name)

    def ps(self, shape, dt=F32, name=None):
        self.nuniq += 1
        name = f"{name or 'p'}_{self.nuniq}"
        h = self.es.enter_context(self.nc.psum_tensor(name, list(shape), dt))
        return T(h, name)

    def dram(self, name, shape, dt, kind):
        h = self.nc.dram_tensor(name, list(shape), dt, kind=kind)
        return T(h.ap() if hasattr(h, "ap") else h, name)

    def _collect(self, eng, reads, writes):
        waits = []
        for t in list(reads) + list(writes):
            for sem, (val, src) in t.w.items():
                waits.append((sem, val, src))
        for t in writes:
            for sem, (val, src) in t.r.items():
                waits.append((sem, val, src))
        need = {}
        seen = self.seen[eng]
        for (sem, val, src) in waits:
            if eng == "pe" and src == "pe":
                continue
            if seen.get(sem, 0) >= val:
                continue
            if need.get(sem, (0,))[0] < val:
                need[sem] = (val,)
        out = []
        for sem, (val,) in need.items():
            seen[sem] = val
            out.append((sem, val))
        return out

    def _commit(self, ev, reads, writes):
        sem, val, src = ev
        for t in reads:
            t.r[sem] = (val, src)
        for t in writes:
            t.w[sem] = (val, src)

    def op(self, eng, fn, reads=(), writes=()):
        waits = self._collect(eng, reads, writes)
        self.ecnt[eng] += 1
        ev = (self.esem[eng], self.ecnt[eng], eng)
        self.q[eng].append(("op", waits, [fn], (self.esem[eng], 1)))
        self._commit(ev, reads, writes)
        return ev

    def group(self, eng, fns, reads=(), writes=()):
        waits = self._collect(eng, reads, writes)
        self.ecnt[eng] += 1
        ev = (self.esem[eng], self.ecnt[eng], eng)
        self.q[eng].append(("op", waits, list(fns), (self.esem[eng], 1)))
        self._commit(ev, reads, writes)
        return ev

    def dma(self, qn, out_t, out_ap, in_t, in_ap, is_output=False, **kw):
        reads = [in_t]
        writes = [out_t]
        waits = self._collect(qn, reads, writes)
        pos = self.ringpos[qn]
        self.ringpos[qn] = (pos + 1) % RING_N[qn]
        sem = self.ring[qn][pos]
        prev = self.ringcnt[qn][pos]
        if prev > 0 and self.seen[qn].get(sem, 0) < prev:
            waits.append((sem, prev))
            self.seen[qn][sem] = prev
        self.ringcnt[qn][pos] = prev + 16
        ev = (sem, prev + 16, "dma")

        def fn(e, out_ap=out_ap, in_ap=in_ap, kw=kw):
            return e.dma_start(out=out_ap, in_=in_ap, **kw)

        self.q[qn].append(("op", waits, [fn], (sem, 16)))
        self._commit(ev, reads, writes)
        if is_output:
            self.out_events.append(ev)
        return ev

    def collective(self, kind, out_t, out_ap, in_t, in_ap, groups):
        qn = "pool"
        waits = self._collect(qn, [in_t], [out_t])
        pos = self.ringpos[qn]
        self.ringpos[qn] = (pos + 1) % RING_N[qn]
        sem = self.ring[qn][pos]
        prev = self.ringcnt[qn][pos]
        if prev > 0 and self.seen[qn].get(sem, 0) < prev:
            waits.append((sem, prev))
            self.seen[qn][sem] = prev
        self.ringcnt[qn][pos] = prev + 16
        ev = (sem, prev + 16, "dma")

        def fn(e):
            return e.collective_compute(kind, ALU.bypass, replica_groups=groups, ins=[in_ap], outs=[out_ap])

        self.q[qn].append(("op", waits, [fn], (sem, 16)))
        self._commit(ev, [in_t], [out_t])
        return ev

    def barrier(self):
        targets = [(self.esem[e], self.ecnt[e]) for e in ENGS if self.ecnt[e] > 0]
        for qn in self.ring:
            for i in range(RING_N[qn]):
                if self.ringcnt[qn][i] > 0:
                    targets.append((self.ring[qn][i], self.ringcnt[qn][i]))
        for e in ENGS:
            waits = [(s, v) for (s, v) in targets if self.seen[e].get(s, 0) < v]
            for s, v in waits:
                self.seen[e][s] = v
            self.q[e].append(("wait", waits, [], None))

    def finish(self):
        need = {}
        for (sem, val, _) in self.out_events:
            need[sem] = max(need.get(sem, 0), val)
        self.q["sp"].append(("wait", list(need.items()), [], None))

    def emit(self):
        nc = self.nc
        eobj = {"pe": "tensor", "dve": "vector", "act": "scalar", "pool": "gpsimd", "sp": "sync"}
        with nc.Block() as block:
            for en in ENGS:
                items = self.q[en]

                def body(e, items=items):
                    for (_, waits, fns, inc) in items:
                        for (sem, val) in waits:
                            e.wait_ge(sem, val)
                        last = None
                        for f in fns:
                            last = f(e)
                        if inc is not None:
                            last.then_inc(inc[0], inc[1])

                getattr(block, eobj[en])(body)


D = 2048
KC = 16
FF = 5632
FT = 44
EPS = 1e-6


class Common:
    def __init__(self, P):
        self.P = P
        self.ones = P.sb([128, 128], BF16, "ones")
        self.eps = P.sb([128, 1], F32, "eps")
        P.op("pool", lambda e: e.memset(self.ones[:], 1.0), writes=[self.ones])
        P.op("pool", lambda e: e.memset(self.eps[:], EPS), writes=[self.eps])
        self.banks = [P.ps([128, 512], F32, f"bank{i}") for i in range(8)]
        self.xt = [P.sb([128, 512], F32, "xt") for _ in range(4)]
        self.ft = [P.sb([128, 512], F32, "ftile") for _ in range(3)]
        self.sq = [P.sb([128, 512], BF16, "sq") for _ in range(4)]
        self.tmp = [P.sb([128, 512], F32, "tmp") for _ in range(2)]
        self.rstd = [P.sb([128, 512], F32, "rstd") for _ in range(2)]
        self.rstdA = [P.sb([128, 512], F32, "rstdA") for _ in range(2)]
        self.sqS = [P.sb([128, 512], BF16, "sqS") for _ in range(6)]
        self.hb = [P.sb([128, 512], BF16, "hb") for _ in range(3)]
        self.cnt = {}

    def rot(self, lst, key):
        i = self.cnt.get(key, 0)
        self.cnt[key] = i + 1
        return lst[i % len(lst)]


def emit_rstd(P, C, stats_bank, rstd_t, n=D):
    tmp = C.rot(C.tmp, "tmp")
    P.op("act", lambda e: e.activation(out=tmp[:], in_=stats_bank[:], func=AF.Sqrt,
                                       bias=C.eps[:, 0:1], scale=1.0 / n),
         reads=[stats_bank, C.eps], writes=[tmp])
    P.op("dve", lambda e: e.reciprocal(out=rstd_t[:], in_=tmp[:]), reads=[tmp], writes=[rstd_t])


def emit_ffn(P, C, res, x_in, x_out, wg, wu, wd, g, jpre, jpost, ntok, h_out=None, jnext=None):
    h = res["h"]
    act = res["act"]
    wgs, wus, wds = res["wgs"], res["wus"], res["wds"]
    f_scr = res["f_scr"]
    bk = C.banks
    nblk = ntok // 1024
    for blk in range(nblk):
        for half in range(2):
            hg = blk * 2 + half
            tsl = slice(hg * 512, (hg + 1) * 512)
            hsl = slice(half * 512, (half + 1) * 512)
            st = bk[6 + half]
            for kc in range(KC):
                xt = C.rot(C.xt, "xt")
                P.dma("sp", xt, xt[:], x_in[(kc, hg)], x_in[(kc, hg)].h[kc, :, tsl])
                sq = C.rot(C.sq, "sq")
                P.op("act", lambda e, sq=sq, xt=xt: e.activation(out=sq[:], in_=xt[:], func=AF.Square),
                     reads=[xt], writes=[sq])
                P.op("pe", lambda e, sq=sq, kc=kc, st=st: e.matmul(st[:], lhsT=C.ones[:], rhs=sq[:],
                                                                     start=(kc == 0), stop=(kc == KC - 1)),
                     reads=[sq, C.ones], writes=[st])
            rstd = C.rstd[half]
            emit_rstd(P, C, st, rstd)
            for kc in range(KC):
                xt = C.rot(C.xt, "xt")
                P.dma("sp", xt, xt[:], x_in[(kc, hg)], x_in[(kc, hg)].h[kc, :, tsl])
                P.op("dve", lambda e, xt=xt, kc=kc, rstd=rstd, hsl=hsl: e.scalar_tensor_tensor(
                    out=h[:, kc, hsl], in0=xt[:], scalar=g[:, jpre * 16 + kc: jpre * 16 + kc + 1],
                    in1=rstd[:], op0=ALU.mult, op1=ALU.mult),
                    reads=[xt, rstd, g], writes=[h])
        for ft in range(FT):
            wgt = C.rot(wgs, "wg")
            wut = C.rot(wus, "wu")
            P.dma("pool", wgt, wgt[:], wg, wg.h[ft])
            P.dma("pool", wut, wut[:], wu, wu.h[ft])
            for half in range(2):
                hsl = slice(half * 512, (half + 1) * 512)
                pg = bk[0 + half]
                pu = bk[2 + half]
                P.group("pe", [
                    (lambda e, kc=kc, pg=pg, wgt=wgt, hsl=hsl: e.matmul(
                        pg[:], lhsT=wgt[:, kc, :], rhs=h[:, kc, hsl], start=(kc == 0), stop=(kc == KC - 1)))
                    for kc in range(KC)], reads=[wgt, h], writes=[pg])
                P.group("pe", [
                    (lambda e, kc=kc, pu=pu, wut=wut, hsl=hsl: e.matmul(
                        pu[:], lhsT=wut[:, kc, :], rhs=h[:, kc, hsl], start=(kc == 0), stop=(kc == KC - 1)))
                    for kc in range(KC)], reads=[wut, h], writes=[pu])
                sl = C.rot(C.hb, "hb")
                P.op("act", lambda e, sl=sl, pg=pg: e.activation(out=sl[:], in_=pg[:], func=AF.Silu),
                     reads=[pg], writes=[sl])
                P.op("dve", lambda e, sl=sl, pu=pu, ft=ft, hsl=hsl: e.tensor_tensor(
                    out=act[:, ft, hsl], in0=pu[:], in1=sl[:], op=ALU.mult),
                    reads=[pu, sl], writes=[act])
        pend = None
        for m in range(KC):
            wdt = C.rot(wds, "wd")
            for q in range(4):
                P.dma("pool", wdt, wdt[:, q * 11:(q + 1) * 11, :], wd, wd.h[m, :, q * 11:(q + 1) * 11, :])
            for half in range(2):
                hg = blk * 2 + half
                tsl = slice(hg * 512, (hg + 1) * 512)
                hsl = slice(half * 512, (half + 1) * 512)
                pd = bk[4 + half]
                st = bk[6 + half]
                P.group("pe", [
                    (lambda e, fc=fc, pd=pd, wdt=wdt, hsl=hsl: e.matmul(
                        pd[:], lhsT=wdt[:, fc, :], rhs=act[:, fc, hsl], start=(fc == 0), stop=(fc == FT - 1)))
                    for fc in range(FT)], reads=[wdt, act], writes=[pd])
                ftile = C.rot(C.ft, "ft")
                P.op("act", lambda e, ftile=ftile, pd=pd: e.copy(out=ftile[:], in_=pd[:]),
                     reads=[pd], writes=[ftile])
                sq = C.rot(C.sq, "sq")
                P.op("act", lambda e, sq=sq, ftile=ftile: e.activation(out=sq[:], in_=ftile[:], func=AF.Square),
                     reads=[ftile], writes=[sq])
                if pend is not None:
                    pend()
                pend = (lambda sq=sq, m=m, st=st: P.op("pe", lambda e: e.matmul(
                    st[:], lhsT=C.ones[:], rhs=sq[:], start=(m == 0), stop=(m == KC - 1)),
                    reads=[sq, C.ones], writes=[st]))
                P.dma("act", f_scr[(m, hg)], f_scr[(m, hg)].h[m, :, tsl], ftile, ftile[:])
        if pend is not None:
            pend()
        for half in range(2):
            hg = blk * 2 + half
            tsl = slice(hg * 512, (hg + 1) * 512)
            st = bk[6 + half]
            rstd = C.rstd[half]
            emit_rstd(P, C, st, rstd)
            st2 = bk[4 + half]
            for kc in range(KC):
                ftile = C.rot(C.ft, "ft")
                xt = C.rot(C.xt, "xt")
                P.dma("sp", ftile, ftile[:], f_scr[(kc, hg)], f_scr[(kc, hg)].h[kc, :, tsl])
                P.dma("sp", xt, xt[:], x_in[(kc, hg)], x_in[(kc, hg)].h[kc, :, tsl])
                P.op("dve", lambda e, ftile=ftile, kc=kc, rstd=rstd: e.scalar_tensor_tensor(
                    out=ftile[:], in0=ftile[:], scalar=g[:, jpost * 16 + kc: jpost * 16 + kc + 1],
                    in1=rstd[:], op0=ALU.mult, op1=ALU.mult),
                    reads=[rstd, g], writes=[ftile])
                P.op("dve", lambda e, ftile=ftile, xt=xt: e.scalar_tensor_tensor(
                    out=xt[:], in0=ftile[:], scalar=0.5, in1=xt[:], op0=ALU.mult, op1=ALU.add),
                    reads=[ftile], writes=[xt])
                P.dma("act", x_out[(kc, hg)], x_out[(kc, hg)].h[kc, :, tsl], xt, xt[:],
                      is_output=x_out.get("is_output", False))
                if h_out is not None:
                    sq = C.rot(C.sq, "sq")
                    P.op("act", lambda e, sq=sq, xt=xt: e.activation(out=sq[:], in_=xt[:], func=AF.Square),
                         reads=[xt], writes=[sq])
                    P.op("pe", lambda e, sq=sq, kc=kc, st2=st2: e.matmul(
                        st2[:], lhsT=C.ones[:], rhs=sq[:], start=(kc == 0), stop=(kc == KC - 1)),
                        reads=[sq, C.ones], writes=[st2])
            if h_out is not None:
                emit_rstd(P, C, st2, rstd)
                for kc in range(KC):
                    xt = C.rot(C.xt, "xt")
                    P.dma("sp", xt, xt[:], x_out[(kc, hg)], x_out[(kc, hg)].h[kc, :, tsl])
                    hb = C.rot(C.hb, "hb")
                    P.op("dve", lambda e, xt=xt, kc=kc, rstd=rstd, hb=hb: e.scalar_tensor_tensor(
                        out=hb[:], in0=xt[:], scalar=g[:, jnext * 16 + kc: jnext * 16 + kc + 1],
                        in1=rstd[:], op0=ALU.mult, op1=ALU.mult),
                        reads=[xt, rstd, g], writes=[hb])
                    P.dma("act", h_out[(kc, hg)], h_out[(kc, hg)].h[kc, :, tsl], hb, hb[:],
                          is_output=h_out.get("is_output", False))


def regions(P, name, shape, dt, kind, nk, nh):
    base = P.dram(name, shape, dt, kind)
    d = {}
    for k in range(nk):
        for hh in range(nh):
            d[(k, hh)] = T(base.h, f"{name}_{k}_{hh}")
    d["is_output"] = (kind == "ExternalOutput")
    d["base"] = base
    return d


def ffn_resources(P, ntok, scr_name="f_scr", f_scr=None):
    res = {}
    res["h"] = P.sb([128, KC, 1024], BF16, "h")
    res["act"] = P.sb([128, FT, 1024], BF16, "act")
    res["wgs"] = [P.sb([128, KC, 128], BF16, "wg") for _ in range(3)]
    res["wus"] = [P.sb([128, KC, 128], BF16, "wu") for _ in range(3)]
    res["wds"] = [P.sb([128, FT, 128], BF16, "wd") for _ in range(2)]
    res["f_scr"] = f_scr if f_scr is not None else regions(P, scr_name, [KC, 128, ntok], F32, "Internal", KC, ntok // 512)
    return res


def emit_outproj(P, C, res, ym, wo, x_in, x_out, g, jpost, ntok, sel=None, alt_chunks=0):
    f_scr = res["f_scr"]
    wgs = res["wgs"]
    bk = C.banks
    for c in range(ntok // 512):
        tsl = slice(c * 512, (c + 1) * 512)
        yt = ym(c)
        st = bk[6 + (c % 2)]
        pend = None
        for m in range(KC):
            wt = C.rot(wgs, "wg")
            P.dma("pool", wt, wt[:], wo, wo.h[m])
            pd = bk[4 + (m % 2)]
            P.group("pe", [(lambda e, kc=kc, pd=pd, wt=wt, yt=yt: e.matmul(
                pd[:], lhsT=wt[:, kc, :], rhs=yt[:, kc, :], start=(kc == 0), stop=(kc == KC - 1)))
                for kc in range(KC)], reads=[wt, yt], writes=[pd])
            ftile = C.rot(C.ft, "ft")
            P.op("act", lambda e, ftile=ftile, pd=pd: e.copy(out=ftile[:], in_=pd[:]), reads=[pd], writes=[ftile])
            sq = C.rot(C.sq, "sq")
            P.op("act", lambda e, sq=sq, ftile=ftile: e.activation(out=sq[:], in_=ftile[:], func=AF.Square),
                 reads=[ftile], writes=[sq])
            if pend is not None:
                pend()
            pend = (lambda sq=sq, m=m, st=st: P.op("pe", lambda e: e.matmul(
                st[:], lhsT=C.ones[:], rhs=sq[:], start=(m == 0), stop=(m == KC - 1)),
                reads=[sq, C.ones], writes=[st]))
            P.dma("act", f_scr[(m, c)], f_scr[(m, c)].h[m, :, tsl], ftile, ftile[:])
        if pend is not None:
            pend()
        rstd = C.rstd[c % 2]
        emit_rstd(P, C, st, rstd)
        for kc in range(KC):
            ftile = C.rot(C.ft, "ft")
            xt = C.rot(C.xt, "xt")
            P.dma("sp", ftile, ftile[:], f_scr[(kc, c)], f_scr[(kc, c)].h[kc, :, tsl])
            P.dma("sp", xt, xt[:], x_in[(kc, c)], x_in[(kc, c)].h[kc, :, tsl])
            if sel is not None:
                c2 = c + alt_chunks
                xb = C.rot(C.xt, "xt")
                P.dma("sp", xb, xb[:], x_in[(kc, c2)], x_in[(kc, c2)].h[kc, :, c2 * 512:(c2 + 1) * 512])
                P.op("pool", lambda e, xt=xt: e.tensor_scalar(out=xt[:], in0=xt[:], scalar1=sel[:, 0:1], scalar2=None,
                                                              op0=ALU.mult), reads=[sel], writes=[xt])
                P.op("dve", lambda e, xt=xt, xb=xb: e.scalar_tensor_tensor(
                    out=xt[:], in0=xb[:], scalar=sel[:, 1:2], in1=xt[:], op0=ALU.mult, op1=ALU.add),
                    reads=[xb, sel], writes=[xt])
            P.op("dve", lambda e, ftile=ftile, kc=kc, rstd=rstd: e.scalar_tensor_tensor(
                out=ftile[:], in0=ftile[:], scalar=g[:, jpost * 16 + kc: jpost * 16 + kc + 1],
                in1=rstd[:], op0=ALU.mult, op1=ALU.mult), reads=[rstd, g], writes=[ftile])
            P.op("dve", lambda e, ftile=ftile, xt=xt: e.tensor_tensor(
                out=xt[:], in0=ftile[:], in1=xt[:], op=ALU.add), reads=[ftile], writes=[xt])
            P.dma("act", x_out[(kc, c)], x_out[(kc, c)].h[kc, :, tsl], xt, xt[:],
                  is_output=x_out.get("is_output", False))


FLUSH = "FLUSH"


def emit_ffn_pipelined(P, C, res, x_in, x_out, wg, wu, wd, g, jpre, jpost, ntok, h_out=None, jnext=None):
    h = res["h"]
    act = res["act"]
    wgs, wus, wds = res["wgs"], res["wus"], res["wds"]
    f_scr = res["f_scr"]
    bk = C.banks
    nblk = ntok // 1024
    rstdA = C.rstdA

    def stats_mm(st, sq, first, last):
        return lambda: P.op("pe", lambda e: e.matmul(st[:], lhsT=C.ones[:], rhs=sq[:], start=first, stop=last),
                            reads=[sq, C.ones], writes=[st])

    def stageA(blk):
        for half in range(2):
            hg = blk * 2 + half
            tsl = slice(hg * 512, (hg + 1) * 512)
            hsl = slice(half * 512, (half + 1) * 512)
            st = bk[0 + half]
            for kc in range(KC):
                xt = C.rot(C.xt, "xt")
                P.dma("sp", xt, xt[:], x_in[(kc, hg)], x_in[(kc, hg)].h[kc, :, tsl])
                sq = C.rot(C.sqS, "sqS")
                P.op("act", lambda e, sq=sq, xt=xt: e.activation(out=sq[:], in_=xt[:], func=AF.Square),
                     reads=[xt], writes=[sq])
                yield stats_mm(st, sq, kc == 0, kc == KC - 1)
            yield FLUSH
            rstd = rstdA[half]
            emit_rstd(P, C, st, rstd)
            for kc in range(KC):
                xt = C.rot(C.xt, "xt")
                P.dma("sp", xt, xt[:], x_in[(kc, hg)], x_in[(kc, hg)].h[kc, :, tsl])
                P.op("dve", lambda e, xt=xt, kc=kc, rstd=rstd, hsl=hsl: e.scalar_tensor_tensor(
                    out=h[:, kc, hsl], in0=xt[:], scalar=g[:, jpre * 16 + kc: jpre * 16 + kc + 1],
                    in1=rstd[:], op0=ALU.mult, op1=ALU.mult),
                    reads=[xt, rstd, g], writes=[h])
                yield None

    def stageB(blk):
        for ft in range(FT):
            wgt = C.rot(wgs, "wg")
            wut = C.rot(wus, "wu")
            P.dma("pool", wgt, wgt[:], wg, wg.h[ft])
            P.dma("pool", wut, wut[:], wu, wu.h[ft])
            for half in range(2):
                hsl = slice(half * 512, (half + 1) * 512)
                pg = bk[0 + half]
                pu = bk[2 + half]
                P.group("pe", [
                    (lambda e, kc=kc, pg=pg, wgt=wgt, hsl=hsl: e.matmul(
                        pg[:], lhsT=wgt[:, kc, :], rhs=h[:, kc, hsl], start=(kc == 0), stop=(kc == KC - 1)))
                    for kc in range(KC)], reads=[wgt, h], writes=[pg])
                P.group("pe", [
                    (lambda e, kc=kc, pu=pu, wut=wut, hsl=hsl: e.matmul(
                        pu[:], lhsT=wut[:, kc, :], rhs=h[:, kc, hsl], start=(kc == 0), stop=(kc == KC - 1)))
                    for kc in range(KC)], reads=[wut, h], writes=[pu])
                sl = C.rot(C.hb, "hb")
                P.op("act", lambda e, sl=sl, pg=pg: e.activation(out=sl[:], in_=pg[:], func=AF.Silu),
                     reads=[pg], writes=[sl])
                P.op("dve", lambda e, sl=sl, pu=pu, ft=ft, hsl=hsl: e.tensor_tensor(
                    out=act[:, ft, hsl], in0=pu[:], in1=sl[:], op=ALU.mult),
                    reads=[pu, sl], writes=[act])
            yield None

    def stageC(blk):
        pend = None
        for m in range(KC):
            wdt = C.rot(wds, "wd")
            for q in range(4):
                P.dma("pool", wdt, wdt[:, q * 11:(q + 1) * 11, :], wd, wd.h[m, :, q * 11:(q + 1) * 11, :])
            for half in range(2):
                hg = blk * 2 + half
                tsl = slice(hg * 512, (hg + 1) * 512)
                hsl = slice(half * 512, (half + 1) * 512)
                pd = bk[4 + half]
                st = bk[6 + half]
                P.group("pe", [
                    (lambda e, fc=fc, pd=pd, wdt=wdt, hsl=hsl: e.matmul(
                        pd[:], lhsT=wdt[:, fc, :], rhs=act[:, fc, hsl], start=(fc == 0), stop=(fc == FT - 1)))
                    for fc in range(FT)], reads=[wdt, act], writes=[pd])
                ftile = C.rot(C.ft, "ft")
                P.op("act", lambda e, ftile=ftile, pd=pd: e.copy(out=ftile[:], in_=pd[:]),
                     reads=[pd], writes=[ftile])
                sq = C.rot(C.sq, "sq")
                P.op("act", lambda e, sq=sq, ftile=ftile: e.activation(out=sq[:], in_=ftile[:], func=AF.Square),
                     reads=[ftile], writes=[sq])
                if pend is not None:
                    pend()
                pend = stats_mm(st, sq, m == 0, m == KC - 1)
                P.dma("act", f_scr[(m, hg)], f_scr[(m, hg)].h[m, :, tsl], ftile, ftile[:])
                yield None
        pend()
        yield None

    def stageDE(blk):
        for half in range(2):
            hg = blk * 2 + half
            tsl = slice(hg * 512, (hg + 1) * 512)
            st = bk[6 + half]
            rstd = C.rstd[half]
            emit_rstd(P, C, st, rstd)
            st2 = bk[4 + half]
            for kc in range(KC):
                ftile = C.rot(C.ft, "ft")
                xt = C.rot(C.xt, "xt")
                P.dma("sp", ftile, ftile[:], f_scr[(kc, hg)], f_scr[(kc, hg)].h[kc, :, tsl])
                P.dma("sp", xt, xt[:], x_in[(kc, hg)], x_in[(kc, hg)].h[kc, :, tsl])
                P.op("dve", lambda e, ftile=ftile, kc=kc, rstd=rstd: e.scalar_tensor_tensor(
                    out=ftile[:], in0=ftile[:], scalar=g[:, jpost * 16 + kc: jpost * 16 + kc + 1],
                    in1=rstd[:], op0=ALU.mult, op1=ALU.mult),
                    reads=[rstd, g], writes=[ftile])
                P.op("dve", lambda e, ftile=ftile, xt=xt: e.scalar_tensor_tensor(
                    out=xt[:], in0=ftile[:], scalar=0.5, in1=xt[:], op0=ALU.mult, op1=ALU.add),
                    reads=[ftile], writes=[xt])
                P.dma("act", x_out[(kc, hg)], x_out[(kc, hg)].h[kc, :, tsl], xt, xt[:],
                      is_output=x_out.get("is_output", False))
                if h_out is not None:
                    sq = C.rot(C.sqS, "sqS")
                    P.op("act", lambda e, sq=sq, xt=xt: e.activation(out=sq[:], in_=xt[:], func=AF.Square),
                         reads=[xt], writes=[sq])
                    yield stats_mm(st2, sq, kc == 0, kc == KC - 1)
                else:
                    yield None
            if h_out is not None:
                yield FLUSH
                emit_rstd(P, C, st2, rstd)
                for kc in range(KC):
                    xt = C.rot(C.xt, "xt")
                    P.dma("sp", xt, xt[:], x_out[(kc, hg)], x_out[(kc, hg)].h[kc, :, tsl])
                    hb = C.rot(C.hb, "hb")
                    P.op("dve", lambda e, xt=xt, kc=kc, rstd=rstd, hb=hb: e.scalar_tensor_tensor(
                        out=hb[:], in0=xt[:], scalar=g[:, jnext * 16 + kc: jnext * 16 + kc + 1],
                        in1=rstd[:], op0=ALU.mult, op1=ALU.mult),
                        reads=[xt, rstd, g], writes=[hb])
                    P.dma("act", h_out[(kc, hg)], h_out[(kc, hg)].h[kc, :, tsl], hb, hb[:],
                          is_output=h_out.get("is_output", False))
                    yield None

    def run(main, side, per_iter):
        deferred = []
        side_done = side is None

        def side_step():
            nonlocal side_done
            try:
                r = next(side)
            except StopIteration:
                side_done = True
                return
            if r is FLUSH:
                for t in deferred:
                    t()
                deferred.clear()
            elif r is not None:
                deferred.append(r)

        if main is None:
            while not side_done:
                side_step()
                for t in deferred:
                    t()
                deferred.clear()
            return
        for _ in main:
            for t in deferred:
                t()
            deferred.clear()
            if not side_done:
                for _ in range(per_iter):
                    if side_done:
                        break
                    side_step()
        while not side_done:
            side_step()
            for t in deferred:
                t()
            deferred.clear()
        for t in deferred:
            t()
        deferred.clear()

    run(None, stageA(0), 0)
    for blk in range(nblk):
        run(stageB(blk), stageDE(blk - 1) if blk > 0 else None, 2)
        run(stageC(blk), stageA(blk + 1) if blk + 1 < nblk else None, 2)
    run(None, stageDE(nblk - 1), 0)


L = 4096
KC = 16
SCALE = 128 ** -0.5
POOL_WINDOWS = (2, 4, 8, 16)


def emit_proj_fm(P, psb, hT, wt, outs, evac):
    pass


def emit_emix(P, hT, wq, wk, wv, wp, pw_d, ps_d, cst_d, ymT, selw_d, tile_of=lambda j: j, is_output=True):
    nc = P.nc
    base_es = P.es
    cst = P.sb([128, 768], F32, "cst")
    P.dma("sp", cst, cst[:], cst_d, cst_d.h[:, :])
    ident_bf = P.sb([128, 128], BF16, "identbf")
    mask_f = cst
    mask_bf = P.sb([128, 128], BF16, "maskbf")
    P.op("dve", lambda e: e.tensor_copy(out=ident_bf[:], in_=cst[:, 0:128]), reads=[cst], writes=[ident_bf])
    P.op("dve", lambda e: e.tensor_copy(out=mask_bf[:], in_=cst[:, 128:256]), reads=[cst], writes=[mask_bf])
    one_c = P.sb([128, 1], F32, "onec")
    P.op("pool", lambda e: e.memset(one_c[:], 1.0), writes=[one_c])
    pscale = P.sb([128, 2], F32, "pscale")
    P.dma("sp", pscale, pscale[:], ps_d, ps_d.h[:, :])
    selw = P.sb([128, 2, 21], F32, "selw")
    P.dma("sp", selw, selw[:], selw_d, selw_d.h[:, :, :])
    hch = [P.sb([128, KC, 256], BF16, "hch") for _ in range(2)]
    pbank = [P.ps([128, 512], F32, "pbank") for _ in range(2)]
    cnt = {}

    def rot(lst, key):
        i = cnt.get(key, 0)
        cnt[key] = i + 1
        return lst[i % len(lst)]

    def load_h(c):
        t = rot(hch, "hch")
        for kq in range(4):
            P.dma("sp", t, t[:, kq * 4:(kq + 1) * 4, :], hT[(0, c)],
                  hT[(0, c)].h[kq * 4:(kq + 1) * 4, :, c * 256:(c + 1) * 256].rearrange("k p t -> p k t"))
        return t

    with ExitStack() as es:
        P.es = es
        wps = [P.sb([128, KC, 128], BF16, "wp") for _ in range(2)]
        pws = [P.sb([128, 128], BF16, "pw") for _ in range(2)]
        for gi in range(2):
            P.dma("pool", wps[gi], wps[gi][:], wp, wp.h[gi])
            P.dma("pool", pws[gi], pws[gi][:], pw_d, pw_d.h[gi])
        u = [P.sb([128, 16 + L], F32, "upool") for _ in range(2)]
        sa = P.sb([128, 16 + L], F32, "sa")
        sbb = P.sb([128, 16 + L], F32, "sbb")
        pooled = P.sb([128, L], BF16, "pooled")
        yp = [P.sb([128, 512], BF16, "yp") for _ in range(2)]
        for t in u + [sa, sbb]:
            P.op("pool", lambda e, t=t: e.memset(t[:, 0:16], 0.0), writes=[t])
        for c in range(16):
            ht = load_h(c)
            for gi in range(2):
                bk = rot(pbank, "pb")
                P.group("pe", [(lambda e, kc=kc, bk=bk, gi=gi, ht=ht: e.matmul(
                    bk[:, 0:256], lhsT=wps[gi][:, kc, :], rhs=ht[:, kc, :], start=(kc == 0), stop=(kc == KC - 1)))
                    for kc in range(KC)], reads=[wps[gi], ht], writes=[bk])
                P.op("act", lambda e, bk=bk, gi=gi, c=c: e.copy(out=u[gi][:, 16 + c * 256:16 + (c + 1) * 256],
                                                                 in_=bk[:, 0:256]), reads=[bk], writes=[u[gi]])
        sc = P.sb([128, L], F32, "sc")
        acc = P.sb([128, L], F32, "acc")
        for gi in range(2):
            src = u[gi]
            bufs = [sa, sbb]
            for k in range(4):
                sh = 1 << k
                dst = bufs[k % 2]
                P.op("dve", lambda e, dst=dst, src=src, sh=sh: e.tensor_tensor(
                    out=dst[:, 16:16 + L], in0=src[:, 16:16 + L], in1=src[:, 16 - sh:16 + L - sh], op=ALU.add),
                    reads=[src], writes=[dst])
                src = dst
                if k == 0:
                    P.op("dve", lambda e, dst=dst, gi=gi: e.tensor_scalar(
                        out=acc[:], in0=dst[:, 16:16 + L], scalar1=selw[:, gi, 0:1], scalar2=None, op0=ALU.mult),
                        reads=[dst, selw], writes=[acc])
                else:
                    P.op("dve", lambda e, dst=dst, gi=gi, k=k: e.scalar_tensor_tensor(
                        out=acc[:], in0=dst[:, 16:16 + L], scalar=selw[:, gi, k:k + 1], in1=acc[:],
                        op0=ALU.mult, op1=ALU.add), reads=[dst, selw], writes=[acc])
            P.op("dve", lambda e, gi=gi: e.tensor_scalar(
                out=sc[:], in0=acc[:], scalar1=selw[:, gi, 4:5], scalar2=None, op0=ALU.mult),
                reads=[acc, selw], writes=[sc])
            P.op("dve", lambda e, gi=gi: e.tensor_tensor(
                out=sc[:, 0:16], in0=acc[:, 0:16], in1=selw[:, gi, 5:21], op=ALU.mult),
                reads=[acc, selw], writes=[sc])
            P.op("dve", lambda e, gi=gi: e.tensor_tensor(
                out=pooled[:], in0=sc[:], in1=u[gi][:, 16:16 + L], op=ALU.subtract),
                reads=[sc, u[gi]], writes=[pooled])
            for c in range(8):
                bk = rot(pbank, "pb")
                P.op("pe", lambda e, bk=bk, gi=gi, c=c: e.matmul(
                    bk[:], lhsT=pws[gi][:], rhs=pooled[:, c * 512:(c + 1) * 512], start=True, stop=True),
                    reads=[pws[gi], pooled], writes=[bk])
                y = rot(yp, "yp")
                P.op("act", lambda e, y=y, bk=bk, gi=gi: e.activation(
                    out=y[:], in_=bk[:], func=AF.Copy, scale=pscale[:, gi:gi + 1]),
                    reads=[bk, pscale], writes=[y])
                P.dma("act", ymT[(tile_of(gi), c)], ymT[(tile_of(gi), c)].h[tile_of(gi), :, c * 512:(c + 1) * 512], y, y[:], is_output=is_output)
        P.barrier()
    for hp in range(3):
        with ExitStack() as es:
            P.es = es
            wqs = [P.sb([128, KC, 128], BF16, "wq") for _ in range(2)]
            wks = [P.sb([128, KC, 128], BF16, "wk") for _ in range(2)]
            wvs = P.sb([128, KC, 256], BF16, "wv")
            for j in range(2):
                P.dma("pool", wqs[j], wqs[j][:], wq, wq.h[hp * 2 + j])
                P.dma("pool", wks[j], wks[j][:], wk, wk.h[hp * 2 + j])
            P.dma("pool", wvs, wvs[:], wv, wv.h[hp])
            qT = [P.sb([128, L], BF16, "qT") for _ in range(2)]
            kT = [P.sb([128, L], BF16, "kT") for _ in range(2)]
            v = P.sb([128, 32, 256], BF16, "v")
            for c in range(16):
                ht = load_h(c)
                for j in range(2):
                    for (ws, dst) in ((wqs[j], qT[j]), (wks[j], kT[j])):
                        bk = rot(pbank, "pb")
                        P.group("pe", [(lambda e, kc=kc, bk=bk, ws=ws, ht=ht: e.matmul(
                            bk[:, 0:256], lhsT=ws[:, kc, :], rhs=ht[:, kc, :], start=(kc == 0), stop=(kc == KC - 1)))
                            for kc in range(KC)], reads=[ws, ht], writes=[bk])
                        P.op("act", lambda e, bk=bk, dst=dst, c=c: e.copy(
                            out=dst[:, c * 256:(c + 1) * 256], in_=bk[:, 0:256]), reads=[bk], writes=[dst])
                for tb in range(2):
                    bk = rot(pbank, "pb")
                    P.group("pe", [(lambda e, kc=kc, bk=bk, tb=tb, ht=ht: e.matmul(
                        bk[:, 0:256], lhsT=ht[:, kc, tb * 128:(tb + 1) * 128], rhs=wvs[:, kc, :],
                        start=(kc == 0), stop=(kc == KC - 1))) for kc in range(KC)],
                        reads=[wvs, ht], writes=[bk])
                    P.op("dve", lambda e, bk=bk, c=c, tb=tb: e.tensor_copy(
                        out=v[:, c * 2 + tb, :], in_=bk[:, 0:256]), reads=[bk], writes=[v])
            Pps = [P.sb([128, L + 1], F32, "Pp") for _ in range(2)]
            for Pp in Pps:
                P.op("pool", lambda e, Pp=Pp: e.memset(Pp[:, 0:1], 0.0), writes=[Pp])
            Eb = [P.sb([128, L], F32, "E") for _ in range(2)]
            wb = [P.sb([128, L], BF16, "w") for _ in range(2)]
            et = [P.sb([128, 512], F32, "et") for _ in range(2)]
            lt = [P.sb([128, 512], F32, "lt") for _ in range(2)]
            negT = [P.sb([128, 1], F32, "negT") for _ in range(2)]
            wTt = [P.sb([128, 512], BF16, "wT") for _ in range(3)]
            yT = [P.sb([128, 128], BF16, "yT") for _ in range(2)]
            zbank = [P.ps([128, 512], F32, "zbank") for _ in range(2)]
            tbank = [P.ps([128, 512], BF16, "tbank") for _ in range(2)]
            obank = [P.ps([128, 128], F32, "obank") for _ in range(2)]
            ones512 = cst
            def p1(i, j):
                    jg = 2 + hp * 2 + j
                    Pp = Pps[j]
                    nk = (i + 1) * 128
                    nch = (nk + 511) // 512
                    E = rot(Eb, "E")
                    w = rot(wb, "w")
                    for c in range(nch):
                        k0 = c * 512
                        kn = min(512, nk - k0)
                        zb = rot(zbank, "zb")
                        P.op("pe", lambda e, zb=zb, j=j, i=i, k0=k0, kn=kn: e.matmul(
                            zb[:, 0:kn], lhsT=qT[j][:, i * 128:(i + 1) * 128], rhs=kT[j][:, k0:k0 + kn],
                            start=True, stop=True), reads=[qT[j], kT[j]], writes=[zb])
                        e_t = rot(et, "et")
                        l_t = rot(lt, "lt")
                        P.op("act", lambda e, e_t=e_t, zb=zb, kn=kn: e.activation(
                            out=e_t[:, 0:kn], in_=zb[:, 0:kn], func=AF.Exp, scale=SCALE), reads=[zb], writes=[e_t])
                        P.op("act", lambda e, e_t=e_t, l_t=l_t, kn=kn: e.activation(
                            out=l_t[:, 0:kn], in_=e_t[:, 0:kn], func=AF.Ln, bias=one_c[:, 0:1], scale=1.0),
                            reads=[e_t, one_c], writes=[l_t])
                        if c == nch - 1:
                            P.op("pool", lambda e, l_t=l_t, kn=kn: e.tensor_tensor(
                                out=l_t[:, kn - 128:kn], in0=l_t[:, kn - 128:kn], in1=cst[:, 128:256], op=ALU.mult),
                                reads=[mask_f], writes=[l_t])
                        init = 0.0 if c == 0 else Pp[:, k0:k0 + 1]
                        P.op("dve", lambda e, l_t=l_t, k0=k0, kn=kn, init=init, Pp=Pp: e.tensor_tensor_scan(
                            out=Pp[:, 1 + k0:1 + k0 + kn], data0=cst[:, 256:256 + kn], data1=l_t[:, 0:kn],
                            initial=init, op0=ALU.mult, op1=ALU.add), reads=[l_t, ones512], writes=[Pp])
                        P.op("dve", lambda e, E=E, zb=zb, k0=k0, kn=kn, Pp=Pp: e.scalar_tensor_tensor(
                            out=E[:, k0:k0 + kn], in0=zb[:, 0:kn], scalar=SCALE, in1=Pp[:, k0:k0 + kn],
                            op0=ALU.mult, op1=ALU.add), reads=[zb, Pp], writes=[E])
                    nt = rot(negT, "negT")
                    P.op("dve", lambda e, nt=nt, nk=nk, Pp=Pp: e.tensor_scalar(
                        out=nt[:], in0=Pp[:, nk:nk + 1], scalar1=-1.0, scalar2=None, op0=ALU.mult),
                        reads=[Pp], writes=[nt])
                    return dict(i=i, j=j, jg=jg, nk=nk, nch=nch, E=E, w=w, nt=nt)

            def p2(st_):
                    i, j, jg, nk, nch, E, w, nt = (st_[k_] for k_ in ('i', 'j', 'jg', 'nk', 'nch', 'E', 'w', 'nt'))
                    for c in range(nch):
                        k0 = c * 512
                        kn = min(512, nk - k0)
                        P.op("act", lambda e, w=w, E=E, nt=nt, k0=k0, kn=kn: e.activation(
                            out=w[:, k0:k0 + kn], in_=E[:, k0:k0 + kn], func=AF.Exp, bias=nt[:, 0:1], scale=1.0),
                            reads=[E, nt], writes=[w])
                    P.op("pool", lambda e, w=w, nk=nk: e.tensor_tensor(
                        out=w[:, nk - 128:nk], in0=w[:, nk - 128:nk], in1=mask_bf[:], op=ALU.mult),
                        reads=[mask_bf], writes=[w])
                    ob = rot(obank, "ob")
                    groups = list(range(0, i + 1, 4))

                    def emit_T(s0):
                        ns = min(4, i + 1 - s0)
                        tb_ = rot(tbank, "tb")
                        P.group("pe", [(lambda e, tb_=tb_, w=w, s0=s0, q=q: e.transpose(
                            out=tb_[:, q * 128:(q + 1) * 128], in_=w[:, (s0 + q) * 128:(s0 + q + 1) * 128],
                            identity=ident_bf[:])) for q in range(ns)], reads=[w, ident_bf], writes=[tb_])
                        wT = rot(wTt, "wT")
                        P.op("dve", lambda e, wT=wT, tb_=tb_, ns=ns: e.tensor_copy(
                            out=wT[:, 0:ns * 128], in_=tb_[:, 0:ns * 128]), reads=[tb_], writes=[wT])
                        return (s0, ns, wT)

                    def emit_PV(g_):
                        s0, ns, wT = g_
                        P.group("pe", [(lambda e, ob=ob, wT=wT, s0=s0, q=q, j=j, i=i: e.matmul(
                            ob[:], lhsT=v[:, s0 + q, j * 128:(j + 1) * 128], rhs=wT[:, q * 128:(q + 1) * 128],
                            start=(s0 + q == 0), stop=(s0 + q == i))) for q in range(ns)],
                            reads=[v, wT], writes=[ob])

                    pg_ = None
                    for s0 in groups:
                        g_ = emit_T(s0)
                        if pg_ is not None:
                            emit_PV(pg_)
                        pg_ = g_
                    emit_PV(pg_)
                    y = rot(yT, "yT")
                    P.op("act", lambda e, y=y, ob=ob: e.copy(out=y[:], in_=ob[:]), reads=[ob], writes=[y])
                    P.dma("act", ymT[(tile_of(jg), i)], ymT[(tile_of(jg), i)].h[tile_of(jg), :, i * 128:(i + 1) * 128], y, y[:], is_output=is_output)

            pend_ = None
            for i in range(32):
                for j in range(2):
                    st_ = p1(i, j)
                    if pend_ is not None:
                        p2(pend_)
                    pend_ = st_
            p2(pend_)
            P.barrier()
    P.es = base_es

import math

L = 4096
KC = 16
NGP = 16
PI = math.pi
MAGIC = 12582912.0
TWO_PI_S = 2 * math.pi * (1 - 2e-6)


def emit_ossm(P, hT, wu_d, prm_d, bsm_d, ct_d, dd_d, cst_d, ysT, ys_ap=None, is_output=True):
    cnt = {}

    def rot(lst, key):
        i = cnt.get(key, 0)
        cnt[key] = i + 1
        return lst[i % len(lst)]

    cst = P.sb([128, 1152], F32, "cst")
    P.dma("sp", cst, cst[:], cst_d, cst_d.h[:, :])
    ident = cst
    prm = P.sb([128, 3, 16], F32, "prm")
    P.dma("sp", prm, prm[:], prm_d, prm_d.h[:, :, :])
    bsm = P.sb([128, 2, 16, 32], F32, "bsm")
    P.dma("sp", bsm, bsm[:], bsm_d, bsm_d.h[:, :, :, :])
    ctf = P.sb([128, 2, 16, 32], F32, "ctf")
    P.dma("sp", ctf, ctf[:], ct_d, ct_d.h[:, :, :, :])
    ddf = P.sb([32, 16, 32], F32, "ddf")
    P.dma("sp", ddf, ddf[:], dd_d, dd_d.h[:, :, :])
    wus = P.sb([128, NGP, KC, 32], BF16, "wus")
    for gp in range(NGP):
        P.dma("pool", wus, wus[:, gp], wu_d, wu_d.h[gp])
    negpi = P.sb([128, 1], F32, "negpi")
    P.op("pool", lambda e: e.memset(negpi[:], -PI), writes=[negpi])

    def small(name):
        return P.sb([128, 16], F32, name)

    dt, adt, th, dec, sa, ca, sn, cs = [small(n) for n in ["dt", "adt", "th", "dec", "sa", "ca", "sn", "cs"]]
    lre, lim, nr, den, f_re, f_im, tA, tB = [small(n) for n in ["lre", "lim", "nr", "den", "fre", "fim", "tA", "tB"]]
    a_re = lambda: prm[:, 0, :]
    a_im = lambda: prm[:, 1, :]
    P.op("act", lambda e: e.activation(out=dt[:], in_=prm[:, 2, :], func=AF.Exp), reads=[prm], writes=[dt])
    P.op("dve", lambda e: e.tensor_tensor(out=adt[:], in0=a_re(), in1=dt[:], op=ALU.mult), reads=[prm, dt], writes=[adt])
    P.op("dve", lambda e: e.tensor_tensor(out=th[:], in0=a_im(), in1=dt[:], op=ALU.mult), reads=[prm, dt], writes=[th])
    P.op("act", lambda e: e.activation(out=dec[:], in_=adt[:], func=AF.Exp), reads=[adt], writes=[dec])
    thn = small("thn")
    P.op("dve", lambda e: e.tensor_scalar(out=thn[:], in0=th[:], scalar1=1.0 / (2 * PI), scalar2=None, op0=ALU.mult),
         reads=[th], writes=[thn])

    def emit_sincos(src, dst_sin, dst_cos, mk):
        for (off, dst) in ((0.0, dst_sin), (0.25, dst_cos)):
            a2 = mk()
            k_ = mk()
            P.op("dve", lambda e, a2=a2, off=off: e.tensor_scalar(out=a2[:], in0=src[:], scalar1=off, scalar2=None, op0=ALU.add),
                 reads=[src], writes=[a2])
            P.op("dve", lambda e, a2=a2, k_=k_: e.tensor_scalar(out=k_[:], in0=a2[:], scalar1=MAGIC, scalar2=MAGIC,
                                                                 op0=ALU.add, op1=ALU.subtract), reads=[a2], writes=[k_])
            P.op("dve", lambda e, a2=a2, k_=k_: e.tensor_tensor(out=a2[:], in0=a2[:], in1=k_[:], op=ALU.subtract),
                 reads=[k_], writes=[a2])
            P.op("act", lambda e, a2=a2, dst=dst: e.activation(out=dst[:], in_=a2[:], func=AF.Sin, scale=TWO_PI_S),
                 reads=[a2], writes=[dst])

    emit_sincos(thn, sn, cs, lambda: small("sc_tmp"))
    P.op("dve", lambda e: e.tensor_tensor(out=lre[:], in0=dec[:], in1=cs[:], op=ALU.mult), reads=[dec, cs], writes=[lre])
    P.op("dve", lambda e: e.tensor_tensor(out=lim[:], in0=dec[:], in1=sn[:], op=ALU.mult), reads=[dec, sn], writes=[lim])
    P.op("dve", lambda e: e.tensor_scalar(out=nr[:], in0=lre[:], scalar1=-1.0, scalar2=None, op0=ALU.add), reads=[lre], writes=[nr])
    P.op("dve", lambda e: e.tensor_tensor(out=tA[:], in0=a_re(), in1=a_re(), op=ALU.mult), reads=[prm], writes=[tA])
    P.op("dve", lambda e: e.tensor_tensor(out=tB[:], in0=a_im(), in1=a_im(), op=ALU.mult), reads=[prm], writes=[tB])
    P.op("dve", lambda e: e.tensor_tensor(out=den[:], in0=tA[:], in1=tB[:], op=ALU.add), reads=[tA, tB], writes=[den])
    P.op("dve", lambda e: e.reciprocal(out=den[:], in_=den[:]), reads=[], writes=[den])
    P.op("dve", lambda e: e.tensor_tensor(out=tA[:], in0=nr[:], in1=a_re(), op=ALU.mult), reads=[nr, prm], writes=[tA])
    P.op("dve", lambda e: e.tensor_tensor(out=tB[:], in0=lim[:], in1=a_im(), op=ALU.mult), reads=[lim, prm], writes=[tB])
    P.op("dve", lambda e: e.tensor_tensor(out=tA[:], in0=tA[:], in1=tB[:], op=ALU.add), reads=[tB], writes=[tA])
    P.op("dve", lambda e: e.tensor_tensor(out=f_re[:], in0=tA[:], in1=den[:], op=ALU.mult), reads=[tA, den], writes=[f_re])
    P.op("dve", lambda e: e.tensor_tensor(out=tA[:], in0=lim[:], in1=a_re(), op=ALU.mult), reads=[lim, prm], writes=[tA])
    P.op("dve", lambda e: e.tensor_tensor(out=tB[:], in0=nr[:], in1=a_im(), op=ALU.mult), reads=[nr, prm], writes=[tB])
    P.op("dve", lambda e: e.tensor_tensor(out=tA[:], in0=tA[:], in1=tB[:], op=ALU.subtract), reads=[tB], writes=[tA])
    P.op("dve", lambda e: e.tensor_tensor(out=f_im[:], in0=tA[:], in1=den[:], op=ALU.mult), reads=[tA, den], writes=[f_im])

    BT = [P.sb([32, NGP, 128], BF16, "BTre"), P.sb([32, NGP, 128], BF16, "BTim")]
    xt_ = [P.sb([128, 32], F32, "xt_") for _ in range(2)]
    xo = [P.sb([128, 32], F32, "xo") for _ in range(2)]
    tps = [P.ps([32, 128], F32, "tps") for _ in range(1)]
    for gp in range(NGP):
        for ri in range(2):
            t_ = rot(xt_, "xt_")
            o_ = rot(xo, "xo")
            if ri == 0:
                P.op("dve", lambda e, t_=t_, gp=gp: e.tensor_scalar(
                    out=t_[:], in0=bsm[:, 1, gp, :], scalar1=f_im[:, gp:gp + 1], scalar2=None, op0=ALU.mult),
                    reads=[bsm, f_im], writes=[t_])
                P.op("dve", lambda e, t_=t_, o_=o_, gp=gp: e.scalar_tensor_tensor(
                    out=o_[:], in0=bsm[:, 0, gp, :], scalar=f_re[:, gp:gp + 1], in1=t_[:],
                    op0=ALU.mult, op1=ALU.subtract), reads=[bsm, f_re, t_], writes=[o_])
            else:
                P.op("dve", lambda e, t_=t_, gp=gp: e.tensor_scalar(
                    out=t_[:], in0=bsm[:, 0, gp, :], scalar1=f_im[:, gp:gp + 1], scalar2=None, op0=ALU.mult),
                    reads=[bsm, f_im], writes=[t_])
                P.op("dve", lambda e, t_=t_, o_=o_, gp=gp: e.scalar_tensor_tensor(
                    out=o_[:], in0=bsm[:, 1, gp, :], scalar=f_re[:, gp:gp + 1], in1=t_[:],
                    op0=ALU.mult, op1=ALU.add), reads=[bsm, f_re, t_], writes=[o_])
            tp = rot(tps, "tps")
            P.op("pe", lambda e, tp=tp, o_=o_: e.transpose(out=tp[:], in_=o_[:], identity=cst[:, 0:128]),
                 reads=[o_, cst], writes=[tp])
            P.op("act", lambda e, tp=tp, ri=ri, gp=gp: e.copy(out=BT[ri][:, gp, :], in_=tp[:]),
                 reads=[tp], writes=[BT[ri]])
    ctb = [P.sb([128, NGP, 32], BF16, "ctre"), P.sb([128, NGP, 32], BF16, "ctimn")]
    P.op("dve", lambda e: e.tensor_copy(out=ctb[0][:], in_=ctf[:, 0]), reads=[ctf], writes=[ctb[0]])
    P.op("dve", lambda e: e.tensor_scalar(out=ctb[1][:], in0=ctf[:, 1], scalar1=-1.0, scalar2=None, op0=ALU.mult),
         reads=[ctf], writes=[ctb[1]])
    ddb = P.sb([32, NGP, 32], BF16, "ddb")
    P.op("dve", lambda e: e.tensor_copy(out=ddb[:], in_=ddf[:]), reads=[ddf], writes=[ddb])

    cosT = [P.sb([128, 512], F32, "cosT") for _ in range(NGP)]
    sinT = [P.sb([128, 512], F32, "sinT") for _ in range(NGP)]
    decT = [P.sb([128, 512], F32, "decT") for _ in range(2)]
    ang = [P.sb([128, 512], F32, "ang") for _ in range(2)]
    arg = [P.sb([128, 512], F32, "arg") for _ in range(4)]
    for gp in range(NGP):
        a_ = rot(ang, "ang")
        P.op("dve", lambda e, a_=a_, gp=gp: e.tensor_scalar(
            out=a_[:], in0=cst[:, 128:640], scalar1=thn[:, gp:gp + 1], scalar2=None, op0=ALU.mult),
            reads=[cst, thn], writes=[a_])
        emit_sincos(a_, sinT[gp], cosT[gp], lambda: rot(arg, "arg"))

    carry = [[P.sb([128, 1], F32, f"car{ri}") for ri in range(2)] for _ in range(NGP)]
    for gp in range(NGP):
        for ri in range(2):
            P.op("pool", lambda e, gp=gp, ri=ri: e.memset(carry[gp][ri][:], 0.0), writes=[carry[gp][ri]])
    hch = [P.sb([128, KC, 512], BF16, "hch") for _ in range(2)]
    pu_b = [P.ps([32, 512], F32, "pu") for _ in range(2)]
    pr_b = [P.ps([128, 512], F32, "pr") for _ in range(2)]
    pi_b = [P.ps([128, 512], F32, "pi") for _ in range(2)]
    py_b = [P.ps([32, 512], F32, "py") for _ in range(1)]
    ugs = [P.sb([32, 512], BF16, "ug") for _ in range(2)]
    tt = [P.sb([128, 512], F32, "tt") for _ in range(8)]
    mm = [P.sb([128, 512], F32, "mm") for _ in range(4)]
    ww = [P.sb([128, 512], F32, "ww") for _ in range(4)]
    sf = [P.sb([128, 512], F32, "sf") for _ in range(4)]
    sbf = [P.sb([128, 512], BF16, "sbf") for _ in range(4)]
    ybs = [P.sb([32, 512], BF16, "yb") for _ in range(2)]
    pend_ = None
    for tc in range(L // 512):
        ht = rot(hch, "hch")
        for kq in range(4):
            P.dma("sp", ht, ht[:, kq * 4:(kq + 1) * 4, :], hT,
                  hT.h[kq * 4:(kq + 1) * 4, :, tc * 512:(tc + 1) * 512].rearrange("k p t -> p k t"))
        def p1(tc, gp, ht):
                pu = rot(pu_b, "pu")
                P.group("pe", [(lambda e, kc=kc, pu=pu, gp=gp, ht=ht: e.matmul(
                    pu[:], lhsT=wus[:, gp, kc, :], rhs=ht[:, kc, :], start=(kc == 0), stop=(kc == KC - 1)))
                    for kc in range(KC)], reads=[wus, ht], writes=[pu])
                ug = rot(ugs, "ug")
                P.op("act", lambda e, ug=ug, pu=pu: e.copy(out=ug[:], in_=pu[:]), reads=[pu], writes=[ug])
                pr = rot(pr_b, "pr")
                pi_ = rot(pi_b, "pi")
                P.op("pe", lambda e, pr=pr, ug=ug, gp=gp: e.matmul(pr[:], lhsT=BT[0][:, gp, :], rhs=ug[:], start=True, stop=True),
                     reads=[BT[0], ug], writes=[pr])
                P.op("pe", lambda e, pi_=pi_, ug=ug, gp=gp: e.matmul(pi_[:], lhsT=BT[1][:, gp, :], rhs=ug[:], start=True, stop=True),
                     reads=[BT[1], ug], writes=[pi_])
                cT, sT, dT = cosT[gp], sinT[gp], rot(decT, "decT")
                P.op("act", lambda e, gp=gp, dT=dT: e.activation(
                    out=dT[:], in_=cst[:, 640:1152], func=AF.Copy, scale=dec[:, gp:gp + 1]),
                    reads=[cst, dec], writes=[dT])
                t1, t2, t3, t4 = [rot(tt, "tt") for _ in range(4)]
                P.op("dve", lambda e, t1=t1, pr=pr, cT=cT: e.tensor_tensor(out=t1[:], in0=pr[:], in1=cT[:], op=ALU.mult),
                     reads=[pr, cT], writes=[t1])
                P.op("dve", lambda e, t2=t2, pi_=pi_, sT=sT: e.tensor_tensor(out=t2[:], in0=pi_[:], in1=sT[:], op=ALU.mult),
                     reads=[pi_, sT], writes=[t2])
                P.op("dve", lambda e, t3=t3, pi_=pi_, cT=cT: e.tensor_tensor(out=t3[:], in0=pi_[:], in1=cT[:], op=ALU.mult),
                     reads=[pi_, cT], writes=[t3])
                P.op("dve", lambda e, t4=t4, pr=pr, sT=sT: e.tensor_tensor(out=t4[:], in0=pr[:], in1=sT[:], op=ALU.mult),
                     reads=[pr, sT], writes=[t4])
                m_re, m_im = rot(mm, "mm"), rot(mm, "mm")
                P.op("pool", lambda e, m_re=m_re, t1=t1, t2=t2: e.tensor_tensor(out=m_re[:], in0=t1[:], in1=t2[:], op=ALU.add),
                     reads=[t1, t2], writes=[m_re])
                P.op("pool", lambda e, m_im=m_im, t3=t3, t4=t4: e.tensor_tensor(out=m_im[:], in0=t3[:], in1=t4[:], op=ALU.subtract),
                     reads=[t3, t4], writes=[m_im])
                w_re, w_im = rot(ww, "ww"), rot(ww, "ww")
                for (w_, m_, ri) in ((w_re, m_re, 0), (w_im, m_im, 1)):
                    P.op("dve", lambda e, w_=w_, m_=m_, ri=ri, gp=gp, dT=dT: e.tensor_tensor_scan(
                        out=w_[:], data0=dT[:], data1=m_[:], initial=carry[gp][ri][:, 0:1], op0=ALU.mult, op1=ALU.add),
                        reads=[dT, m_, carry[gp][ri]], writes=[w_])
                a1, a2, a3, a4 = [rot(tt, "tt") for _ in range(4)]
                P.op("pool", lambda e, a1=a1, w_re=w_re, cT=cT: e.tensor_tensor(out=a1[:], in0=w_re[:], in1=cT[:], op=ALU.mult),
                     reads=[w_re, cT], writes=[a1])
                P.op("pool", lambda e, a2=a2, w_im=w_im, sT=sT: e.tensor_tensor(out=a2[:], in0=w_im[:], in1=sT[:], op=ALU.mult),
                     reads=[w_im, sT], writes=[a2])
                P.op("pool", lambda e, a3=a3, w_re=w_re, sT=sT: e.tensor_tensor(out=a3[:], in0=w_re[:], in1=sT[:], op=ALU.mult),
                     reads=[w_re, sT], writes=[a3])
                P.op("pool", lambda e, a4=a4, w_im=w_im, cT=cT: e.tensor_tensor(out=a4[:], in0=w_im[:], in1=cT[:], op=ALU.mult),
                     reads=[w_im, cT], writes=[a4])
                s_re, s_im = rot(sf, "sf"), rot(sf, "sf")
                P.op("dve", lambda e, s_re=s_re, a1=a1, a2=a2: e.tensor_tensor(out=s_re[:], in0=a1[:], in1=a2[:], op=ALU.subtract),
                     reads=[a1, a2], writes=[s_re])
                P.op("dve", lambda e, s_im=s_im, a3=a3, a4=a4: e.tensor_tensor(out=s_im[:], in0=a3[:], in1=a4[:], op=ALU.add),
                     reads=[a3, a4], writes=[s_im])
                sb_re, sb_im = rot(sbf, "sbf"), rot(sbf, "sbf")
                for (sb_, s_, ri) in ((sb_re, s_re, 0), (sb_im, s_im, 1)):
                    P.op("act", lambda e, sb_=sb_, s_=s_: e.copy(out=sb_[:], in_=s_[:]), reads=[s_], writes=[sb_])
                    P.op("act", lambda e, s_=s_, gp=gp, ri=ri: e.copy(out=carry[gp][ri][:], in_=s_[:, 511:512]),
                         reads=[s_], writes=[carry[gp][ri]])
                return dict(tc=tc, gp=gp, ug=ug, sb_re=sb_re, sb_im=sb_im)

        def p2(st_):
                tc, gp, ug, sb_re, sb_im = (st_[k_] for k_ in ('tc', 'gp', 'ug', 'sb_re', 'sb_im'))
                py = rot(py_b, "py")
                P.group("pe", [
                    (lambda e, py=py, sb_re=sb_re, gp=gp: e.matmul(py[:], lhsT=ctb[0][:, gp, :], rhs=sb_re[:], start=True, stop=False)),
                    (lambda e, py=py, sb_im=sb_im, gp=gp: e.matmul(py[:], lhsT=ctb[1][:, gp, :], rhs=sb_im[:], start=False, stop=False)),
                    (lambda e, py=py, ug=ug, gp=gp: e.matmul(py[:], lhsT=ddb[:, gp, :], rhs=ug[:], start=False, stop=True)),
                ], reads=[ctb[0], ctb[1], ddb, sb_re, sb_im, ug], writes=[py])
                yb = rot(ybs, "yb")
                P.op("act", lambda e, yb=yb, py=py: e.activation(out=yb[:], in_=py[:], func=AF.Gelu), reads=[py], writes=[yb])
                P.dma("act", ysT[(gp, tc)], (ys_ap(gp, tc) if ys_ap else ysT[(gp, tc)].h[gp, :, tc * 512:(tc + 1) * 512]), yb, yb[:], is_output=is_output)

        for gp in range(NGP):
            st_ = p1(tc, gp, ht)
            if pend_ is not None:
                p2(pend_)
            pend_ = st_
    p2(pend_)


KC = 16
EPS = 1e-6


def emit_omix2(P, hT, yss, wzu_d, wzv_d, glw_d, glb_d, sgn_d, wsT_d, sgb_d, mle_d, ymx, ntok, sel=None, alt_off=0):
    cnt = {}

    def rot(lst, key):
        i = cnt.get(key, 0)
        cnt[key] = i + 1
        return lst[i % len(lst)]

    wzu = [P.sb([128, KC, 128], BF16, "wzu") for _ in range(8)]
    wzv = [P.sb([128, KC, 512], BF16, "wzv") for _ in range(2)]
    glw = [P.sb([128, 8, 128], BF16, "glw") for _ in range(8)]
    for m in range(8):
        P.dma("pool", wzu[m], wzu[m][:], wzu_d, wzu_d.h[m])
        P.dma("pool", glw[m], glw[m][:], glw_d, glw_d.h[m])
    for hf in range(2):
        for q in range(4):
            P.dma("pool", wzv[hf], wzv[hf][:, q * 4:(q + 1) * 4, :], wzv_d, wzv_d.h[hf, :, q * 4:(q + 1) * 4, :])
    glb = P.sb([128, 8], F32, "glb")
    P.dma("sp", glb, glb[:], glb_d, glb_d.h[:, :])
    sgn = P.sb([128, 1024], F32, "sgn")
    P.dma("sp", sgn, sgn[:], sgn_d, sgn_d.h[:, :])
    wsf = P.sb([128, 8, 128], F32, "wsf")
    P.dma("sp", wsf, wsf[:], wsT_d, wsT_d.h[:, :, :])
    sgb = P.sb([128, 8, 128], F32, "sgb")
    P.dma("sp", sgb, sgb[:], sgb_d, sgb_d.h[:, :, :])
    mle = P.sb([128, 128], F32, "mle")
    P.dma("sp", mle, mle[:], mle_d, mle_d.h[:, :])
    wsb = P.sb([128, 8, 128], BF16, "wsb")
    for hd in range(8):
        P.op("dve", lambda e, hd=hd: e.tensor_tensor(out=wsb[:, hd, :], in0=wsf[:, hd, :], in1=mle[:], op=ALU.mult),
             reads=[wsf, mle], writes=[wsb])
    epst = P.sb([128, 1], F32, "epst")
    P.op("pool", lambda e: e.memset(epst[:], EPS), writes=[epst])

    hch = [P.sb([128, KC, 512], BF16, "hch") for _ in range(2)]
    ych = [P.sb([128, 8, 512], BF16, "ych") for _ in range(2)]
    uT = P.sb([128, 8, 512], F32, "uT")
    vg = [P.sb([128, 1024], F32, "vg") for _ in range(2)]
    sqv = P.sb([128, 1024], F32, "sqv")
    ss = [P.sb([128, 1], F32, "ss") for _ in range(2)]
    rs = [P.sb([128, 1], F32, "rs") for _ in range(2)]
    vtm = [P.sb([128, 1024], BF16, "vtm") for _ in range(2)]
    sig = [P.sb([128, 512], F32, "sig") for _ in range(2)]
    tmp4 = [P.sb([128, 4, 128], F32, "tmp4") for _ in range(2)]
    yo = [P.sb([128, 512], BF16, "yo") for _ in range(3)]
    yo4 = [P.sb([128, 4, 512], BF16, "yo4") for _ in range(2)]
    bank = [P.ps([128, 512], F32, "bk") for _ in range(4)]
    bank4 = [P.ps([128, 4, 128], F32, "bk4") for _ in range(2)]

    for c in range(ntok // 512):
        tsl = slice(c * 512, (c + 1) * 512)
        ht = rot(hch, "hch")
        for kq in range(4):
            P.dma("sp", ht, ht[:, kq * 4:(kq + 1) * 4, :], hT,
                  hT.h[kq * 4:(kq + 1) * 4, :, tsl].rearrange("k p t -> p k t"))
        yt = rot(ych, "ych")
        for kq in range(2):
            P.dma("sp", yt, yt[:, kq * 4:(kq + 1) * 4, :], yss,
                  yss.h[kq * 4:(kq + 1) * 4, :, tsl].rearrange("k p t -> p k t"))
        if sel is not None:
            asl = slice(alt_off + c * 512, alt_off + (c + 1) * 512)
            hb_ = rot(hch, "hch")
            for kq in range(4):
                P.dma("sp", hb_, hb_[:, kq * 4:(kq + 1) * 4, :], hT,
                      hT.h[kq * 4:(kq + 1) * 4, :, asl].rearrange("k p t -> p k t"))
            yb_ = rot(ych, "ych")
            for kq in range(2):
                P.dma("sp", yb_, yb_[:, kq * 4:(kq + 1) * 4, :], yss,
                      yss.h[kq * 4:(kq + 1) * 4, :, asl].rearrange("k p t -> p k t"))
            for (a_, b_) in ((ht, hb_), (yt, yb_)):
                P.op("pool", lambda e, a_=a_: e.tensor_scalar(out=a_[:], in0=a_[:], scalar1=sel[:, 0:1], scalar2=None,
                                                              op0=ALU.mult), reads=[sel], writes=[a_])
                P.op("dve", lambda e, a_=a_, b_=b_: e.scalar_tensor_tensor(
                    out=a_[:], in0=b_[:], scalar=sel[:, 1:2], in1=a_[:], op0=ALU.mult, op1=ALU.add),
                    reads=[b_, sel], writes=[a_])
        for m in range(8):
            bk = rot(bank, "bk")
            P.group("pe", [(lambda e, kc=kc, bk=bk, m=m, yt=yt: e.matmul(
                bk[:], lhsT=glw[m][:, kc, :], rhs=yt[:, kc, :], start=(kc == 0), stop=(kc == 7)))
                for kc in range(8)], reads=[glw[m], yt], writes=[bk])
            sg = rot(sig, "sig")
            P.op("act", lambda e, sg=sg, bk=bk, m=m: e.activation(
                out=sg[:], in_=bk[:], func=AF.Sigmoid, bias=glb[:, m:m + 1], scale=1.0),
                reads=[bk, glb], writes=[sg])
            y_ = rot(yo, "yo")
            P.op("dve", lambda e, y_=y_, sg=sg, yt=yt, m=m: e.tensor_tensor(
                out=y_[:], in0=sg[:], in1=yt[:, m, :], op=ALU.mult), reads=[sg, yt], writes=[y_])
            P.dma("act", ymx[(m, c)], ymx[(m, c)].h[m, :, tsl], y_, y_[:])
        for m in range(8):
            bk = rot(bank, "bk")
            P.group("pe", [(lambda e, kc=kc, bk=bk, m=m, ht=ht: e.matmul(
                bk[:], lhsT=wzu[m][:, kc, :], rhs=ht[:, kc, :], start=(kc == 0), stop=(kc == KC - 1)))
                for kc in range(KC)], reads=[wzu[m], ht], writes=[bk])
            P.op("act", lambda e, bk=bk, m=m: e.activation(out=uT[:, m, :], in_=bk[:], func=AF.Gelu),
                 reads=[bk], writes=[uT])
        y4 = [rot(yo4, "yo4") for _ in range(2)]
        for tb in range(4):
            vg_ = rot(vg, "vg")
            for hf in range(2):
                bk = rot(bank, "bk")
                P.group("pe", [(lambda e, kc=kc, bk=bk, hf=hf, ht=ht, tb=tb: e.matmul(
                    bk[:], lhsT=ht[:, kc, tb * 128:(tb + 1) * 128], rhs=wzv[hf][:, kc, :],
                    start=(kc == 0), stop=(kc == KC - 1))) for kc in range(KC)],
                    reads=[wzv[hf], ht], writes=[bk])
                P.op("act", lambda e, bk=bk, vg_=vg_, hf=hf: e.activation(
                    out=vg_[:, hf * 512:(hf + 1) * 512], in_=bk[:], func=AF.Gelu), reads=[bk], writes=[vg_])
            ss_ = rot(ss, "ss")
            rs_ = rot(rs, "rs")
            P.op("dve", lambda e, vg_=vg_: e.tensor_tensor(out=sqv[:], in0=vg_[:], in1=vg_[:], op=ALU.mult),
                 reads=[vg_], writes=[sqv])
            P.op("dve", lambda e, ss_=ss_: e.reduce_sum(out=ss_[:], in_=sqv[:], axis=AX.X), reads=[sqv], writes=[ss_])
            P.op("act", lambda e, ss_=ss_, rs_=rs_: e.activation(
                out=rs_[:], in_=ss_[:], func=AF.Sqrt, bias=epst[:, 0:1], scale=1.0 / 1024),
                reads=[ss_, epst], writes=[rs_])
            P.op("dve", lambda e, rs_=rs_: e.reciprocal(out=rs_[:], in_=rs_[:]), reads=[], writes=[rs_])
            vt = rot(vtm, "vtm")
            P.op("dve", lambda e, vt=vt, vg_=vg_, rs_=rs_: e.scalar_tensor_tensor(
                out=vt[:], in0=vg_[:], scalar=rs_[:, 0:1], in1=sgn[:], op0=ALU.mult, op1=ALU.mult),
                reads=[vg_, rs_, sgn], writes=[vt])
            for j in range(2):
                b4 = rot(bank4, "bk4")
                P.group("pe", [(lambda e, b4=b4, vt=vt, j=j, q=q: e.matmul(
                    b4[:, q, :], lhsT=vt[:, (4 * j + q) * 128:(4 * j + q + 1) * 128], rhs=wsb[:, 4 * j + q, :],
                    start=True, stop=True)) for q in range(4)], reads=[vt, wsb], writes=[b4])
                t4 = rot(tmp4, "tmp4")
                P.op("dve", lambda e, t4=t4, b4=b4, j=j: e.tensor_tensor(
                    out=t4[:], in0=b4[:], in1=sgb[:, 4 * j:4 * j + 4, :], op=ALU.add), reads=[b4, sgb], writes=[t4])
                P.op("dve", lambda e, t4=t4, j=j, tb=tb, y4=y4: e.tensor_tensor(
                    out=y4[j][:, :, tb * 128:(tb + 1) * 128], in0=t4[:],
                    in1=uT[:, 4 * j:4 * j + 4, tb * 128:(tb + 1) * 128], op=ALU.mult),
                    reads=[t4, uT], writes=[y4[j]])
        for j in range(2):
            for q in range(4):
                m = 8 + 4 * j + q
                P.dma("act", ymx[(m, c)], ymx[(m, c)].h[m, :, tsl], y4[j], y4[j][:, q, :])

NCORE = 8
NT = 2048
BF = ml_dtypes.bfloat16


def _bass():
    return bass.Bass("TRN2", target_bir_lowering=False)


def _ffn_w_inputs(P, sfx):
    wg = P.dram("wg" + sfx, [FT, 128, KC, 128], F32, "ExternalInput")
    wu = P.dram("wu" + sfx, [FT, 128, KC, 128], F32, "ExternalInput")
    wd = P.dram("wd" + sfx, [KC, 128, FT, 128], F32, "ExternalInput")
    return wg, wu, wd


def _load_g(P, name):
    gd = P.dram(name, [128, 96], F32, "ExternalInput")
    g = P.sb([128, 96], F32, name)
    P.dma("sp", g, g[:], gd, gd.h[:, :])
    return g


LSEQ = 4096
NACT = 8
LT = 2048


def _whole(P, name, shape, dt, kind):
    base = P.dram(name, shape, dt, kind)

    class _D(dict):
        def __missing__(self, k):
            return base
    d = _D()
    d["is_output"] = (kind == "ExternalOutput")
    d["base"] = base
    return d


def _ym_loader(P, res, ymd):
    def ym(c):
        h = res["h"]
        for kq in range(4):
            P.dma("sp", h, h[:, kq * 4:(kq + 1) * 4, 0:512], ymd,
                  ymd.h[kq * 4:(kq + 1) * 4, :, c * 512:(c + 1) * 512].rearrange("k p t -> p k t"))
        return _View(h)
    return ym


class _View:
    def __init__(self, t):
        self.t = t
        self.w = t.w
        self.r = t.r
        self.name = t.name

    def __getitem__(self, idx):
        a, b, c = idx
        assert c == slice(None)
        return self.t.h[a, b, 0:512]


STOP_AFTER = 99
SKIP = set()


def _on(k):
    return STOP_AFTER >= k and k not in SKIP


def build_FUSED():
    nc = _bass()
    NH = LSEQ // 512
    with ExitStack() as es:
        P = Prog(nc, es)
        base = P.es
        x_in = regions(P, "xT", [KC, 128, LSEQ], F32, "ExternalInput", KC, NH)
        xo = regions(P, "xoT", [KC, 128, LT], F32, "ExternalOutput", KC, LT // 512)
        xs = [regions(P, f"x{i}s", [KC, 128, LSEQ], F32, "Internal", KC, NH) for i in range(1, 5)]
        x1, x2, x3, x4 = xs
        x5 = regions(P, "x5s", [KC, 128, LT], F32, "Internal", KC, LT // 512)
        seld = P.dram("selh", [128, 2], F32, "ExternalInput")
        h1 = regions(P, "h1s", [KC, 128, LSEQ], BF16, "Internal", KC, NH)
        h2 = regions(P, "h2s", [KC, 128, LSEQ], BF16, "Internal", KC, NH)
        ymT = regions(P, "ymTs", [KC, 128, LSEQ], BF16, "Internal", KC, 32)
        ymx = regions(P, "ymxs", [KC, 128, LT], BF16, "Internal", KC, LT // 512)
        ysd = regions(P, "yss", [8, 128, LSEQ], BF16, "Internal", 32, NH)
        wf = [_ffn_w_inputs(P, str(i)) for i in range(4)]
        gd = [P.dram(f"g{i}", [128, 96], F32, "ExternalInput") for i in range(2)]
        em = []
        for hh in range(2):
            s = f"_{hh}"
            em.append(dict(
                wq=P.dram("wq" + s, [6, 128, KC, 128], F32, "ExternalInput"),
                wk=P.dram("wk" + s, [6, 128, KC, 128], F32, "ExternalInput"),
                wv=P.dram("wv" + s, [3, 128, KC, 256], F32, "ExternalInput"),
                wp=P.dram("wp" + s, [2, 128, KC, 128], F32, "ExternalInput"),
                pw=P.dram("pw" + s, [2, 128, 128], F32, "ExternalInput"),
                psc=P.dram("psc" + s, [128, 2], F32, "ExternalInput"),
                selw=P.dram("selw" + s, [128, 2, 21], F32, "ExternalInput")))
        cst = P.dram("cst", [128, 768], F32, "ExternalInput")
        wo0 = P.dram("wo0", [KC, 128, KC, 128], F32, "ExternalInput")
        wo1 = P.dram("wo1", [KC, 128, KC, 128], F32, "ExternalInput")
        om = []
        for hh in range(2):
            s = f"_{hh}"
            om.append(dict(
                wu=P.dram("swu" + s, [NGP, 128, KC, 32], F32, "ExternalInput"),
                prm=P.dram("prm" + s, [128, 3, 16], F32, "ExternalInput"),
                bsm=P.dram("bsm" + s, [128, 2, 16, 32], F32, "ExternalInput"),
                ct=P.dram("ct" + s, [128, 2, 16, 32], F32, "ExternalInput"),
                dd=P.dram("dd" + s, [32, 16, 32], F32, "ExternalInput")))
        cst2 = P.dram("cst2", [128, 1152], F32, "ExternalInput")
        o2 = dict(
            wzu=P.dram("wzu", [8, 128, KC, 128], F32, "ExternalInput"),
            wzv=P.dram("wzv", [2, 128, KC, 512], F32, "ExternalInput"),
            glw=P.dram("glw", [8, 128, 8, 128], F32, "ExternalInput"),
            glb=P.dram("glb", [128, 8], F32, "ExternalInput"),
            sgn=P.dram("sgn", [128, 1024], F32, "ExternalInput"),
            wsT=P.dram("wsT", [128, 8, 128], F32, "ExternalInput"),
            sgb=P.dram("sgb", [128, 8, 128], F32, "ExternalInput"),
            mle=P.dram("mle", [128, 128], F32, "ExternalInput"))
        fscr = [regions(P, f"fscr{i}", [KC, 128, LSEQ], F32, "Internal", KC, NH) for i in range(2)]
        fscr.append(regions(P, "fscr2", [KC, 128, LT], F32, "Internal", KC, LT // 512))

        def load_g(i):
            g = P.sb([128, 96], F32, f"g{i}")
            P.dma("sp", g, g[:], gd[i], gd[i].h[:, :])
            return g

        def ffn_res(fs):
            res = ffn_resources(P, LSEQ, scr_name=None, f_scr=fs)
            return res

        with ExitStack() as sc:
            P.es = sc
            C = Common(P)
            g0 = load_g(0)
            res = ffn_res(fscr[0])
            emit_ffn_pipelined(P, C, res, x_in, x1, wf[0][0], wf[0][1], wf[0][2], g0, 0, 1, LSEQ, h_out=h1, jnext=2)
            P.barrier()
        for hh in range(2 if _on(2) else 0):
            with ExitStack() as sc:
                P.es = sc
                e_ = em[hh]
                emit_emix(P, {(0, c): h1["base"] for c in range(16)}, e_["wq"], e_["wk"], e_["wv"], e_["wp"],
                          e_["pw"], e_["psc"], cst, ymT, e_["selw"],
                          tile_of=(lambda j, hh=hh: (2 * hh + j) if j < 2 else (4 + 6 * hh + j - 2)),
                          is_output=False)
                P.barrier()
        for _ in range(1 if _on(3) else 0):
          with ExitStack() as sc:
            P.es = sc
            C = Common(P)
            g0 = load_g(0)
            g1 = load_g(1)
            res = ffn_res(fscr[1])
            emit_outproj(P, C, res, _ym_loader(P, res, ymT["base"]), wo0, x1, x2, g0, 3, LSEQ)
            emit_ffn_pipelined(P, C, res, x2, x3, wf[1][0], wf[1][1], wf[1][2], g0, 4, 5, LSEQ)
            emit_ffn_pipelined(P, C, res, x3, x4, wf[2][0], wf[2][1], wf[2][2], g1, 0, 1, LSEQ, h_out=h2, jnext=2)
            P.barrier()
        for hh in range(2 if _on(4) else 0):
            with ExitStack() as sc:
                P.es = sc
                o_ = om[hh]

                def ys_ap(gp, tc, hh=hh):
                    gg = 16 * hh + gp
                    return ysd["base"].h[gg // 4, (gg % 4) * 32:(gg % 4) * 32 + 32, tc * 512:(tc + 1) * 512]
                ysT = {(gp, tc): ysd[(16 * hh + gp, tc)] for gp in range(NGP) for tc in range(NH)}
                emit_ossm(P, h2["base"], o_["wu"], o_["prm"], o_["bsm"], o_["ct"], o_["dd"], cst2, ysT,
                          ys_ap=ys_ap, is_output=False)
                P.barrier()
        for _ in range(1 if _on(5) else 0):
          with ExitStack() as sc:
            P.es = sc
            selt = P.sb([128, 2], F32, "selt")
            P.dma("sp", selt, selt[:], seld, seld.h[:, :])
            emit_omix2(P, h2["base"], ysd["base"], o2["wzu"], o2["wzv"], o2["glw"], o2["glb"], o2["sgn"],
                       o2["wsT"], o2["sgb"], o2["mle"], ymx, LT, sel=selt, alt_off=LT)
            P.barrier()
        for _ in range(1 if _on(5) else 0):
          with ExitStack() as sc:
            P.es = sc
            C = Common(P)
            g1 = load_g(1)
            res = ffn_resources(P, LT, scr_name=None, f_scr=fscr[2])
            selt = P.sb([128, 2], F32, "selt")
            P.dma("sp", selt, selt[:], seld, seld.h[:, :])

            def ym(c):
                h = res["h"]
                for kc in range(KC):
                    P.dma("sp", h, h[:, kc, 0:512], ymx[(kc, c)], ymx[(kc, c)].h[kc, :, c * 512:(c + 1) * 512])
                return _View(h)
            emit_outproj(P, C, res, ym, wo1, x4, x5, g1, 3, LT, sel=selt, alt_chunks=LT // 512)
            emit_ffn_pipelined(P, C, res, x5, xo, wf[3][0], wf[3][1], wf[3][2], g1, 4, 5, LT)
            P.barrier()
        P.es = base
        P.finish()
        P.emit()
    return nc


def _colT(cols):
    K = cols.shape[0] // 128
    n = cols.shape[1] // 128
    return np.ascontiguousarray(cols.reshape(K, 128, n, 128).transpose(2, 1, 0, 3))


def _ffn_layout(w_gate, w_up, w_down, sfx):
    return {"wg" + sfx: _colT(w_gate), "wu" + sfx: _colT(w_up),
            "wd" + sfx: np.ascontiguousarray(w_down.reshape(FT, 128, KC, 128).transpose(2, 1, 0, 3))}


def _g_layout(gn):
    return np.ascontiguousarray(gn.reshape(6, KC, 128).transpose(2, 0, 1).reshape(128, 96))


def _fm(a):
    return np.ascontiguousarray(a.T).reshape(a.shape[1] // 128, 128, a.shape[0])


def _emix_inputs(w_in, pool_w, pool_scale, hh):
    heads = range(6 * hh, 6 * hh + 6)
    qc = np.concatenate([w_in[:, 512 + h * 128: 512 + (h + 1) * 128] for h in heads], 1)
    kc = np.concatenate([w_in[:, 512 + 1536 + h * 128: 512 + 1536 + (h + 1) * 128] for h in heads], 1)
    vc = np.concatenate([w_in[:, 512 + 3072 + h * 128: 512 + 3072 + (h + 1) * 128] for h in heads], 1)
    pc = w_in[:, hh * 256:(hh + 1) * 256]
    wv = np.ascontiguousarray(vc.reshape(KC, 128, 3, 256).transpose(2, 1, 0, 3))
    pw = np.ascontiguousarray(pool_w[2 * hh:2 * hh + 2])
    psc = np.ascontiguousarray(pool_scale[hh * 256:(hh + 1) * 256].reshape(2, 128).T)
    selw = np.zeros((128, 2, 21), np.float32)
    for gi in range(2):
        k = 2 * hh + gi
        w = POOL_WINDOWS[k]
        selw[:, gi, k] = 1.0
        selw[:, gi, 4] = 1.0 / w
        selw[:, gi, 5:21] = (1.0 / np.minimum(np.arange(1, 17), w))[None, :]
    cst = np.zeros((128, 768), np.float32)
    cst[:, 0:128] = np.eye(128)
    cst[:, 128:256] = np.tril(np.ones((128, 128)), -1)
    cst[:, 256:768] = 1.0
    return {"wq": _colT(qc), "wk": _colT(kc), "wv": wv, "wp": _colT(pc), "pw": pw, "psc": psc,
            "selw": selw, "cst": cst}


def _ossm_inputs(w_in, a_re, a_im, log_dt, b_re, b_im, c_re, c_im, d, hh):
    G0 = 32 * hh
    wu = w_in[:, G0 * 16:(G0 + 32) * 16]
    wu = np.ascontiguousarray(wu.reshape(KC, 128, NGP, 32).transpose(2, 1, 0, 3))
    prm = np.zeros((128, 3, 16), np.float32)
    bsm = np.zeros((128, 2, 16, 32), np.float32)
    ct = np.zeros((128, 2, 16, 32), np.float32)
    dd = np.zeros((32, 16, 32), np.float32)
    for gp in range(NGP):
        for gl in range(2):
            g = G0 + 2 * gp + gl
            sl = slice(gl * 64, (gl + 1) * 64)
            prm[sl, 0, gp] = a_re[g]
            prm[sl, 1, gp] = a_im[g]
            prm[sl, 2, gp] = log_dt[g]
            bsm[sl, 0, gp, gl * 16:(gl + 1) * 16] = b_re[g]
            bsm[sl, 1, gp, gl * 16:(gl + 1) * 16] = b_im[g]
            ct[sl, 0, gp, gl * 16:(gl + 1) * 16] = c_re[g].T
            ct[sl, 1, gp, gl * 16:(gl + 1) * 16] = c_im[g].T
            idx = np.arange(16)
            dd[gl * 16 + idx, gp, gl * 16 + idx] = d[g * 16:(g + 1) * 16]
    c2 = np.zeros((128, 1152), np.float32)
    c2[:, 0:128] = np.eye(128)
    c2[:, 128:640] = np.arange(1, 513, dtype=np.float32)[None, :]
    c2[:, 640:1152] = 1.0
    return {"wu": wu, "prm": prm, "bsm": bsm, "ct": ct, "dd": dd, "cst2": c2}


def _omix2_inputs(w_in, glu_w, glu_b, sgu_norm_g, sgu_w, sgu_b):
    zu = w_in[:, 1024:2048]
    zv = w_in[:, 2048:3072]
    wzv = np.ascontiguousarray(zv.reshape(KC, 128, 2, 512).transpose(2, 1, 0, 3))
    glw = _colT(glu_w)
    glb = np.ascontiguousarray(glu_b.reshape(8, 128).T)
    sgn = np.ascontiguousarray(np.broadcast_to(sgu_norm_g[None, :], (128, 1024)))
    wsT = np.ascontiguousarray(sgu_w.transpose(2, 0, 1))
    sgb = np.ascontiguousarray(np.broadcast_to(sgu_b[None, :, :], (128, 8, 128)))
    mle = np.triu(np.ones((128, 128), np.float32))
    return {"wzu": _colT(zu), "wzv": wzv, "glw": glw, "glb": glb, "sgn": sgn, "wsT": wsT, "sgb": sgb, "mle": mle}


_PROGS = {}


def _make_inputs(x, norm_g, ffn_w_gate, ffn_w_up, ffn_w_down, ev_w_in, ev_pool_w, ev_pool_scale, ev_w_out,
           od_w_in, od_ssm_a_re, od_ssm_a_im, od_ssm_log_dt, od_ssm_b_re, od_ssm_b_im, od_ssm_c_re,
           od_ssm_c_im, od_ssm_d, od_glu_w, od_glu_b, od_sgu_norm_g, od_sgu_w, od_sgu_b, od_w_out):
    f = lambda a: np.asarray(a, dtype=np.float32)
    x = f(x)
    norm_g, ffn_w_gate, ffn_w_up, ffn_w_down = f(norm_g), f(ffn_w_gate), f(ffn_w_up), f(ffn_w_down)
    B, Lq, Dm = x.shape
    xts = [_fm(x[b]) for b in range(B)]
    shared = {}
    for i, (l, j) in enumerate(((0, 0), (0, 1), (1, 0), (1, 1))):
        shared.update(_ffn_layout(ffn_w_gate[l, j], ffn_w_up[l, j], ffn_w_down[l, j], str(i)))
    shared["g0"] = _g_layout(norm_g[0])
    shared["g1"] = _g_layout(norm_g[1])
    for hh in range(2):
        e_ = _emix_inputs(f(ev_w_in[0]), f(ev_pool_w[0]), f(ev_pool_scale[0]), hh)
        shared["cst"] = e_.pop("cst")
        for k, v in e_.items():
            shared[f"{k}_{hh}"] = v
        o_ = _ossm_inputs(f(od_w_in[0]), f(od_ssm_a_re[0]), f(od_ssm_a_im[0]), f(od_ssm_log_dt[0]),
                          f(od_ssm_b_re[0]), f(od_ssm_b_im[0]), f(od_ssm_c_re[0]), f(od_ssm_c_im[0]),
                          f(od_ssm_d[0]), hh)
        shared["cst2"] = o_.pop("cst2")
        shared[f"swu_{hh}"] = o_.pop("wu")
        for k, v in o_.items():
            shared[f"{k}_{hh}"] = v
    shared["wo0"] = _colT(f(ev_w_out[0]))
    shared["wo1"] = _colT(f(od_w_out[0]))
    shared.update(_omix2_inputs(f(od_w_in[0]), f(od_glu_w[0]), f(od_glu_b[0]), f(od_sgu_norm_g[0]),
                                f(od_sgu_w[0]), f(od_sgu_b[0])))
    ims = []
    for c in range(NACT):
        b, th = c // 2, c % 2
        selh = np.zeros((128, 2), np.float32)
        selh[:, th] = 1.0
        ims.append(dict(shared, xT=xts[b], selh=selh))
    return ims


def kernel(**inputs):
    x = np.asarray(inputs["x"])
    B, Lq, Dm = x.shape
    ims = _make_inputs(**inputs)
    if "F" not in _PROGS:
        _PROGS["F"] = build_FUSED()
    r = run_bass_kernel_spmd(_PROGS["F"], ims, core_ids=list(range(NACT)))
    out = np.empty((B, Lq, Dm), np.float32)
    for c in range(NACT):
        b, th = c // 2, c % 2
        out[b, th * LT:(th + 1) * LT] = r.results[c]["xoT"].reshape(Dm, LT).T
    return out
```

```python
import math
import ml_dtypes
from concourse.bass_utils import run_bass_kernel_spmd
import numpy as np
from contextlib import ExitStack
import concourse.bass as bass
import concourse.mybir as mybir

F32 = mybir.dt.float32
BF16 = mybir.dt.bfloat16
ALU = mybir.AluOpType
AF = mybir.ActivationFunctionType
AX = mybir.AxisListType

ENGS = ["pe", "dve", "act", "pool", "sp"]
NRING = 16
RING_N = {"sp": 24, "act": 16, "pool": 16}


class T:
    def __init__(self, h, name):
        self.h = h
        self.name = name
        self.w = {}
        self.r = {}

    def __getitem__(self, idx):
        return self.h[idx]


class Prog:
    def __init__(self, nc, es):
        self.nc = nc
        self.es = es
        self.q = {e: [] for e in ENGS}
        self.esem = {e: es.enter_context(nc.semaphore(f"s_{e}")) for e in ENGS}
        self.ecnt = {e: 0 for e in ENGS}
        self.seen = {e: {} for e in ENGS}
        self.ring = {}
        self.ringcnt = {}
        self.ringpos = {}
        for qn in ["sp", "act", "pool"]:
            self.ring[qn] = [es.enter_context(nc.semaphore(f"d_{qn}{i}")) for i in range(RING_N[qn])]
            self.ringcnt[qn] = [0] * RING_N[qn]
            self.ringpos[qn] = 0
        self.nuniq = 0
        self.out_events = []

    def sb(self, shape, dt, name=None):
        self.nuniq += 1
        name = f"{name or 't'}_{self.nuniq}"
        h = self.es.enter_context(self.nc.sbuf_tensor(name, list(shape), dt))
        return T(h, name)

    def ps(self, shape, dt=F32, name=None):
        self.nuniq += 1
        name = f"{name or 'p'}_{self.nuniq}"
        h = self.es.enter_context(self.nc.psum_tensor(name, list(shape), dt))
        return T(h, name)

    def dram(self, name, shape, dt, kind):
        h = self.nc.dram_tensor(name, list(shape), dt, kind=kind)
        return T(h.ap() if hasattr(h, "ap") else h, name)

    def _collect(self, eng, reads, writes):
        waits = []
        for t in list(reads) + list(writes):
            for sem, (val, src) in t.w.items():
                waits.append((sem, val, src))
        for t in writes:
            for sem, (val, src) in t.r.items():
                waits.append((sem, val, src))
        need = {}
        seen = self.seen[eng]
        for (sem, val, src) in waits:
            if eng == "pe" and src == "pe":
                continue
            if seen.get(sem, 0) >= val:
                continue
            if need.get(sem, (0,))[0] < val:
                need[sem] = (val,)
        out = []
        for sem, (val,) in need.items():
            seen[sem] = val
            out.append((sem, val))
        return out

    def _commit(self, ev, reads, writes):
        sem, val, src = ev
        for t in reads:
            t.r[sem] = (val, src)
        for t in writes:
            t.w[sem] = (val, src)

    def op(self, eng, fn, reads=(), writes=()):
        waits = self._collect(eng, reads, writes)
        self.ecnt[eng] += 1
        ev = (self.esem[eng], self.ecnt[eng], eng)
        self.q[eng].append(("op", waits, [fn], (self.esem[eng], 1)))
        self._commit(ev, reads, writes)
        return ev

    def group(self, eng, fns, reads=(), writes=()):
        waits = self._collect(eng, reads, writes)
        self.ecnt[eng] += 1
        ev = (self.esem[eng], self.ecnt[eng], eng)
        self.q[eng].append(("op", waits, list(fns), (self.esem[eng], 1)))
        self._commit(ev, reads, writes)
        return ev

    def dma(self, qn, out_t, out_ap, in_t, in_ap, is_output=False, **kw):
        reads = [in_t]
        writes = [out_t]
        waits = self._collect(qn, reads, writes)
        pos = self.ringpos[qn]
        self.ringpos[qn] = (pos + 1) % RING_N[qn]
        sem = self.ring[qn][pos]
        prev = self.ringcnt[qn][pos]
        if prev > 0 and self.seen[qn].get(sem, 0) < prev:
            waits.append((sem, prev))
            self.seen[qn][sem] = prev
        self.ringcnt[qn][pos] = prev + 16
        ev = (sem, prev + 16, "dma")

        def fn(e, out_ap=out_ap, in_ap=in_ap, kw=kw):
            return e.dma_start(out=out_ap, in_=in_ap, **kw)

        self.q[qn].append(("op", waits, [fn], (sem, 16)))
        self._commit(ev, reads, writes)
        if is_output:
            self.out_events.append(ev)
        return ev

    def collective(self, kind, out_t, out_ap, in_t, in_ap, groups):
        qn = "pool"
        waits = self._collect(qn, [in_t], [out_t])
        pos = self.ringpos[qn]
        self.ringpos[qn] = (pos + 1) % RING_N[qn]
        sem = self.ring[qn][pos]
        prev = self.ringcnt[qn][pos]
        if prev > 0 and self.seen[qn].get(sem, 0) < prev:
            waits.append((sem, prev))
            self.seen[qn][sem] = prev
        self.ringcnt[qn][pos] = prev + 16
        ev = (sem, prev + 16, "dma")

        def fn(e):
            return e.collective_compute(kind, ALU.bypass, replica_groups=groups, ins=[in_ap], outs=[out_ap])

        self.q[qn].append(("op", waits, [fn], (sem, 16)))
        self._commit(ev, [in_t], [out_t])
        return ev

    def barrier(self):
        targets = [(self.esem[e], self.ecnt[e]) for e in ENGS if self.ecnt[e] > 0]
        for qn in self.ring:
            for i in range(RING_N[qn]):
                if self.ringcnt[qn][i] > 0:
                    targets.append((self.ring[qn][i], self.ringcnt[qn][i]))
        for e in ENGS:
            waits = [(s, v) for (s, v) in targets if self.seen[e].get(s, 0) < v]
            for s, v in waits:
                self.seen[e][s] = v
            self.q[e].append(("wait", waits, [], None))

    def finish(self):
        need = {}
        for (sem, val, _) in self.out_events:
            need[sem] = max(need.get(sem, 0), val)
        self.q["sp"].append(("wait", list(need.items()), [], None))

    def emit(self):
        nc = self.nc
        eobj = {"pe": "tensor", "dve": "vector", "act": "scalar", "pool": "gpsimd", "sp": "sync"}
        with nc.Block() as block:
            for en in ENGS:
                items = self.q[en]

                def body(e, items=items):
                    for (_, waits, fns, inc) in items:
                        for (sem, val) in waits:
                            e.wait_ge(sem, val)
                        last = None
                        for f in fns:
                            last = f(e)
                        if inc is not None:
                            last.then_inc(inc[0], inc[1])

                getattr(block, eobj[en])(body)


D = 2048
KC = 16
FF = 5632
FT = 44
EPS = 1e-6


class Common:
    def __init__(self, P):
        self.P = P
        self.ones = P.sb([128, 128], BF16, "ones")
        self.eps = P.sb([128, 1], F32, "eps")
        P.op("pool", lambda e: e.memset(self.ones[:], 1.0), writes=[self.ones])
        P.op("pool", lambda e: e.memset(self.eps[:], EPS), writes=[self.eps])
        self.banks = [P.ps([128, 512], F32, f"bank{i}") for i in range(8)]
        self.xt = [P.sb([128, 512], F32, "xt") for _ in range(4)]
        self.ft = [P.sb([128, 512], F32, "ftile") for _ in range(3)]
        self.sq = [P.sb([128, 512], BF16, "sq") for _ in range(4)]
        self.tmp = [P.sb([128, 512], F32, "tmp") for _ in range(2)]
        self.rstd = [P.sb([128, 512], F32, "rstd") for _ in range(2)]
        self.rstdA = [P.sb([128, 512], F32, "rstdA") for _ in range(2)]
        self.sqS = [P.sb([128, 512], BF16, "sqS") for _ in range(6)]
        self.hb = [P.sb([128, 512], BF16, "hb") for _ in range(3)]
        self.cnt = {}

    def rot(self, lst, key):
        i = self.cnt.get(key, 0)
        self.cnt[key] = i + 1
        return lst[i % len(lst)]


def emit_rstd(P, C, stats_bank, rstd_t, n=D):
    tmp = C.rot(C.tmp, "tmp")
    P.op("act", lambda e: e.activation(out=tmp[:], in_=stats_bank[:], func=AF.Sqrt,
                                       bias=C.eps[:, 0:1], scale=1.0 / n),
         reads=[stats_bank, C.eps], writes=[tmp])
    P.op("dve", lambda e: e.reciprocal(out=rstd_t[:], in_=tmp[:]), reads=[tmp], writes=[rstd_t])


def emit_ffn(P, C, res, x_in, x_out, wg, wu, wd, g, jpre, jpost, ntok, h_out=None, jnext=None):
    h = res["h"]
    act = res["act"]
    wgs, wus, wds = res["wgs"], res["wus"], res["wds"]
    f_scr = res["f_scr"]
    bk = C.banks
    nblk = ntok // 1024
    for blk in range(nblk):
        for half in range(2):
            hg = blk * 2 + half
            tsl = slice(hg * 512, (hg + 1) * 512)
            hsl = slice(half * 512, (half + 1) * 512)
            st = bk[6 + half]
            for kc in range(KC):
                xt = C.rot(C.xt, "xt")
                P.dma("sp", xt, xt[:], x_in[(kc, hg)], x_in[(kc, hg)].h[kc, :, tsl])
                sq = C.rot(C.sq, "sq")
                P.op("act", lambda e, sq=sq, xt=xt: e.activation(out=sq[:], in_=xt[:], func=AF.Square),
                     reads=[xt], writes=[sq])
                P.op("pe", lambda e, sq=sq, kc=kc, st=st: e.matmul(st[:], lhsT=C.ones[:], rhs=sq[:],
                                                                     start=(kc == 0), stop=(kc == KC - 1)),
                     reads=[sq, C.ones], writes=[st])
            rstd = C.rstd[half]
            emit_rstd(P, C, st, rstd)
            for kc in range(KC):
                xt = C.rot(C.xt, "xt")
                P.dma("sp", xt, xt[:], x_in[(kc, hg)], x_in[(kc, hg)].h[kc, :, tsl])
                P.op("dve", lambda e, xt=xt, kc=kc, rstd=rstd, hsl=hsl: e.scalar_tensor_tensor(
                    out=h[:, kc, hsl], in0=xt[:], scalar=g[:, jpre * 16 + kc: jpre * 16 + kc + 1],
                    in1=rstd[:], op0=ALU.mult, op1=ALU.mult),
                    reads=[xt, rstd, g], writes=[h])
        for ft in range(FT):
            wgt = C.rot(wgs, "wg")
            wut = C.rot(wus, "wu")
            P.dma("pool", wgt, wgt[:], wg, wg.h[ft])
            P.dma("pool", wut, wut[:], wu, wu.h[ft])
            for half in range(2):
                hsl = slice(half * 512, (half + 1) * 512)
                pg = bk[0 + half]
                pu = bk[2 + half]
                P.group("pe", [
                    (lambda e, kc=kc, pg=pg, wgt=wgt, hsl=hsl: e.matmul(
                        pg[:], lhsT=wgt[:, kc, :], rhs=h[:, kc, hsl], start=(kc == 0), stop=(kc == KC - 1)))
                    for kc in range(KC)], reads=[wgt, h], writes=[pg])
                P.group("pe", [
                    (lambda e, kc=kc, pu=pu, wut=wut, hsl=hsl: e.matmul(
                        pu[:], lhsT=wut[:, kc, :], rhs=h[:, kc, hsl], start=(kc == 0), stop=(kc == KC - 1)))
                    for kc in range(KC)], reads=[wut, h], writes=[pu])
                sl = C.rot(C.hb, "hb")
                P.op("act", lambda e, sl=sl, pg=pg: e.activation(out=sl[:], in_=pg[:], func=AF.Silu),
                     reads=[pg], writes=[sl])
                P.op("dve", lambda e, sl=sl, pu=pu, ft=ft, hsl=hsl: e.tensor_tensor(
                    out=act[:, ft, hsl], in0=pu[:], in1=sl[:], op=ALU.mult),
                    reads=[pu, sl], writes=[act])
        pend = None
        for m in range(KC):
            wdt = C.rot(wds, "wd")
            for q in range(4):
                P.dma("pool", wdt, wdt[:, q * 11:(q + 1) * 11, :], wd, wd.h[m, :, q * 11:(q + 1) * 11, :])
            for half in range(2):
                hg = blk * 2 + half
                tsl = slice(hg * 512, (hg + 1) * 512)
                hsl = slice(half * 512, (half + 1) * 512)
                pd = bk[4 + half]
                st = bk[6 + half]
                P.group("pe", [
                    (lambda e, fc=fc, pd=pd, wdt=wdt, hsl=hsl: e.matmul(
                        pd[:], lhsT=wdt[:, fc, :], rhs=act[:, fc, hsl], start=(fc == 0), stop=(fc == FT - 1)))
                    for fc in range(FT)], reads=[wdt, act], writes=[pd])
                ftile = C.rot(C.ft, "ft")
                P.op("act", lambda e, ftile=ftile, pd=pd: e.copy(out=ftile[:], in_=pd[:]),
                     reads=[pd], writes=[ftile])
                sq = C.rot(C.sq, "sq")
                P.op("act", lambda e, sq=sq, ftile=ftile: e.activation(out=sq[:], in_=ftile[:], func=AF.Square),
                     reads=[ftile], writes=[sq])
                if pend is not None:
                    pend()
                pend = (lambda sq=sq, m=m, st=st: P.op("pe", lambda e: e.matmul(
                    st[:], lhsT=C.ones[:], rhs=sq[:], start=(m == 0), stop=(m == KC - 1)),
                    reads=[sq, C.ones], writes=[st]))
                P.dma("act", f_scr[(m, hg)], f_scr[(m, hg)].h[m, :, tsl], ftile, ftile[:])
        if pend is not None:
            pend()
        for half in range(2):
            hg = blk * 2 + half
            tsl = slice(hg * 512, (hg + 1) * 512)
            st = bk[6 + half]
            rstd = C.rstd[half]
            emit_rstd(P, C, st, rstd)
            st2 = bk[4 + half]
            for kc in range(KC):
                ftile = C.rot(C.ft, "ft")
                xt = C.rot(C.xt, "xt")
                P.dma("sp", ftile, ftile[:], f_scr[(kc, hg)], f_scr[(kc, hg)].h[kc, :, tsl])
                P.dma("sp", xt, xt[:], x_in[(kc, hg)], x_in[(kc, hg)].h[kc, :, tsl])
                P.op("dve", lambda e, ftile=ftile, kc=kc, rstd=rstd: e.scalar_tensor_tensor(
                    out=ftile[:], in0=ftile[:], scalar=g[:, jpost * 16 + kc: jpost * 16 + kc + 1],
                    in1=rstd[:], op0=ALU.mult, op1=ALU.mult),
                    reads=[rstd, g], writes=[ftile])
                P.op("dve", lambda e, ftile=ftile, xt=xt: e.scalar_tensor_tensor(
                    out=xt[:], in0=ftile[:], scalar=0.5, in1=xt[:], op0=ALU.mult, op1=ALU.add),
                    reads=[ftile], writes=[xt])
                P.dma("act", x_out[(kc, hg)], x_out[(kc, hg)].h[kc, :, tsl], xt, xt[:],
                      is_output=x_out.get("is_output", False))
                if h_out is not None:
                    sq = C.rot(C.sq, "sq")
                    P.op("act", lambda e, sq=sq, xt=xt: e.activation(out=sq[:], in_=xt[:], func=AF.Square),
                         reads=[xt], writes=[sq])
                    P.op("pe", lambda e, sq=sq, kc=kc, st2=st2: e.matmul(
                        st2[:], lhsT=C.ones[:], rhs=sq[:], start=(kc == 0), stop=(kc == KC - 1)),
                        reads=[sq, C.ones], writes=[st2])
            if h_out is not None:
                emit_rstd(P, C, st2, rstd)
                for kc in range(KC):
                    xt = C.rot(C.xt, "xt")
                    P.dma("sp", xt, xt[:], x_out[(kc, hg)], x_out[(kc, hg)].h[kc, :, tsl])
                    hb = C.rot(C.hb, "hb")
                    P.op("dve", lambda e, xt=xt, kc=kc, rstd=rstd, hb=hb: e.scalar_tensor_tensor(
                        out=hb[:], in0=xt[:], scalar=g[:, jnext * 16 + kc: jnext * 16 + kc + 1],
                        in1=rstd[:], op0=ALU.mult, op1=ALU.mult),
                        reads=[xt, rstd, g], writes=[hb])
                    P.dma("act", h_out[(kc, hg)], h_out[(kc, hg)].h[kc, :, tsl], hb, hb[:],
                          is_output=h_out.get("is_output", False))


def regions(P, name, shape, dt, kind, nk, nh):
    base = P.dram(name, shape, dt, kind)
    d = {}
    for k in range(nk):
        for hh in range(nh):
            d[(k, hh)] = T(base.h, f"{name}_{k}_{hh}")
    d["is_output"] = (kind == "ExternalOutput")
    d["base"] = base
    return d


def ffn_resources(P, ntok, scr_name="f_scr", f_scr=None):
    res = {}
    res["h"] = P.sb([128, KC, 1024], BF16, "h")
    res["act"] = P.sb([128, FT, 1024], BF16, "act")
    res["wgs"] = [P.sb([128, KC, 128], BF16, "wg") for _ in range(3)]
    res["wus"] = [P.sb([128, KC, 128], BF16, "wu") for _ in range(3)]
    res["wds"] = [P.sb([128, FT, 128], BF16, "wd") for _ in range(2)]
    res["f_scr"] = f_scr if f_scr is not None else regions(P, scr_name, [KC, 128, ntok], F32, "Internal", KC, ntok // 512)
    return res


def emit_outproj(P, C, res, ym, wo, x_in, x_out, g, jpost, ntok, sel=None, alt_chunks=0):
    f_scr = res["f_scr"]
    wgs = res["wgs"]
    bk = C.banks
    for c in range(ntok // 512):
        tsl = slice(c * 512, (c + 1) * 512)
        yt = ym(c)
        st = bk[6 + (c % 2)]
        pend = None
        for m in range(KC):
            wt = C.rot(wgs, "wg")
            P.dma("pool", wt, wt[:], wo, wo.h[m])
            pd = bk[4 + (m % 2)]
            P.group("pe", [(lambda e, kc=kc, pd=pd, wt=wt, yt=yt: e.matmul(
                pd[:], lhsT=wt[:, kc, :], rhs=yt[:, kc, :], start=(kc == 0), stop=(kc == KC - 1)))
                for kc in range(KC)], reads=[wt, yt], writes=[pd])
            ftile = C.rot(C.ft, "ft")
            P.op("act", lambda e, ftile=ftile, pd=pd: e.copy(out=ftile[:], in_=pd[:]), reads=[pd], writes=[ftile])
            sq = C.rot(C.sq, "sq")
            P.op("act", lambda e, sq=sq, ftile=ftile: e.activation(out=sq[:], in_=ftile[:], func=AF.Square),
                 reads=[ftile], writes=[sq])
            if pend is not None:
                pend()
            pend = (lambda sq=sq, m=m, st=st: P.op("pe", lambda e: e.matmul(
                st[:], lhsT=C.ones[:], rhs=sq[:], start=(m == 0), stop=(m == KC - 1)),
                reads=[sq, C.ones], writes=[st]))
            P.dma("act", f_scr[(m, c)], f_scr[(m, c)].h[m, :, tsl], ftile, ftile[:])
        if pend is not None:
            pend()
        rstd = C.rstd[c % 2]
        emit_rstd(P, C, st, rstd)
        for kc in range(KC):
            ftile = C.rot(C.ft, "ft")
            xt = C.rot(C.xt, "xt")
            P.dma("sp", ftile, ftile[:], f_scr[(kc, c)], f_scr[(kc, c)].h[kc, :, tsl])
            P.dma("sp", xt, xt[:], x_in[(kc, c)], x_in[(kc, c)].h[kc, :, tsl])
            if sel is not None:
                c2 = c + alt_chunks
                xb = C.rot(C.xt, "xt")
                P.dma("sp", xb, xb[:], x_in[(kc, c2)], x_in[(kc, c2)].h[kc, :, c2 * 512:(c2 + 1) * 512])
                P.op("pool", lambda e, xt=xt: e.tensor_scalar(out=xt[:], in0=xt[:], scalar1=sel[:, 0:1], scalar2=None,
                                                              op0=ALU.mult), reads=[sel], writes=[xt])
                P.op("dve", lambda e, xt=xt, xb=xb: e.scalar_tensor_tensor(
                    out=xt[:], in0=xb[:], scalar=sel[:, 1:2], in1=xt[:], op0=ALU.mult, op1=ALU.add),
                    reads=[xb, sel], writes=[xt])
            P.op("dve", lambda e, ftile=ftile, kc=kc, rstd=rstd: e.scalar_tensor_tensor(
                out=ftile[:], in0=ftile[:], scalar=g[:, jpost * 16 + kc: jpost * 16 + kc + 1],
                in1=rstd[:], op0=ALU.mult, op1=ALU.mult), reads=[rstd, g], writes=[ftile])
            P.op("dve", lambda e, ftile=ftile, xt=xt: e.tensor_tensor(
                out=xt[:], in0=ftile[:], in1=xt[:], op=ALU.add), reads=[ftile], writes=[xt])
            P.dma("act", x_out[(kc, c)], x_out[(kc, c)].h[kc, :, tsl], xt, xt[:],
                  is_output=x_out.get("is_output", False))


FLUSH = "FLUSH"


def emit_ffn_pipelined(P, C, res, x_in, x_out, wg, wu, wd, g, jpre, jpost, ntok, h_out=None, jnext=None):
    h = res["h"]
    act = res["act"]
    wgs, wus, wds = res["wgs"], res["wus"], res["wds"]
    f_scr = res["f_scr"]
    bk = C.banks
    nblk = ntok // 1024
    rstdA = C.rstdA

    def stats_mm(st, sq, first, last):
        return lambda: P.op("pe", lambda e: e.matmul(st[:], lhsT=C.ones[:], rhs=sq[:], start=first, stop=last),
                            reads=[sq, C.ones], writes=[st])

    def stageA(blk):
        for half in range(2):
            hg = blk * 2 + half
            tsl = slice(hg * 512, (hg + 1) * 512)
            hsl = slice(half * 512, (half + 1) * 512)
            st = bk[0 + half]
            for kc in range(KC):
                xt = C.rot(C.xt, "xt")
                P.dma("sp", xt, xt[:], x_in[(kc, hg)], x_in[(kc, hg)].h[kc, :, tsl])
                sq = C.rot(C.sqS, "sqS")
                P.op("act", lambda e, sq=sq, xt=xt: e.activation(out=sq[:], in_=xt[:], func=AF.Square),
                     reads=[xt], writes=[sq])
                yield stats_mm(st, sq, kc == 0, kc == KC - 1)
            yield FLUSH
            rstd = rstdA[half]
            emit_rstd(P, C, st, rstd)
            for kc in range(KC):
                xt = C.rot(C.xt, "xt")
                P.dma("sp", xt, xt[:], x_in[(kc, hg)], x_in[(kc, hg)].h[kc, :, tsl])
                P.op("dve", lambda e, xt=xt, kc=kc, rstd=rstd, hsl=hsl: e.scalar_tensor_tensor(
                    out=h[:, kc, hsl], in0=xt[:], scalar=g[:, jpre * 16 + kc: jpre * 16 + kc + 1],
                    in1=rstd[:], op0=ALU.mult, op1=ALU.mult),
                    reads=[xt, rstd, g], writes=[h])
                yield None

    def stageB(blk):
        for ft in range(FT):
            wgt = C.rot(wgs, "wg")
            wut = C.rot(wus, "wu")
            P.dma("pool", wgt, wgt[:], wg, wg.h[ft])
            P.dma("pool", wut, wut[:], wu, wu.h[ft])
            for half in range(2):
                hsl = slice(half * 512, (half + 1) * 512)
                pg = bk[0 + half]
                pu = bk[2 + half]
                P.group("pe", [
                    (lambda e, kc=kc, pg=pg, wgt=wgt, hsl=hsl: e.matmul(
                        pg[:], lhsT=wgt[:, kc, :], rhs=h[:, kc, hsl], start=(kc == 0), stop=(kc == KC - 1)))
                    for kc in range(KC)], reads=[wgt, h], writes=[pg])
                P.group("pe", [
                    (lambda e, kc=kc, pu=pu, wut=wut, hsl=hsl: e.matmul(
                        pu[:], lhsT=wut[:, kc, :], rhs=h[:, kc, hsl], start=(kc == 0), stop=(kc == KC - 1)))
                    for kc in range(KC)], reads=[wut, h], writes=[pu])
                sl = C.rot(C.hb, "hb")
                P.op("act", lambda e, sl=sl, pg=pg: e.activation(out=sl[:], in_=pg[:], func=AF.Silu),
                     reads=[pg], writes=[sl])
                P.op("dve", lambda e, sl=sl, pu=pu, ft=ft, hsl=hsl: e.tensor_tensor(
                    out=act[:, ft, hsl], in0=pu[:], in1=sl[:], op=ALU.mult),
                    reads=[pu, sl], writes=[act])
            yield None

    def stageC(blk):
        pend = None
        for m in range(KC):
            wdt = C.rot(wds, "wd")
            for q in range(4):
                P.dma("pool", wdt, wdt[:, q * 11:(q + 1) * 11, :], wd, wd.h[m, :, q * 11:(q + 1) * 11, :])
            for half in range(2):
                hg = blk * 2 + half
                tsl = slice(hg * 512, (hg + 1) * 512)
                hsl = slice(half * 512, (half + 1) * 512)
                pd = bk[4 + half]
                st = bk[6 + half]
                P.group("pe", [
                    (lambda e, fc=fc, pd=pd, wdt=wdt, hsl=hsl: e.matmul(
                        pd[:], lhsT=wdt[:, fc, :], rhs=act[:, fc, hsl], start=(fc == 0), stop=(fc == FT - 1)))
                    for fc in range(FT)], reads=[wdt, act], writes=[pd])
                ftile = C.rot(C.ft, "ft")
                P.op("act", lambda e, ftile=ftile, pd=pd: e.copy(out=ftile[:], in_=pd[:]),
                     reads=[pd], writes=[ftile])
                sq = C.rot(C.sq, "sq")
                P.op("act", lambda e, sq=sq, ftile=ftile: e.activation(out=sq[:], in_=ftile[:], func=AF.Square),
                     reads=[ftile], writes=[sq])
                if pend is not None:
                    pend()
                pend = stats_mm(st, sq, m == 0, m == KC - 1)
                P.dma("act", f_scr[(m, hg)], f_scr[(m, hg)].h[m, :, tsl], ftile, ftile[:])
                yield None
        pend()
        yield None

    def stageDE(blk):
        for half in range(2):
            hg = blk * 2 + half
            tsl = slice(hg * 512, (hg + 1) * 512)
            st = bk[6 + half]
            rstd = C.rstd[half]
            emit_rstd(P, C, st, rstd)
            st2 = bk[4 + half]
            for kc in range(KC):
                ftile = C.rot(C.ft, "ft")
                xt = C.rot(C.xt, "xt")
                P.dma("sp", ftile, ftile[:], f_scr[(kc, hg)], f_scr[(kc, hg)].h[kc, :, tsl])
                P.dma("sp", xt, xt[:], x_in[(kc, hg)], x_in[(kc, hg)].h[kc, :, tsl])
                P.op("dve", lambda e, ftile=ftile, kc=kc, rstd=rstd: e.scalar_tensor_tensor(
                    out=ftile[:], in0=ftile[:], scalar=g[:, jpost * 16 + kc: jpost * 16 + kc + 1],
                    in1=rstd[:], op0=ALU.mult, op1=ALU.mult),
                    reads=[rstd, g], writes=[ftile])
                P.op("dve", lambda e, ftile=ftile, xt=xt: e.scalar_tensor_tensor(
                    out=xt[:], in0=ftile[:], scalar=0.5, in1=xt[:], op0=ALU.mult, op1=ALU.add),
                    reads=[ftile], writes=[xt])
                P.dma("act", x_out[(kc, hg)], x_out[(kc, hg)].h[kc, :, tsl], xt, xt[:],
                      is_output=x_out.get("is_output", False))
                if h_out is not None:
                    sq = C.rot(C.sqS, "sqS")
                    P.op("act", lambda e, sq=sq, xt=xt: e.activation(out=sq[:], in_=xt[:], func=AF.Square),
                         reads=[xt], writes=[sq])
                    yield stats_mm(st2, sq, kc == 0, kc == KC - 1)
                else:
                    yield None
            if h_out is not None:
                yield FLUSH
                emit_rstd(P, C, st2, rstd)
                for kc in range(KC):
                    xt = C.rot(C.xt, "xt")
                    P.dma("sp", xt, xt[:], x_out[(kc, hg)], x_out[(kc, hg)].h[kc, :, tsl])
                    hb = C.rot(C.hb, "hb")
                    P.op("dve", lambda e, xt=xt, kc=kc, rstd=rstd, hb=hb: e.scalar_tensor_tensor(
                        out=hb[:], in0=xt[:], scalar=g[:, jnext * 16 + kc: jnext * 16 + kc + 1],
                        in1=rstd[:], op0=ALU.mult, op1=ALU.mult),
                        reads=[xt, rstd, g], writes=[hb])
                    P.dma("act", h_out[(kc, hg)], h_out[(kc, hg)].h[kc, :, tsl], hb, hb[:],
                          is_output=h_out.get("is_output", False))
                    yield None

    def run(main, side, per_iter):
        deferred = []
        side_done = side is None

        def side_step():
            nonlocal side_done
            try:
                r = next(side)
            except StopIteration:
                side_done = True
                return
            if r is FLUSH:
                for t in deferred:
                    t()
                deferred.clear()
            elif r is not None:
                deferred.append(r)

        if main is None:
            while not side_done:
                side_step()
                for t in deferred:
                    t()
                deferred.clear()
            return
        for _ in main:
            for t in deferred:
                t()
            deferred.clear()
            if not side_done:
                for _ in range(per_iter):
                    if side_done:
                        break
                    side_step()
        while not side_done:
            side_step()
            for t in deferred:
                t()
            deferred.clear()
        for t in deferred:
            t()
        deferred.clear()

    run(None, stageA(0), 0)
    for blk in range(nblk):
        run(stageB(blk), stageDE(blk - 1) if blk > 0 else None, 2)
        run(stageC(blk), stageA(blk + 1) if blk + 1 < nblk else None, 2)
    run(None, stageDE(nblk - 1), 0)


L = 4096
KC = 16
SCALE = 128 ** -0.5
POOL_WINDOWS = (2, 4, 8, 16)


def emit_proj_fm(P, psb, hT, wt, outs, evac):
    pass


def emit_emix(P, hT, wq, wk, wv, wp, pw_d, ps_d, cst_d, ymT, selw_d, tile_of=lambda j: j, is_output=True):
    nc = P.nc
    base_es = P.es
    cst = P.sb([128, 768], F32, "cst")
    P.dma("sp", cst, cst[:], cst_d, cst_d.h[:, :])
    ident_bf = P.sb([128, 128], BF16, "identbf")
    mask_f = cst
    mask_bf = P.sb([128, 128], BF16, "maskbf")
    P.op("dve", lambda e: e.tensor_copy(out=ident_bf[:], in_=cst[:, 0:128]), reads=[cst], writes=[ident_bf])
    P.op("dve", lambda e: e.tensor_copy(out=mask_bf[:], in_=cst[:, 128:256]), reads=[cst], writes=[mask_bf])
    one_c = P.sb([128, 1], F32, "onec")
    P.op("pool", lambda e: e.memset(one_c[:], 1.0), writes=[one_c])
    pscale = P.sb([128, 2], F32, "pscale")
    P.dma("sp", pscale, pscale[:], ps_d, ps_d.h[:, :])
    selw = P.sb([128, 2, 21], F32, "selw")
    P.dma("sp", selw, selw[:], selw_d, selw_d.h[:, :, :])
    hch = [P.sb([128, KC, 256], BF16, "hch") for _ in range(2)]
    pbank = [P.ps([128, 512], F32, "pbank") for _ in range(2)]
    cnt = {}

    def rot(lst, key):
        i = cnt.get(key, 0)
        cnt[key] = i + 1
        return lst[i % len(lst)]

    def load_h(c):
        t = rot(hch, "hch")
        for kq in range(4):
            P.dma("sp", t, t[:, kq * 4:(kq + 1) * 4, :], hT[(0, c)],
                  hT[(0, c)].h[kq * 4:(kq + 1) * 4, :, c * 256:(c + 1) * 256].rearrange("k p t -> p k t"))
        return t

    with ExitStack() as es:
        P.es = es
        wps = [P.sb([128, KC, 128], BF16, "wp") for _ in range(2)]
        pws = [P.sb([128, 128], BF16, "pw") for _ in range(2)]
        for gi in range(2):
            P.dma("pool", wps[gi], wps[gi][:], wp, wp.h[gi])
            P.dma("pool", pws[gi], pws[gi][:], pw_d, pw_d.h[gi])
        u = [P.sb([128, 16 + L], F32, "upool") for _ in range(2)]
        sa = P.sb([128, 16 + L], F32, "sa")
        sbb = P.sb([128, 16 + L], F32, "sbb")
        pooled = P.sb([128, L], BF16, "pooled")
        yp = [P.sb([128, 512], BF16, "yp") for _ in range(2)]
        for t in u + [sa, sbb]:
            P.op("pool", lambda e, t=t: e.memset(t[:, 0:16], 0.0), writes=[t])
        for c in range(16):
            ht = load_h(c)
            for gi in range(2):
                bk = rot(pbank, "pb")
                P.group("pe", [(lambda e, kc=kc, bk=bk, gi=gi, ht=ht: e.matmul(
                    bk[:, 0:256], lhsT=wps[gi][:, kc, :], rhs=ht[:, kc, :], start=(kc == 0), stop=(kc == KC - 1)))
                    for kc in range(KC)], reads=[wps[gi], ht], writes=[bk])
                P.op("act", lambda e, bk=bk, gi=gi, c=c: e.copy(out=u[gi][:, 16 + c * 256:16 + (c + 1) * 256],
                                                                 in_=bk[:, 0:256]), reads=[bk], writes=[u[gi]])
        sc = P.sb([128, L], F32, "sc")
        acc = P.sb([128, L], F32, "acc")
        for gi in range(2):
            src = u[gi]
            bufs = [sa, sbb]
            for k in range(4):
                sh = 1 << k
                dst = bufs[k % 2]
                P.op("dve", lambda e, dst=dst, src=src, sh=sh: e.tensor_tensor(
                    out=dst[:, 16:16 + L], in0=src[:, 16:16 + L], in1=src[:, 16 - sh:16 + L - sh], op=ALU.add),
                    reads=[src], writes=[dst])
                src = dst
                if k == 0:
                    P.op("dve", lambda e, dst=dst, gi=gi: e.tensor_scalar(
                        out=acc[:], in0=dst[:, 16:16 + L], scalar1=selw[:, gi, 0:1], scalar2=None, op0=ALU.mult),
                        reads=[dst, selw], writes=[acc])
                else:
                    P.op("dve", lambda e, dst=dst, gi=gi, k=k: e.scalar_tensor_tensor(
                        out=acc[:], in0=dst[:, 16:16 + L], scalar=selw[:, gi, k:k + 1], in1=acc[:],
                        op0=ALU.mult, op1=ALU.add), reads=[dst, selw], writes=[acc])
            P.op("dve", lambda e, gi=gi: e.tensor_scalar(
                out=sc[:], in0=acc[:], scalar1=selw[:, gi, 4:5], scalar2=None, op0=ALU.mult),
                reads=[acc, selw], writes=[sc])
            P.op("dve", lambda e, gi=gi: e.tensor_tensor(
                out=sc[:, 0:16], in0=acc[:, 0:16], in1=selw[:, gi, 5:21], op=ALU.mult),
                reads=[acc, selw], writes=[sc])
            P.op("dve", lambda e, gi=gi: e.tensor_tensor(
                out=pooled[:], in0=sc[:], in1=u[gi][:, 16:16 + L], op=ALU.subtract),
                reads=[sc, u[gi]], writes=[pooled])
            for c in range(8):
                bk = rot(pbank, "pb")
                P.op("pe", lambda e, bk=bk, gi=gi, c=c: e.matmul(
                    bk[:], lhsT=pws[gi][:], rhs=pooled[:, c * 512:(c + 1) * 512], start=True, stop=True),
                    reads=[pws[gi], pooled], writes=[bk])
                y = rot(yp, "yp")
                P.op("act", lambda e, y=y, bk=bk, gi=gi: e.activation(
                    out=y[:], in_=bk[:], func=AF.Copy, scale=pscale[:, gi:gi + 1]),
                    reads=[bk, pscale], writes=[y])
                P.dma("act", ymT[(tile_of(gi), c)], ymT[(tile_of(gi), c)].h[tile_of(gi), :, c * 512:(c + 1) * 512], y, y[:], is_output=is_output)
        P.barrier()
    for hp in range(3):
        with ExitStack() as es:
            P.es = es
            wqs = [P.sb([128, KC, 128], BF16, "wq") for _ in range(2)]
            wks = [P.sb([128, KC, 128], BF16, "wk") for _ in range(2)]
            wvs = P.sb([128, KC, 256], BF16, "wv")
            for j in range(2):
                P.dma("pool", wqs[j], wqs[j][:], wq, wq.h[hp * 2 + j])
                P.dma("pool", wks[j], wks[j][:], wk, wk.h[hp * 2 + j])
            P.dma("pool", wvs, wvs[:], wv, wv.h[hp])
            qT = [P.sb([128, L], BF16, "qT") for _ in range(2)]
            kT = [P.sb([128, L], BF16, "kT") for _ in range(2)]
            v = P.sb([128, 32, 256], BF16, "v")
            for c in range(16):
                ht = load_h(c)
                for j in range(2):
                    for (ws, dst) in ((wqs[j], qT[j]), (wks[j], kT[j])):
                        bk = rot(pbank, "pb")
                        P.group("pe", [(lambda e, kc=kc, bk=bk, ws=ws, ht=ht: e.matmul(
                            bk[:, 0:256], lhsT=ws[:, kc, :], rhs=ht[:, kc, :], start=(kc == 0), stop=(kc == KC - 1)))
                            for kc in range(KC)], reads=[ws, ht], writes=[bk])
                        P.op("act", lambda e, bk=bk, dst=dst, c=c: e.copy(
                            out=dst[:, c * 256:(c + 1) * 256], in_=bk[:, 0:256]), reads=[bk], writes=[dst])
                for tb in range(2):
                    bk = rot(pbank, "pb")
                    P.group("pe", [(lambda e, kc=kc, bk=bk, tb=tb, ht=ht: e.matmul(
                        bk[:, 0:256], lhsT=ht[:, kc, tb * 128:(tb + 1) * 128], rhs=wvs[:, kc, :],
                        start=(kc == 0), stop=(kc == KC - 1))) for kc in range(KC)],
                        reads=[wvs, ht], writes=[bk])
                    P.op("dve", lambda e, bk=bk, c=c, tb=tb: e.tensor_copy(
                        out=v[:, c * 2 + tb, :], in_=bk[:, 0:256]), reads=[bk], writes=[v])
            Pps = [P.sb([128, L + 1], F32, "Pp") for _ in range(2)]
            for Pp in Pps:
                P.op("pool", lambda e, Pp=Pp: e.memset(Pp[:, 0:1], 0.0), writes=[Pp])
            Eb = [P.sb([128, L], F32, "E") for _ in range(2)]
            wb = [P.sb([128, L], BF16, "w") for _ in range(2)]
            et = [P.sb([128, 512], F32, "et") for _ in range(4)]
            lt = [P.sb([128, 512], F32, "lt") for _ in range(4)]
            negT = [P.sb([128, 1], F32, "negT") for _ in range(2)]
            wTt = [P.sb([128, 512], BF16, "wT") for _ in range(3)]
            yT = [P.sb([128, 128], BF16, "yT") for _ in range(2)]
            zbank = [P.ps([128, 512], F32, "zbank") for _ in range(2)] + pbank
            tbank = [P.ps([128, 512], BF16, "tbank") for _ in range(2)]
            obank = [P.ps([128, 128], F32, "obank") for _ in range(2)]
            ones512 = cst
            def p1(i, j):
                    jg = 2 + hp * 2 + j
                    Pp = Pps[j]
                    nk = (i + 1) * 128
                    nch = (nk + 511) // 512
                    E = rot(Eb, "E")
                    w = rot(wb, "w")
                    for c in range(nch):
                        k0 = c * 512
                        kn = min(512, nk - k0)
                        zb = rot(zbank, "zb")
                        P.op("pe", lambda e, zb=zb, j=j, i=i, k0=k0, kn=kn: e.matmul(
                            zb[:, 0:kn], lhsT=qT[j][:, i * 128:(i + 1) * 128], rhs=kT[j][:, k0:k0 + kn],
                            start=True, stop=True), reads=[qT[j], kT[j]], writes=[zb])
                        e_t = rot(et, "et")
                        l_t = rot(lt, "lt")
                        P.op("act", lambda e, e_t=e_t, zb=zb, kn=kn: e.activation(
                            out=e_t[:, 0:kn], in_=zb[:, 0:kn], func=AF.Exp, scale=SCALE), reads=[zb], writes=[e_t])
                        P.op("act", lambda e, e_t=e_t, l_t=l_t, kn=kn: e.activation(
                            out=l_t[:, 0:kn], in_=e_t[:, 0:kn], func=AF.Ln, bias=one_c[:, 0:1], scale=1.0),
                            reads=[e_t, one_c], writes=[l_t])
                        if c == nch - 1:
                            P.op("pool", lambda e, l_t=l_t, kn=kn: e.tensor_tensor(
                                out=l_t[:, kn - 128:kn], in0=l_t[:, kn - 128:kn], in1=cst[:, 128:256], op=ALU.mult),
                                reads=[mask_f], writes=[l_t])
                        init = 0.0 if c == 0 else Pp[:, k0:k0 + 1]
                        P.op("dve", lambda e, l_t=l_t, k0=k0, kn=kn, init=init, Pp=Pp: e.tensor_tensor_scan(
                            out=Pp[:, 1 + k0:1 + k0 + kn], data0=cst[:, 256:256 + kn], data1=l_t[:, 0:kn],
                            initial=init, op0=ALU.mult, op1=ALU.add), reads=[l_t, ones512], writes=[Pp])
                        P.op("dve", lambda e, E=E, zb=zb, k0=k0, kn=kn, Pp=Pp: e.scalar_tensor_tensor(
                            out=E[:, k0:k0 + kn], in0=zb[:, 0:kn], scalar=SCALE, in1=Pp[:, k0:k0 + kn],
                            op0=ALU.mult, op1=ALU.add), reads=[zb, Pp], writes=[E])
                    nt = rot(negT, "negT")
                    P.op("dve", lambda e, nt=nt, nk=nk, Pp=Pp: e.tensor_scalar(
                        out=nt[:], in0=Pp[:, nk:nk + 1], scalar1=-1.0, scalar2=None, op0=ALU.mult),
                        reads=[Pp], writes=[nt])
                    return dict(i=i, j=j, jg=jg, nk=nk, nch=nch, E=E, w=w, nt=nt)

            def p2(st_):
                    i, j, jg, nk, nch, E, w, nt = (st_[k_] for k_ in ('i', 'j', 'jg', 'nk', 'nch', 'E', 'w', 'nt'))
                    for c in range(nch):
                        k0 = c * 512
                        kn = min(512, nk - k0)
                        P.op("act", lambda e, w=w, E=E, nt=nt, k0=k0, kn=kn: e.activation(
                            out=w[:, k0:k0 + kn], in_=E[:, k0:k0 + kn], func=AF.Exp, bias=nt[:, 0:1], scale=1.0),
                            reads=[E, nt], writes=[w])
                    P.op("pool", lambda e, w=w, nk=nk: e.tensor_tensor(
                        out=w[:, nk - 128:nk], in0=w[:, nk - 128:nk], in1=mask_bf[:], op=ALU.mult),
                        reads=[mask_bf], writes=[w])
                    ob = rot(obank, "ob")
                    groups = list(range(0, i + 1, 4))

                    def emit_T(s0):
                        ns = min(4, i + 1 - s0)
                        tb_ = rot(tbank, "tb")
                        P.group("pe", [(lambda e, tb_=tb_, w=w, s0=s0, q=q: e.transpose(
                            out=tb_[:, q * 128:(q + 1) * 128], in_=w[:, (s0 + q) * 128:(s0 + q + 1) * 128],
                            identity=ident_bf[:])) for q in range(ns)], reads=[w, ident_bf], writes=[tb_])
                        wT = rot(wTt, "wT")
                        P.op("dve", lambda e, wT=wT, tb_=tb_, ns=ns: e.tensor_copy(
                            out=wT[:, 0:ns * 128], in_=tb_[:, 0:ns * 128]), reads=[tb_], writes=[wT])
                        return (s0, ns, wT)

                    def emit_PV(g_):
                        s0, ns, wT = g_
                        P.group("pe", [(lambda e, ob=ob, wT=wT, s0=s0, q=q, j=j, i=i: e.matmul(
                            ob[:], lhsT=v[:, s0 + q, j * 128:(j + 1) * 128], rhs=wT[:, q * 128:(q + 1) * 128],
                            start=(s0 + q == 0), stop=(s0 + q == i))) for q in range(ns)],
                            reads=[v, wT], writes=[ob])

                    pg_ = None
                    for s0 in groups:
                        g_ = emit_T(s0)
                        if pg_ is not None:
                            emit_PV(pg_)
                        pg_ = g_
                    emit_PV(pg_)
                    y = rot(yT, "yT")
                    P.op("act", lambda e, y=y, ob=ob: e.copy(out=y[:], in_=ob[:]), reads=[ob], writes=[y])
                    P.dma("act", ymT[(tile_of(jg), i)], ymT[(tile_of(jg), i)].h[tile_of(jg), :, i * 128:(i + 1) * 128], y, y[:], is_output=is_output)

            pend_ = None
            for i in range(32):
                for j in range(2):
                    st_ = p1(i, j)
                    if pend_ is not None:
                        p2(pend_)
                    pend_ = st_
            p2(pend_)
            P.barrier()
    P.es = base_es

import math

L = 4096
KC = 16
NGP = 16
PI = math.pi
MAGIC = 12582912.0
TWO_PI_S = 2 * math.pi * (1 - 2e-6)


def emit_ossm(P, hT, wu_d, prm_d, bsm_d, ct_d, dd_d, cst_d, ysT, ys_ap=None, is_output=True):
    cnt = {}

    def rot(lst, key):
        i = cnt.get(key, 0)
        cnt[key] = i + 1
        return lst[i % len(lst)]

    cst = P.sb([128, 1152], F32, "cst")
    P.dma("sp", cst, cst[:], cst_d, cst_d.h[:, :])
    ident = cst
    prm = P.sb([128, 3, 16], F32, "prm")
    P.dma("sp", prm, prm[:], prm_d, prm_d.h[:, :, :])
    bsm = P.sb([128, 2, 16, 32], F32, "bsm")
    P.dma("sp", bsm, bsm[:], bsm_d, bsm_d.h[:, :, :, :])
    ctf = P.sb([128, 2, 16, 32], F32, "ctf")
    P.dma("sp", ctf, ctf[:], ct_d, ct_d.h[:, :, :, :])
    ddf = P.sb([32, 16, 32], F32, "ddf")
    P.dma("sp", ddf, ddf[:], dd_d, dd_d.h[:, :, :])
    wus = P.sb([128, NGP, KC, 32], BF16, "wus")
    for gp in range(NGP):
        P.dma("pool", wus, wus[:, gp], wu_d, wu_d.h[gp])
    negpi = P.sb([128, 1], F32, "negpi")
    P.op("pool", lambda e: e.memset(negpi[:], -PI), writes=[negpi])

    def small(name):
        return P.sb([128, 16], F32, name)

    dt, adt, th, dec, sa, ca, sn, cs = [small(n) for n in ["dt", "adt", "th", "dec", "sa", "ca", "sn", "cs"]]
    lre, lim, nr, den, f_re, f_im, tA, tB = [small(n) for n in ["lre", "lim", "nr", "den", "fre", "fim", "tA", "tB"]]
    a_re = lambda: prm[:, 0, :]
    a_im = lambda: prm[:, 1, :]
    P.op("act", lambda e: e.activation(out=dt[:], in_=prm[:, 2, :], func=AF.Exp), reads=[prm], writes=[dt])
    P.op("dve", lambda e: e.tensor_tensor(out=adt[:], in0=a_re(), in1=dt[:], op=ALU.mult), reads=[prm, dt], writes=[adt])
    P.op("dve", lambda e: e.tensor_tensor(out=th[:], in0=a_im(), in1=dt[:], op=ALU.mult), reads=[prm, dt], writes=[th])
    P.op("act", lambda e: e.activation(out=dec[:], in_=adt[:], func=AF.Exp), reads=[adt], writes=[dec])
    thn = small("thn")
    P.op("dve", lambda e: e.tensor_scalar(out=thn[:], in0=th[:], scalar1=1.0 / (2 * PI), scalar2=None, op0=ALU.mult),
         reads=[th], writes=[thn])

    def emit_sincos(src, dst_sin, dst_cos, mk):
        for (off, dst) in ((0.0, dst_sin), (0.25, dst_cos)):
            a2 = mk()
            k_ = mk()
            P.op("dve", lambda e, a2=a2, off=off: e.tensor_scalar(out=a2[:], in0=src[:], scalar1=off, scalar2=None, op0=ALU.add),
                 reads=[src], writes=[a2])
            P.op("dve", lambda e, a2=a2, k_=k_: e.tensor_scalar(out=k_[:], in0=a2[:], scalar1=MAGIC, scalar2=MAGIC,
                                                                 op0=ALU.add, op1=ALU.subtract), reads=[a2], writes=[k_])
            P.op("dve", lambda e, a2=a2, k_=k_: e.tensor_tensor(out=a2[:], in0=a2[:], in1=k_[:], op=ALU.subtract),
                 reads=[k_], writes=[a2])
            P.op("act", lambda e, a2=a2, dst=dst: e.activation(out=dst[:], in_=a2[:], func=AF.Sin, scale=TWO_PI_S),
                 reads=[a2], writes=[dst])

    emit_sincos(thn, sn, cs, lambda: small("sc_tmp"))
    P.op("dve", lambda e: e.tensor_tensor(out=lre[:], in0=dec[:], in1=cs[:], op=ALU.mult), reads=[dec, cs], writes=[lre])
    P.op("dve", lambda e: e.tensor_tensor(out=lim[:], in0=dec[:], in1=sn[:], op=ALU.mult), reads=[dec, sn], writes=[lim])
    P.op("dve", lambda e: e.tensor_scalar(out=nr[:], in0=lre[:], scalar1=-1.0, scalar2=None, op0=ALU.add), reads=[lre], writes=[nr])
    P.op("dve", lambda e: e.tensor_tensor(out=tA[:], in0=a_re(), in1=a_re(), op=ALU.mult), reads=[prm], writes=[tA])
    P.op("dve", lambda e: e.tensor_tensor(out=tB[:], in0=a_im(), in1=a_im(), op=ALU.mult), reads=[prm], writes=[tB])
    P.op("dve", lambda e: e.tensor_tensor(out=den[:], in0=tA[:], in1=tB[:], op=ALU.add), reads=[tA, tB], writes=[den])
    P.op("dve", lambda e: e.reciprocal(out=den[:], in_=den[:]), reads=[], writes=[den])
    P.op("dve", lambda e: e.tensor_tensor(out=tA[:], in0=nr[:], in1=a_re(), op=ALU.mult), reads=[nr, prm], writes=[tA])
    P.op("dve", lambda e: e.tensor_tensor(out=tB[:], in0=lim[:], in1=a_im(), op=ALU.mult), reads=[lim, prm], writes=[tB])
    P.op("dve", lambda e: e.tensor_tensor(out=tA[:], in0=tA[:], in1=tB[:], op=ALU.add), reads=[tB], writes=[tA])
    P.op("dve", lambda e: e.tensor_tensor(out=f_re[:], in0=tA[:], in1=den[:], op=ALU.mult), reads=[tA, den], writes=[f_re])
    P.op("dve", lambda e: e.tensor_tensor(out=tA[:], in0=lim[:], in1=a_re(), op=ALU.mult), reads=[lim, prm], writes=[tA])
    P.op("dve", lambda e: e.tensor_tensor(out=tB[:], in0=nr[:], in1=a_im(), op=ALU.mult), reads=[nr, prm], writes=[tB])
    P.op("dve", lambda e: e.tensor_tensor(out=tA[:], in0=tA[:], in1=tB[:], op=ALU.subtract), reads=[tB], writes=[tA])
    P.op("dve", lambda e: e.tensor_tensor(out=f_im[:], in0=tA[:], in1=den[:], op=ALU.mult), reads=[tA, den], writes=[f_im])

    BT = [P.sb([32, NGP, 128], BF16, "BTre"), P.sb([32, NGP, 128], BF16, "BTim")]
    xt_ = [P.sb([128, 32], F32, "xt_") for _ in range(2)]
    xo = [P.sb([128, 32], F32, "xo") for _ in range(2)]
    tps = [P.ps([32, 128], F32, "tps") for _ in range(1)]
    for gp in range(NGP):
        for ri in range(2):
            t_ = rot(xt_, "xt_")
            o_ = rot(xo, "xo")
            if ri == 0:
                P.op("dve", lambda e, t_=t_, gp=gp: e.tensor_scalar(
                    out=t_[:], in0=bsm[:, 1, gp, :], scalar1=f_im[:, gp:gp + 1], scalar2=None, op0=ALU.mult),
                    reads=[bsm, f_im], writes=[t_])
                P.op("dve", lambda e, t_=t_, o_=o_, gp=gp: e.scalar_tensor_tensor(
                    out=o_[:], in0=bsm[:, 0, gp, :], scalar=f_re[:, gp:gp + 1], in1=t_[:],
                    op0=ALU.mult, op1=ALU.subtract), reads=[bsm, f_re, t_], writes=[o_])
            else:
                P.op("dve", lambda e, t_=t_, gp=gp: e.tensor_scalar(
                    out=t_[:], in0=bsm[:, 0, gp, :], scalar1=f_im[:, gp:gp + 1], scalar2=None, op0=ALU.mult),
                    reads=[bsm, f_im], writes=[t_])
                P.op("dve", lambda e, t_=t_, o_=o_, gp=gp: e.scalar_tensor_tensor(
                    out=o_[:], in0=bsm[:, 1, gp, :], scalar=f_re[:, gp:gp + 1], in1=t_[:],
                    op0=ALU.mult, op1=ALU.add), reads=[bsm, f_re, t_], writes=[o_])
            tp = rot(tps, "tps")
            P.op("pe", lambda e, tp=tp, o_=o_: e.transpose(out=tp[:], in_=o_[:], identity=cst[:, 0:128]),
                 reads=[o_, cst], writes=[tp])
            P.op("act", lambda e, tp=tp, ri=ri, gp=gp: e.copy(out=BT[ri][:, gp, :], in_=tp[:]),
                 reads=[tp], writes=[BT[ri]])
    ctb = [P.sb([128, NGP, 32], BF16, "ctre"), P.sb([128, NGP, 32], BF16, "ctimn")]
    P.op("dve", lambda e: e.tensor_copy(out=ctb[0][:], in_=ctf[:, 0]), reads=[ctf], writes=[ctb[0]])
    P.op("dve", lambda e: e.tensor_scalar(out=ctb[1][:], in0=ctf[:, 1], scalar1=-1.0, scalar2=None, op0=ALU.mult),
         reads=[ctf], writes=[ctb[1]])
    ddb = P.sb([32, NGP, 32], BF16, "ddb")
    P.op("dve", lambda e: e.tensor_copy(out=ddb[:], in_=ddf[:]), reads=[ddf], writes=[ddb])

    cosT = [P.sb([128, 512], F32, "cosT") for _ in range(NGP)]
    sinT = [P.sb([128, 512], F32, "sinT") for _ in range(NGP)]
    decT = [P.sb([128, 512], F32, "decT") for _ in range(2)]
    ang = [P.sb([128, 512], F32, "ang") for _ in range(2)]
    arg = [P.sb([128, 512], F32, "arg") for _ in range(4)]
    for gp in range(NGP):
        a_ = rot(ang, "ang")
        P.op("dve", lambda e, a_=a_, gp=gp: e.tensor_scalar(
            out=a_[:], in0=cst[:, 128:640], scalar1=thn[:, gp:gp + 1], scalar2=None, op0=ALU.mult),
            reads=[cst, thn], writes=[a_])
        emit_sincos(a_, sinT[gp], cosT[gp], lambda: rot(arg, "arg"))

    carry = [[P.sb([128, 1], F32, f"car{ri}") for ri in range(2)] for _ in range(NGP)]
    for gp in range(NGP):
        for ri in range(2):
            P.op("pool", lambda e, gp=gp, ri=ri: e.memset(carry[gp][ri][:], 0.0), writes=[carry[gp][ri]])
    hch = [P.sb([128, KC, 512], BF16, "hch") for _ in range(2)]
    pu_b = [P.ps([32, 512], F32, "pu") for _ in range(2)]
    pr_b = [P.ps([128, 512], F32, "pr") for _ in range(2)]
    pi_b = [P.ps([128, 512], F32, "pi") for _ in range(2)]
    py_b = [P.ps([32, 512], F32, "py") for _ in range(1)]
    ugs = [P.sb([32, 512], BF16, "ug") for _ in range(2)]
    tt = [P.sb([128, 512], F32, "tt") for _ in range(8)]
    mm = [P.sb([128, 512], F32, "mm") for _ in range(4)]
    ww = [P.sb([128, 512], F32, "ww") for _ in range(4)]
    sf = [P.sb([128, 512], F32, "sf") for _ in range(4)]
    sbf = [P.sb([128, 512], BF16, "sbf") for _ in range(4)]
    ybs = [P.sb([32, 512], BF16, "yb") for _ in range(2)]
    pend_ = None
    for tc in range(L // 512):
        ht = rot(hch, "hch")
        for kq in range(4):
            P.dma("sp", ht, ht[:, kq * 4:(kq + 1) * 4, :], hT,
                  hT.h[kq * 4:(kq + 1) * 4, :, tc * 512:(tc + 1) * 512].rearrange("k p t -> p k t"))
        def p1(tc, gp, ht):
                pu = rot(pu_b, "pu")
                P.group("pe", [(lambda e, kc=kc, pu=pu, gp=gp, ht=ht: e.matmul(
                    pu[:], lhsT=wus[:, gp, kc, :], rhs=ht[:, kc, :], start=(kc == 0), stop=(kc == KC - 1)))
                    for kc in range(KC)], reads=[wus, ht], writes=[pu])
                ug = rot(ugs, "ug")
                P.op("act", lambda e, ug=ug, pu=pu: e.copy(out=ug[:], in_=pu[:]), reads=[pu], writes=[ug])
                pr = rot(pr_b, "pr")
                pi_ = rot(pi_b, "pi")
                P.op("pe", lambda e, pr=pr, ug=ug, gp=gp: e.matmul(pr[:], lhsT=BT[0][:, gp, :], rhs=ug[:], start=True, stop=True),
                     reads=[BT[0], ug], writes=[pr])
                P.op("pe", lambda e, pi_=pi_, ug=ug, gp=gp: e.matmul(pi_[:], lhsT=BT[1][:, gp, :], rhs=ug[:], start=True, stop=True),
                     reads=[BT[1], ug], writes=[pi_])
                cT, sT, dT = cosT[gp], sinT[gp], rot(decT, "decT")
                P.op("act", lambda e, gp=gp, dT=dT: e.activation(
                    out=dT[:], in_=cst[:, 640:1152], func=AF.Copy, scale=dec[:, gp:gp + 1]),
                    reads=[cst, dec], writes=[dT])
                t1, t2, t3, t4 = [rot(tt, "tt") for _ in range(4)]
                P.op("dve", lambda e, t1=t1, pr=pr, cT=cT: e.tensor_tensor(out=t1[:], in0=pr[:], in1=cT[:], op=ALU.mult),
                     reads=[pr, cT], writes=[t1])
                P.op("dve", lambda e, t2=t2, pi_=pi_, sT=sT: e.tensor_tensor(out=t2[:], in0=pi_[:], in1=sT[:], op=ALU.mult),
                     reads=[pi_, sT], writes=[t2])
                P.op("dve", lambda e, t3=t3, pi_=pi_, cT=cT: e.tensor_tensor(out=t3[:], in0=pi_[:], in1=cT[:], op=ALU.mult),
                     reads=[pi_, cT], writes=[t3])
                P.op("dve", lambda e, t4=t4, pr=pr, sT=sT: e.tensor_tensor(out=t4[:], in0=pr[:], in1=sT[:], op=ALU.mult),
                     reads=[pr, sT], writes=[t4])
                m_re, m_im = rot(mm, "mm"), rot(mm, "mm")
                P.op("pool", lambda e, m_re=m_re, t1=t1, t2=t2: e.tensor_tensor(out=m_re[:], in0=t1[:], in1=t2[:], op=ALU.add),
                     reads=[t1, t2], writes=[m_re])
                P.op("pool", lambda e, m_im=m_im, t3=t3, t4=t4: e.tensor_tensor(out=m_im[:], in0=t3[:], in1=t4[:], op=ALU.subtract),
                     reads=[t3, t4], writes=[m_im])
                w_re, w_im = rot(ww, "ww"), rot(ww, "ww")
                for (w_, m_, ri) in ((w_re, m_re, 0), (w_im, m_im, 1)):
                    P.op("dve", lambda e, w_=w_, m_=m_, ri=ri, gp=gp, dT=dT: e.tensor_tensor_scan(
                        out=w_[:], data0=dT[:], data1=m_[:], initial=carry[gp][ri][:, 0:1], op0=ALU.mult, op1=ALU.add),
                        reads=[dT, m_, carry[gp][ri]], writes=[w_])
                a1, a2, a3, a4 = [rot(tt, "tt") for _ in range(4)]
                P.op("pool", lambda e, a1=a1, w_re=w_re, cT=cT: e.tensor_tensor(out=a1[:], in0=w_re[:], in1=cT[:], op=ALU.mult),
                     reads=[w_re, cT], writes=[a1])
                P.op("pool", lambda e, a2=a2, w_im=w_im, sT=sT: e.tensor_tensor(out=a2[:], in0=w_im[:], in1=sT[:], op=ALU.mult),
                     reads=[w_im, sT], writes=[a2])
                P.op("pool", lambda e, a3=a3, w_re=w_re, sT=sT: e.tensor_tensor(out=a3[:], in0=w_re[:], in1=sT[:], op=ALU.mult),
                     reads=[w_re, sT], writes=[a3])
                P.op("pool", lambda e, a4=a4, w_im=w_im, cT=cT: e.tensor_tensor(out=a4[:], in0=w_im[:], in1=cT[:], op=ALU.mult),
                     reads=[w_im, cT], writes=[a4])
                s_re, s_im = rot(sf, "sf"), rot(sf, "sf")
                P.op("dve", lambda e, s_re=s_re, a1=a1, a2=a2: e.tensor_tensor(out=s_re[:], in0=a1[:], in1=a2[:], op=ALU.subtract),
                     reads=[a1, a2], writes=[s_re])
                P.op("dve", lambda e, s_im=s_im, a3=a3, a4=a4: e.tensor_tensor(out=s_im[:], in0=a3[:], in1=a4[:], op=ALU.add),
                     reads=[a3, a4], writes=[s_im])
                sb_re, sb_im = rot(sbf, "sbf"), rot(sbf, "sbf")
                for (sb_, s_, ri) in ((sb_re, s_re, 0), (sb_im, s_im, 1)):
                    P.op("act", lambda e, sb_=sb_, s_=s_: e.copy(out=sb_[:], in_=s_[:]), reads=[s_], writes=[sb_])
                    P.op("act", lambda e, s_=s_, gp=gp, ri=ri: e.copy(out=carry[gp][ri][:], in_=s_[:, 511:512]),
                         reads=[s_], writes=[carry[gp][ri]])
                return dict(tc=tc, gp=gp, ug=ug, sb_re=sb_re, sb_im=sb_im)

        def p2(st_):
                tc, gp, ug, sb_re, sb_im = (st_[k_] for k_ in ('tc', 'gp', 'ug', 'sb_re', 'sb_im'))
                py = rot(py_b, "py")
                P.group("pe", [
                    (lambda e, py=py, sb_re=sb_re, gp=gp: e.matmul(py[:], lhsT=ctb[0][:, gp, :], rhs=sb_re[:], start=True, stop=False)),
                    (lambda e, py=py, sb_im=sb_im, gp=gp: e.matmul(py[:], lhsT=ctb[1][:, gp, :], rhs=sb_im[:], start=False, stop=False)),
                    (lambda e, py=py, ug=ug, gp=gp: e.matmul(py[:], lhsT=ddb[:, gp, :], rhs=ug[:], start=False, stop=True)),
                ], reads=[ctb[0], ctb[1], ddb, sb_re, sb_im, ug], writes=[py])
                yb = rot(ybs, "yb")
                P.op("act", lambda e, yb=yb, py=py: e.activation(out=yb[:], in_=py[:], func=AF.Gelu), reads=[py], writes=[yb])
                P.dma("act", ysT[(gp, tc)], (ys_ap(gp, tc) if ys_ap else ysT[(gp, tc)].h[gp, :, tc * 512:(tc + 1) * 512]), yb, yb[:], is_output=is_output)

        for gp in range(NGP):
            st_ = p1(tc, gp, ht)
            if pend_ is not None:
                p2(pend_)
            pend_ = st_
    p2(pend_)


KC = 16
EPS = 1e-6


def emit_omix2(P, hT, yss, wzu_d, wzv_d, glw_d, glb_d, sgn_d, wsT_d, sgb_d, mle_d, ymx, ntok, sel=None, alt_off=0):
    cnt = {}

    def rot(lst, key):
        i = cnt.get(key, 0)
        cnt[key] = i + 1
        return lst[i % len(lst)]

    wzu = [P.sb([128, KC, 128], BF16, "wzu") for _ in range(8)]
    wzv = [P.sb([128, KC, 512], BF16, "wzv") for _ in range(2)]
    glw = [P.sb([128, 8, 128], BF16, "glw") for _ in range(8)]
    for m in range(8):
        P.dma("pool", wzu[m], wzu[m][:], wzu_d, wzu_d.h[m])
        P.dma("pool", glw[m], glw[m][:], glw_d, glw_d.h[m])
    for hf in range(2):
        for q in range(4):
            P.dma("pool", wzv[hf], wzv[hf][:, q * 4:(q + 1) * 4, :], wzv_d, wzv_d.h[hf, :, q * 4:(q + 1) * 4, :])
    glb = P.sb([128, 8], F32, "glb")
    P.dma("sp", glb, glb[:], glb_d, glb_d.h[:, :])
    sgn = P.sb([128, 1024], F32, "sgn")
    P.dma("sp", sgn, sgn[:], sgn_d, sgn_d.h[:, :])
    wsf = P.sb([128, 8, 128], F32, "wsf")
    P.dma("sp", wsf, wsf[:], wsT_d, wsT_d.h[:, :, :])
    sgb = P.sb([128, 8, 128], F32, "sgb")
    P.dma("sp", sgb, sgb[:], sgb_d, sgb_d.h[:, :, :])
    mle = P.sb([128, 128], F32, "mle")
    P.dma("sp", mle, mle[:], mle_d, mle_d.h[:, :])
    wsb = P.sb([128, 8, 128], BF16, "wsb")
    for hd in range(8):
        P.op("dve", lambda e, hd=hd: e.tensor_tensor(out=wsb[:, hd, :], in0=wsf[:, hd, :], in1=mle[:], op=ALU.mult),
             reads=[wsf, mle], writes=[wsb])
    epst = P.sb([128, 1], F32, "epst")
    P.op("pool", lambda e: e.memset(epst[:], EPS), writes=[epst])

    hch = [P.sb([128, KC, 512], BF16, "hch") for _ in range(2)]
    ych = [P.sb([128, 8, 512], BF16, "ych") for _ in range(2)]
    uT = P.sb([128, 8, 512], F32, "uT")
    vg = [P.sb([128, 1024], F32, "vg") for _ in range(2)]
    sqv = P.sb([128, 1024], F32, "sqv")
    ss = [P.sb([128, 1], F32, "ss") for _ in range(2)]
    rs = [P.sb([128, 1], F32, "rs") for _ in range(2)]
    vtm = [P.sb([128, 1024], BF16, "vtm") for _ in range(2)]
    sig = [P.sb([128, 512], F32, "sig") for _ in range(2)]
    tmp4 = [P.sb([128, 4, 128], F32, "tmp4") for _ in range(2)]
    yo = [P.sb([128, 512], BF16, "yo") for _ in range(3)]
    yo4 = [P.sb([128, 4, 512], BF16, "yo4") for _ in range(2)]
    bank = [P.ps([128, 512], F32, "bk") for _ in range(4)]
    bank4 = [P.ps([128, 4, 128], F32, "bk4") for _ in range(2)]

    for c in range(ntok // 512):
        tsl = slice(c * 512, (c + 1) * 512)
        ht = rot(hch, "hch")
        for kq in range(4):
            P.dma("sp", ht, ht[:, kq * 4:(kq + 1) * 4, :], hT,
                  hT.h[kq * 4:(kq + 1) * 4, :, tsl].rearrange("k p t -> p k t"))
        yt = rot(ych, "ych")
        for kq in range(2):
            P.dma("sp", yt, yt[:, kq * 4:(kq + 1) * 4, :], yss,
                  yss.h[kq * 4:(kq + 1) * 4, :, tsl].rearrange("k p t -> p k t"))
        if sel is not None:
            asl = slice(alt_off + c * 512, alt_off + (c + 1) * 512)
            hb_ = rot(hch, "hch")
            for kq in range(4):
                P.dma("sp", hb_, hb_[:, kq * 4:(kq + 1) * 4, :], hT,
                      hT.h[kq * 4:(kq + 1) * 4, :, asl].rearrange("k p t -> p k t"))
            yb_ = rot(ych, "ych")
            for kq in range(2):
                P.dma("sp", yb_, yb_[:, kq * 4:(kq + 1) * 4, :], yss,
                      yss.h[kq * 4:(kq + 1) * 4, :, asl].rearrange("k p t -> p k t"))
            for (a_, b_) in ((ht, hb_), (yt, yb_)):
                P.op("pool", lambda e, a_=a_: e.tensor_scalar(out=a_[:], in0=a_[:], scalar1=sel[:, 0:1], scalar2=None,
                                                              op0=ALU.mult), reads=[sel], writes=[a_])
                P.op("dve", lambda e, a_=a_, b_=b_: e.scalar_tensor_tensor(
                    out=a_[:], in0=b_[:], scalar=sel[:, 1:2], in1=a_[:], op0=ALU.mult, op1=ALU.add),
                    reads=[b_, sel], writes=[a_])
        for m in range(8):
            bk = rot(bank, "bk")
            P.group("pe", [(lambda e, kc=kc, bk=bk, m=m, yt=yt: e.matmul(
                bk[:], lhsT=glw[m][:, kc, :], rhs=yt[:, kc, :], start=(kc == 0), stop=(kc == 7)))
                for kc in range(8)], reads=[glw[m], yt], writes=[bk])
            sg = rot(sig, "sig")
            P.op("act", lambda e, sg=sg, bk=bk, m=m: e.activation(
                out=sg[:], in_=bk[:], func=AF.Sigmoid, bias=glb[:, m:m + 1], scale=1.0),
                reads=[bk, glb], writes=[sg])
            y_ = rot(yo, "yo")
            P.op("dve", lambda e, y_=y_, sg=sg, yt=yt, m=m: e.tensor_tensor(
                out=y_[:], in0=sg[:], in1=yt[:, m, :], op=ALU.mult), reads=[sg, yt], writes=[y_])
            P.dma("act", ymx[(m, c)], ymx[(m, c)].h[m, :, tsl], y_, y_[:])
        for m in range(8):
            bk = rot(bank, "bk")
            P.group("pe", [(lambda e, kc=kc, bk=bk, m=m, ht=ht: e.matmul(
                bk[:], lhsT=wzu[m][:, kc, :], rhs=ht[:, kc, :], start=(kc == 0), stop=(kc == KC - 1)))
                for kc in range(KC)], reads=[wzu[m], ht], writes=[bk])
            P.op("act", lambda e, bk=bk, m=m: e.activation(out=uT[:, m, :], in_=bk[:], func=AF.Gelu),
                 reads=[bk], writes=[uT])
        y4 = [rot(yo4, "yo4") for _ in range(2)]
        for tb in range(4):
            vg_ = rot(vg, "vg")
            for hf in range(2):
                bk = rot(bank, "bk")
                P.group("pe", [(lambda e, kc=kc, bk=bk, hf=hf, ht=ht, tb=tb: e.matmul(
                    bk[:], lhsT=ht[:, kc, tb * 128:(tb + 1) * 128], rhs=wzv[hf][:, kc, :],
                    start=(kc == 0), stop=(kc == KC - 1))) for kc in range(KC)],
                    reads=[wzv[hf], ht], writes=[bk])
                P.op("act", lambda e, bk=bk, vg_=vg_, hf=hf: e.activation(
                    out=vg_[:, hf * 512:(hf + 1) * 512], in_=bk[:], func=AF.Gelu), reads=[bk], writes=[vg_])
            ss_ = rot(ss, "ss")
            rs_ = rot(rs, "rs")
            P.op("dve", lambda e, vg_=vg_: e.tensor_tensor(out=sqv[:], in0=vg_[:], in1=vg_[:], op=ALU.mult),
                 reads=[vg_], writes=[sqv])
            P.op("dve", lambda e, ss_=ss_: e.reduce_sum(out=ss_[:], in_=sqv[:], axis=AX.X), reads=[sqv], writes=[ss_])
            P.op("act", lambda e, ss_=ss_, rs_=rs_: e.activation(
                out=rs_[:], in_=ss_[:], func=AF.Sqrt, bias=epst[:, 0:1], scale=1.0 / 1024),
                reads=[ss_, epst], writes=[rs_])
            P.op("dve", lambda e, rs_=rs_: e.reciprocal(out=rs_[:], in_=rs_[:]), reads=[], writes=[rs_])
            vt = rot(vtm, "vtm")
            P.op("dve", lambda e, vt=vt, vg_=vg_, rs_=rs_: e.scalar_tensor_tensor(
                out=vt[:], in0=vg_[:], scalar=rs_[:, 0:1], in1=sgn[:], op0=ALU.mult, op1=ALU.mult),
                reads=[vg_, rs_, sgn], writes=[vt])
            for j in range(2):
                b4 = rot(bank4, "bk4")
                P.group("pe", [(lambda e, b4=b4, vt=vt, j=j, q=q: e.matmul(
                    b4[:, q, :], lhsT=vt[:, (4 * j + q) * 128:(4 * j + q + 1) * 128], rhs=wsb[:, 4 * j + q, :],
                    start=True, stop=True)) for q in range(4)], reads=[vt, wsb], writes=[b4])
                t4 = rot(tmp4, "tmp4")
                P.op("dve", lambda e, t4=t4, b4=b4, j=j: e.tensor_tensor(
                    out=t4[:], in0=b4[:], in1=sgb[:, 4 * j:4 * j + 4, :], op=ALU.add), reads=[b4, sgb], writes=[t4])
                P.op("dve", lambda e, t4=t4, j=j, tb=tb, y4=y4: e.tensor_tensor(
                    out=y4[j][:, :, tb * 128:(tb + 1) * 128], in0=t4[:],
                    in1=uT[:, 4 * j:4 * j + 4, tb * 128:(tb + 1) * 128], op=ALU.mult),
                    reads=[t4, uT], writes=[y4[j]])
        for j in range(2):
            for q in range(4):
                m = 8 + 4 * j + q
                P.dma("act", ymx[(m, c)], ymx[(m, c)].h[m, :, tsl], y4[j], y4[j][:, q, :])

NCORE = 8
NT = 2048
BF = ml_dtypes.bfloat16


def _bass():
    return bass.Bass("TRN2", target_bir_lowering=False)


def _ffn_w_inputs(P, sfx):
    wg = P.dram("wg" + sfx, [FT, 128, KC, 128], F32, "ExternalInput")
    wu = P.dram("wu" + sfx, [FT, 128, KC, 128], F32, "ExternalInput")
    wd = P.dram("wd" + sfx, [KC, 128, FT, 128], F32, "ExternalInput")
    return wg, wu, wd


def _load_g(P, name):
    gd = P.dram(name, [128, 96], F32, "ExternalInput")
    g = P.sb([128, 96], F32, name)
    P.dma("sp", g, g[:], gd, gd.h[:, :])
    return g


LSEQ = 4096
NACT = 8
LT = 2048


def _whole(P, name, shape, dt, kind):
    base = P.dram(name, shape, dt, kind)

    class _D(dict):
        def __missing__(self, k):
            return base
    d = _D()
    d["is_output"] = (kind == "ExternalOutput")
    d["base"] = base
    return d


def _ym_loader(P, res, ymd):
    def ym(c):
        h = res["h"]
        for kq in range(4):
            P.dma("sp", h, h[:, kq * 4:(kq + 1) * 4, 0:512], ymd,
                  ymd.h[kq * 4:(kq + 1) * 4, :, c * 512:(c + 1) * 512].rearrange("k p t -> p k t"))
        return _View(h)
    return ym


class _View:
    def __init__(self, t):
        self.t = t
        self.w = t.w
        self.r = t.r
        self.name = t.name

    def __getitem__(self, idx):
        a, b, c = idx
        assert c == slice(None)
        return self.t.h[a, b, 0:512]


STOP_AFTER = 99
SKIP = set()


def _on(k):
    return STOP_AFTER >= k and k not in SKIP


def build_FUSED():
    nc = _bass()
    NH = LSEQ // 512
    with ExitStack() as es:
        P = Prog(nc, es)
        base = P.es
        x_in = regions(P, "xT", [KC, 128, LSEQ], F32, "ExternalInput", KC, NH)
        xo = regions(P, "xoT", [KC, 128, LT], F32, "ExternalOutput", KC, LT // 512)
        xs = [regions(P, f"x{i}s", [KC, 128, LSEQ], F32, "Internal", KC, NH) for i in range(1, 5)]
        x1, x2, x3, x4 = xs
        x5 = regions(P, "x5s", [KC, 128, LT], F32, "Internal", KC, LT // 512)
        seld = P.dram("selh", [128, 2], F32, "ExternalInput")
        h1 = regions(P, "h1s", [KC, 128, LSEQ], BF16, "Internal", KC, NH)
        h2 = regions(P, "h2s", [KC, 128, LSEQ], BF16, "Internal", KC, NH)
        ymT = regions(P, "ymTs", [KC, 128, LSEQ], BF16, "Internal", KC, 32)
        ymx = regions(P, "ymxs", [KC, 128, LT], BF16, "Internal", KC, LT // 512)
        ysd = regions(P, "yss", [8, 128, LSEQ], BF16, "Internal", 32, NH)
        wf = [_ffn_w_inputs(P, str(i)) for i in range(4)]
        gd = [P.dram(f"g{i}", [128, 96], F32, "ExternalInput") for i in range(2)]
        em = []
        for hh in range(2):
            s = f"_{hh}"
            em.append(dict(
                wq=P.dram("wq" + s, [6, 128, KC, 128], F32, "ExternalInput"),
                wk=P.dram("wk" + s, [6, 128, KC, 128], F32, "ExternalInput"),
                wv=P.dram("wv" + s, [3, 128, KC, 256], F32, "ExternalInput"),
                wp=P.dram("wp" + s, [2, 128, KC, 128], F32, "ExternalInput"),
                pw=P.dram("pw" + s, [2, 128, 128], F32, "ExternalInput"),
                psc=P.dram("psc" + s, [128, 2], F32, "ExternalInput"),
                selw=P.dram("selw" + s, [128, 2, 21], F32, "ExternalInput")))
        cst = P.dram("cst", [128, 768], F32, "ExternalInput")
        wo0 = P.dram("wo0", [KC, 128, KC, 128], F32, "ExternalInput")
        wo1 = P.dram("wo1", [KC, 128, KC, 128], F32, "ExternalInput")
        om = []
        for hh in range(2):
            s = f"_{hh}"
            om.append(dict(
                wu=P.dram("swu" + s, [NGP, 128, KC, 32], F32, "ExternalInput"),
                prm=P.dram("prm" + s, [128, 3, 16], F32, "ExternalInput"),
                bsm=P.dram("bsm" + s, [128, 2, 16, 32], F32, "ExternalInput"),
                ct=P.dram("ct" + s, [128, 2, 16, 32], F32, "ExternalInput"),
                dd=P.dram("dd" + s, [32, 16, 32], F32, "ExternalInput")))
        cst2 = P.dram("cst2", [128, 1152], F32, "ExternalInput")
        o2 = dict(
            wzu=P.dram("wzu", [8, 128, KC, 128], F32, "ExternalInput"),
            wzv=P.dram("wzv", [2, 128, KC, 512], F32, "ExternalInput"),
            glw=P.dram("glw", [8, 128, 8, 128], F32, "ExternalInput"),
            glb=P.dram("glb", [128, 8], F32, "ExternalInput"),
            sgn=P.dram("sgn", [128, 1024], F32, "ExternalInput"),
            wsT=P.dram("wsT", [128, 8, 128], F32, "ExternalInput"),
            sgb=P.dram("sgb", [128, 8, 128], F32, "ExternalInput"),
            mle=P.dram("mle", [128, 128], F32, "ExternalInput"))
        fscr = [regions(P, f"fscr{i}", [KC, 128, LSEQ], F32, "Internal", KC, NH) for i in range(2)]
        fscr.append(regions(P, "fscr2", [KC, 128, LT], F32, "Internal", KC, LT // 512))

        def load_g(i):
            g = P.sb([128, 96], F32, f"g{i}")
            P.dma("sp", g, g[:], gd[i], gd[i].h[:, :])
            return g

        def ffn_res(fs):
            res = ffn_resources(P, LSEQ, scr_name=None, f_scr=fs)
            return res

        with ExitStack() as sc:
            P.es = sc
            C = Common(P)
            g0 = load_g(0)
            res = ffn_res(fscr[0])
            emit_ffn_pipelined(P, C, res, x_in, x1, wf[0][0], wf[0][1], wf[0][2], g0, 0, 1, LSEQ, h_out=h1, jnext=2)
            P.barrier()
        for hh in range(2 if _on(2) else 0):
            with ExitStack() as sc:
                P.es = sc
                e_ = em[hh]
                emit_emix(P, {(0, c): h1["base"] for c in range(16)}, e_["wq"], e_["wk"], e_["wv"], e_["wp"],
                          e_["pw"], e_["psc"], cst, ymT, e_["selw"],
                          tile_of=(lambda j, hh=hh: (2 * hh + j) if j < 2 else (4 + 6 * hh + j - 2)),
                          is_output=False)
                P.barrier()
        for _ in range(1 if _on(3) else 0):
          with ExitStack() as sc:
            P.es = sc
            C = Common(P)
            g0 = load_g(0)
            g1 = load_g(1)
            res = ffn_res(fscr[1])
            emit_outproj(P, C, res, _ym_loader(P, res, ymT["base"]), wo0, x1, x2, g0, 3, LSEQ)
            emit_ffn_pipelined(P, C, res, x2, x3, wf[1][0], wf[1][1], wf[1][2], g0, 4, 5, LSEQ)
            emit_ffn_pipelined(P, C, res, x3, x4, wf[2][0], wf[2][1], wf[2][2], g1, 0, 1, LSEQ, h_out=h2, jnext=2)
            P.barrier()
        for hh in range(2 if _on(4) else 0):
            with ExitStack() as sc:
                P.es = sc
                o_ = om[hh]

                def ys_ap(gp, tc, hh=hh):
                    gg = 16 * hh + gp
                    return ysd["base"].h[gg // 4, (gg % 4) * 32:(gg % 4) * 32 + 32, tc * 512:(tc + 1) * 512]
                ysT = {(gp, tc): ysd[(16 * hh + gp, tc)] for gp in range(NGP) for tc in range(NH)}
                emit_ossm(P, h2["base"], o_["wu"], o_["prm"], o_["bsm"], o_["ct"], o_["dd"], cst2, ysT,
                          ys_ap=ys_ap, is_output=False)
                P.barrier()
        for _ in range(1 if _on(5) else 0):
          with ExitStack() as sc:
            P.es = sc
            selt = P.sb([128, 2], F32, "selt")
            P.dma("sp", selt, selt[:], seld, seld.h[:, :])
            emit_omix2(P, h2["base"], ysd["base"], o2["wzu"], o2["wzv"], o2["glw"], o2["glb"], o2["sgn"],
                       o2["wsT"], o2["sgb"], o2["mle"], ymx, LT, sel=selt, alt_off=LT)
            P.barrier()
        for _ in range(1 if _on(5) else 0):
          with ExitStack() as sc:
            P.es = sc
            C = Common(P)
            g1 = load_g(1)
            res = ffn_resources(P, LT, scr_name=None, f_scr=fscr[2])
            selt = P.sb([128, 2], F32, "selt")
            P.dma("sp", selt, selt[:], seld, seld.h[:, :])

            def ym(c):
                h = res["h"]
                for kc in range(KC):
                    P.dma("sp", h, h[:, kc, 0:512], ymx[(kc, c)], ymx[(kc, c)].h[kc, :, c * 512:(c + 1) * 512])
                return _View(h)
            emit_outproj(P, C, res, ym, wo1, x4, x5, g1, 3, LT, sel=selt, alt_chunks=LT // 512)
            emit_ffn_pipelined(P, C, res, x5, xo, wf[3][0], wf[3][1], wf[3][2], g1, 4, 5, LT)
            P.barrier()
        P.es = base
        P.finish()
        P.emit()
    return nc


def _colT(cols):
    K = cols.shape[0] // 128
    n = cols.shape[1] // 128
    return np.ascontiguousarray(cols.reshape(K, 128, n, 128).transpose(2, 1, 0, 3))


def _ffn_layout(w_gate, w_up, w_down, sfx):
    return {"wg" + sfx: _colT(w_gate), "wu" + sfx: _colT(w_up),
            "wd" + sfx: np.ascontiguousarray(w_down.reshape(FT, 128, KC, 128).transpose(2, 1, 0, 3))}


def _g_layout(gn):
    return np.ascontiguousarray(gn.reshape(6, KC, 128).transpose(2, 0, 1).reshape(128, 96))


def _fm(a):
    return np.ascontiguousarray(a.T).reshape(a.shape[1] // 128, 128, a.shape[0])


def _emix_inputs(w_in, pool_w, pool_scale, hh):
    heads = range(6 * hh, 6 * hh + 6)
    qc = np.concatenate([w_in[:, 512 + h * 128: 512 + (h + 1) * 128] for h in heads], 1)
    kc = np.concatenate([w_in[:, 512 + 1536 + h * 128: 512 + 1536 + (h + 1) * 128] for h in heads], 1)
    vc = np.concatenate([w_in[:, 512 + 3072 + h * 128: 512 + 3072 + (h + 1) * 128] for h in heads], 1)
    pc = w_in[:, hh * 256:(hh + 1) * 256]
    wv = np.ascontiguousarray(vc.reshape(KC, 128, 3, 256).transpose(2, 1, 0, 3))
    pw = np.ascontiguousarray(pool_w[2 * hh:2 * hh + 2])
    psc = np.ascontiguousarray(pool_scale[hh * 256:(hh + 1) * 256].reshape(2, 128).T)
    selw = np.zeros((128, 2, 21), np.float32)
    for gi in range(2):
        k = 2 * hh + gi
        w = POOL_WINDOWS[k]
        selw[:, gi, k] = 1.0
        selw[:, gi, 4] = 1.0 / w
        selw[:, gi, 5:21] = (1.0 / np.minimum(np.arange(1, 17), w))[None, :]
    cst = np.zeros((128, 768), np.float32)
    cst[:, 0:128] = np.eye(128)
    cst[:, 128:256] = np.tril(np.ones((128, 128)), -1)
    cst[:, 256:768] = 1.0
    return {"wq": _colT(qc), "wk": _colT(kc), "wv": wv, "wp": _colT(pc), "pw": pw, "psc": psc,
            "selw": selw, "cst": cst}


def _ossm_inputs(w_in, a_re, a_im, log_dt, b_re, b_im, c_re, c_im, d, hh):
    G0 = 32 * hh
    wu = w_in[:, G0 * 16:(G0 + 32) * 16]
    wu = np.ascontiguousarray(wu.reshape(KC, 128, NGP, 32).transpose(2, 1, 0, 3))
    prm = np.zeros((128, 3, 16), np.float32)
    bsm = np.zeros((128, 2, 16, 32), np.float32)
    ct = np.zeros((128, 2, 16, 32), np.float32)
    dd = np.zeros((32, 16, 32), np.float32)
    for gp in range(NGP):
        for gl in range(2):
            g = G0 + 2 * gp + gl
            sl = slice(gl * 64, (gl + 1) * 64)
            prm[sl, 0, gp] = a_re[g]
            prm[sl, 1, gp] = a_im[g]
            prm[sl, 2, gp] = log_dt[g]
            bsm[sl, 0, gp, gl * 16:(gl + 1) * 16] = b_re[g]
            bsm[sl, 1, gp, gl * 16:(gl + 1) * 16] = b_im[g]
            ct[sl, 0, gp, gl * 16:(gl + 1) * 16] = c_re[g].T
            ct[sl, 1, gp, gl * 16:(gl + 1) * 16] = c_im[g].T
            idx = np.arange(16)
            dd[gl * 16 + idx, gp, gl * 16 + idx] = d[g * 16:(g + 1) * 16]
    c2 = np.zeros((128, 1152), np.float32)
    c2[:, 0:128] = np.eye(128)
    c2[:, 128:640] = np.arange(1, 513, dtype=np.float32)[None, :]
    c2[:, 640:1152] = 1.0
    return {"wu": wu, "prm": prm, "bsm": bsm, "ct": ct, "dd": dd, "cst2": c2}


def _omix2_inputs(w_in, glu_w, glu_b, sgu_norm_g, sgu_w, sgu_b):
    zu = w_in[:, 1024:2048]
    zv = w_in[:, 2048:3072]
    wzv = np.ascontiguousarray(zv.reshape(KC, 128, 2, 512).transpose(2, 1, 0, 3))
    glw = _colT(glu_w)
    glb = np.ascontiguousarray(glu_b.reshape(8, 128).T)
    sgn = np.ascontiguousarray(np.broadcast_to(sgu_norm_g[None, :], (128, 1024)))
    wsT = np.ascontiguousarray(sgu_w.transpose(2, 0, 1))
    sgb = np.ascontiguousarray(np.broadcast_to(sgu_b[None, :, :], (128, 8, 128)))
    mle = np.triu(np.ones((128, 128), np.float32))
    return {"wzu": _colT(zu), "wzv": wzv, "glw": glw, "glb": glb, "sgn": sgn, "wsT": wsT, "sgb": sgb, "mle": mle}


_PROGS = {}


def _make_inputs(x, norm_g, ffn_w_gate, ffn_w_up, ffn_w_down, ev_w_in, ev_pool_w, ev_pool_scale, ev_w_out,
           od_w_in, od_ssm_a_re, od_ssm_a_im, od_ssm_log_dt, od_ssm_b_re, od_ssm_b_im, od_ssm_c_re,
           od_ssm_c_im, od_ssm_d, od_glu_w, od_glu_b, od_sgu_norm_g, od_sgu_w, od_sgu_b, od_w_out):
    f = lambda a: np.asarray(a, dtype=np.float32)
    x = f(x)
    norm_g, ffn_w_gate, ffn_w_up, ffn_w_down = f(norm_g), f(ffn_w_gate), f(ffn_w_up), f(ffn_w_down)
    B, Lq, Dm = x.shape
    xts = [_fm(x[b]) for b in range(B)]
    shared = {}
    for i, (l, j) in enumerate(((0, 0), (0, 1), (1, 0), (1, 1))):
        shared.update(_ffn_layout(ffn_w_gate[l, j], ffn_w_up[l, j], ffn_w_down[l, j], str(i)))
    shared["g0"] = _g_layout(norm_g[0])
    shared["g1"] = _g_layout(norm_g[1])
    for hh in range(2):
        e_ = _emix_inputs(f(ev_w_in[0]), f(ev_pool_w[0]), f(ev_pool_scale[0]), hh)
        shared["cst"] = e_.pop("cst")
        for k, v in e_.items():
            shared[f"{k}_{hh}"] = v
        o_ = _ossm_inputs(f(od_w_in[0]), f(od_ssm_a_re[0]), f(od_ssm_a_im[0]), f(od_ssm_log_dt[0]),
                          f(od_ssm_b_re[0]), f(od_ssm_b_im[0]), f(od_ssm_c_re[0]), f(od_ssm_c_im[0]),
                          f(od_ssm_d[0]), hh)
        shared["cst2"] = o_.pop("cst2")
        shared[f"swu_{hh}"] = o_.pop("wu")
        for k, v in o_.items():
            shared[f"{k}_{hh}"] = v
    shared["wo0"] = _colT(f(ev_w_out[0]))
    shared["wo1"] = _colT(f(od_w_out[0]))
    shared.update(_omix2_inputs(f(od_w_in[0]), f(od_glu_w[0]), f(od_glu_b[0]), f(od_sgu_norm_g[0]),
                                f(od_sgu_w[0]), f(od_sgu_b[0])))
    ims = []
    for c in range(NACT):
        b, th = c // 2, c % 2
        selh = np.zeros((128, 2), np.float32)
        selh[:, th] = 1.0
        ims.append(dict(shared, xT=xts[b], selh=selh))
    return ims


def kernel(**inputs):
    x = np.asarray(inputs["x"])
    B, Lq, Dm = x.shape
    ims = _make_inputs(**inputs)
    if "F" not in _PROGS:
        _PROGS["F"] = build_FUSED()
    r = run_bass_kernel_spmd(_PROGS["F"], ims, core_ids=list(range(NACT)))
    out = np.empty((B, Lq, Dm), np.float32)
    for c in range(NACT):
        b, th = c // 2, c % 2
        out[b, th * LT:(th + 1) * LT] = r.results[c]["xoT"].reshape(Dm, LT).T
    return out
```

```python
import math
import ml_dtypes
from concourse.bass_utils import run_bass_kernel_spmd
import numpy as np
from contextlib import ExitStack
import concourse.bass as bass
import concourse.mybir as mybir

F32 = mybir.dt.float32
BF16 = mybir.dt.bfloat16
ALU = mybir.AluOpType
AF = mybir.ActivationFunctionType
AX = mybir.AxisListType

ENGS = ["pe", "dve", "act", "pool", "sp"]
NRING = 16
RING_N = {"sp": 24, "act": 16, "pool": 16}


class T:
    def __init__(self, h, name):
        self.h = h
        self.name = name
        self.w = {}
        self.r = {}

    def __getitem__(self, idx):
        return self.h[idx]


class Prog:
    def __init__(self, nc, es):
        self.nc = nc
        self.es = es
        self.q = {e: [] for e in ENGS}
        self.esem = {e: es.enter_context(nc.semaphore(f"s_{e}")) for e in ENGS}
        self.ecnt = {e: 0 for e in ENGS}
        self.seen = {e: {} for e in ENGS}
        self.ring = {}
        self.ringcnt = {}
        self.ringpos = {}
        for qn in ["sp", "act", "pool"]:
            self.ring[qn] = [es.enter_context(nc.semaphore(f"d_{qn}{i}")) for i in range(RING_N[qn])]
            self.ringcnt[qn] = [0] * RING_N[qn]
            self.ringpos[qn] = 0
        self.nuniq = 0
        self.out_events = []

    def sb(self, shape, dt, name=None):
        self.nuniq += 1
        name = f"{name or 't'}_{self.nuniq}"
        h = self.es.enter_context(self.nc.sbuf_tensor(name, list(shape), dt))
        return T(h, name)

    def ps(self, shape, dt=F32, name=None):
        self.nuniq += 1
        name = f"{name or 'p'}_{self.nuniq}"
        h = self.es.enter_context(self.nc.psum_tensor(name, list(shape), dt))
        return T(h, name)

    def dram(self, name, shape, dt, kind):
        h = self.nc.dram_tensor(name, list(shape), dt, kind=kind)
        return T(h.ap() if hasattr(h, "ap") else h, name)

    def _collect(self, eng, reads, writes):
        waits = []
        for t in list(reads) + list(writes):
            for sem, (val, src) in t.w.items():
                waits.append((sem, val, src))
        for t in writes:
            for sem, (val, src) in t.r.items():
                waits.append((sem, val, src))
        need = {}
        seen = self.seen[eng]
        for (sem, val, src) in waits:
            if eng == "pe" and src == "pe":
                continue
            if seen.get(sem, 0) >= val:
                continue
            if need.get(sem, (0,))[0] < val:
                need[sem] = (val,)
        out = []
        for sem, (val,) in need.items():
            seen[sem] = val
            out.append((sem, val))
        return out

    def _commit(self, ev, reads, writes):
        sem, val, src = ev
        for t in reads:
            t.r[sem] = (val, src)
        for t in writes:
            t.w[sem] = (val, src)

    def op(self, eng, fn, reads=(), writes=()):
        waits = self._collect(eng, reads, writes)
        self.ecnt[eng] += 1
        ev = (self.esem[eng], self.ecnt[eng], eng)
        self.q[eng].append(("op", waits, [fn], (self.esem[eng], 1)))
        self._commit(ev, reads, writes)
        return ev

    def group(self, eng, fns, reads=(), writes=()):
        waits = self._collect(eng, reads, writes)
        self.ecnt[eng] += 1
        ev = (self.esem[eng], self.ecnt[eng], eng)
        self.q[eng].append(("op", waits, list(fns), (self.esem[eng], 1)))
        self._commit(ev, reads, writes)
        return ev

    def dma(self, qn, out_t, out_ap, in_t, in_ap, is_output=False, **kw):
        reads = [in_t]
        writes = [out_t]
        waits = self._collect(qn, reads, writes)
        pos = self.ringpos[qn]
        self.ringpos[qn] = (pos + 1) % RING_N[qn]
        sem = self.ring[qn][pos]
        prev = self.ringcnt[qn][pos]
        if prev > 0 and self.seen[qn].get(sem, 0) < prev:
            waits.append((sem, prev))
            self.seen[qn][sem] = prev
        self.ringcnt[qn][pos] = prev + 16
        ev = (sem, prev + 16, "dma")

        def fn(e, out_ap=out_ap, in_ap=in_ap, kw=kw):
            return e.dma_start(out=out_ap, in_=in_ap, **kw)

        self.q[qn].append(("op", waits, [fn], (sem, 16)))
        self._commit(ev, reads, writes)
        if is_output:
            self.out_events.append(ev)
        return ev

    def collective(self, kind, out_t, out_ap, in_t, in_ap, groups):
        qn = "pool"
        waits = self._collect(qn, [in_t], [out_t])
        pos = self.ringpos[qn]
        self.ringpos[qn] = (pos + 1) % RING_N[qn]
        sem = self.ring[qn][pos]
        prev = self.ringcnt[qn][pos]
        if prev > 0 and self.seen[qn].get(sem, 0) < prev:
            waits.append((sem, prev))
            self.seen[qn][sem] = prev
        self.ringcnt[qn][pos] = prev + 16
        ev = (sem, prev + 16, "dma")

        def fn(e):
            return e.collective_compute(kind, ALU.bypass, replica_groups=groups, ins=[in_ap], outs=[out_ap])

        self.q[qn].append(("op", waits, [fn], (sem, 16)))
        self._commit(ev, [in_t], [out_t])
        return ev

    def barrier(self):
        targets = [(self.esem[e], self.ecnt[e]) for e in ENGS if self.ecnt[e] > 0]
        for qn in self.ring:
            for i in range(RING_N[qn]):
                if self.ringcnt[qn][i] > 0:
                    targets.append((self.ring[qn][i], self.ringcnt[qn][i]))
        for e in ENGS:
            waits = [(s, v) for (s, v) in targets if self.seen[e].get(s, 0) < v]
            for s, v in waits:
                self.seen[e][s] = v
            self.q[e].append(("wait", waits, [], None))

    def finish(self):
        need = {}
        for (sem, val, _) in self.out_events:
            need[sem] = max(need.get(sem, 0), val)
        self.q["sp"].append(("wait", list(need.items()), [], None))

    def emit(self):
        nc = self.nc
        eobj = {"pe": "tensor", "dve": "vector", "act": "scalar", "pool": "gpsimd", "sp": "sync"}
        with nc.Block() as block:
            for en in ENGS:
                items = self.q[en]

                def body(e, items=items):
                    for (_, waits, fns, inc) in items:
                        for (sem, val) in waits:
                            e.wait_ge(sem, val)
                        last = None
                        for f in fns:
                            last = f(e)
                        if inc is not None:
                            last.then_inc(inc[0], inc[1])

                getattr(block, eobj[en])(body)


D = 2048
KC = 16
FF = 5632
FT = 44
EPS = 1e-6


class Common:
    def __init__(self, P):
        self.P = P
        self.ones = P.sb([128, 128], BF16, "ones")
        self.eps = P.sb([128, 1], F32, "eps")
        P.op("pool", lambda e: e.memset(self.ones[:], 1.0), writes=[self.ones])
        P.op("pool", lambda e: e.memset(self.eps[:], EPS), writes=[self.eps])
        self.banks = [P.ps([128, 512], F32, f"bank{i}") for i in range(8)]
        self.xt = [P.sb([128, 512], F32, "xt") for _ in range(4)]
        self.ft = [P.sb([128, 512], F32, "ftile") for _ in range(3)]
        self.sq = [P.sb([128, 512], BF16, "sq") for _ in range(4)]
        self.tmp = [P.sb([128, 512], F32, "tmp") for _ in range(2)]
        self.rstd = [P.sb([128, 512], F32, "rstd") for _ in range(2)]
        self.rstdA = [P.sb([128, 512], F32, "rstdA") for _ in range(2)]
        self.sqS = [P.sb([128, 512], BF16, "sqS") for _ in range(6)]
        self.hb = [P.sb([128, 512], BF16, "hb") for _ in range(3)]
        self.cnt = {}

    def rot(self, lst, key):
        i = self.cnt.get(key, 0)
        self.cnt[key] = i + 1
        return lst[i % len(lst)]


def emit_rstd(P, C, stats_bank, rstd_t, n=D):
    tmp = C.rot(C.tmp, "tmp")
    P.op("act", lambda e: e.activation(out=tmp[:], in_=stats_bank[:], func=AF.Sqrt,
                                       bias=C.eps[:, 0:1], scale=1.0 / n),
         reads=[stats_bank, C.eps], writes=[tmp])
    P.op("dve", lambda e: e.reciprocal(out=rstd_t[:], in_=tmp[:]), reads=[tmp], writes=[rstd_t])


def emit_ffn(P, C, res, x_in, x_out, wg, wu, wd, g, jpre, jpost, ntok, h_out=None, jnext=None):
    h = res["h"]
    act = res["act"]
    wgs, wus, wds = res["wgs"], res["wus"], res["wds"]
    f_scr = res["f_scr"]
    bk = C.banks
    nblk = ntok // 1024
    for blk in range(nblk):
        for half in range(2):
            hg = blk * 2 + half
            tsl = slice(hg * 512, (hg + 1) * 512)
            hsl = slice(half * 512, (half + 1) * 512)
            st = bk[6 + half]
            for kc in range(KC):
                xt = C.rot(C.xt, "xt")
                P.dma("sp", xt, xt[:], x_in[(kc, hg)], x_in[(kc, hg)].h[kc, :, tsl])
                sq = C.rot(C.sq, "sq")
                P.op("act", lambda e, sq=sq, xt=xt: e.activation(out=sq[:], in_=xt[:], func=AF.Square),
                     reads=[xt], writes=[sq])
                P.op("pe", lambda e, sq=sq, kc=kc, st=st: e.matmul(st[:], lhsT=C.ones[:], rhs=sq[:],
                                                                     start=(kc == 0), stop=(kc == KC - 1)),
                     reads=[sq, C.ones], writes=[st])
            rstd = C.rstd[half]
            emit_rstd(P, C, st, rstd)
            for kc in range(KC):
                xt = C.rot(C.xt, "xt")
                P.dma("sp", xt, xt[:], x_in[(kc, hg)], x_in[(kc, hg)].h[kc, :, tsl])
                P.op("dve", lambda e, xt=xt, kc=kc, rstd=rstd, hsl=hsl: e.scalar_tensor_tensor(
                    out=h[:, kc, hsl], in0=xt[:], scalar=g[:, jpre * 16 + kc: jpre * 16 + kc + 1],
                    in1=rstd[:], op0=ALU.mult, op1=ALU.mult),
                    reads=[xt, rstd, g], writes=[h])
        for ft in range(FT):
            wgt = C.rot(wgs, "wg")
            wut = C.rot(wus, "wu")
            P.dma("pool", wgt, wgt[:], wg, wg.h[ft])
            P.dma("pool", wut, wut[:], wu, wu.h[ft])
            for half in range(2):
                hsl = slice(half * 512, (half + 1) * 512)
                pg = bk[0 + half]
                pu = bk[2 + half]
                P.group("pe", [
                    (lambda e, kc=kc, pg=pg, wgt=wgt, hsl=hsl: e.matmul(
                        pg[:], lhsT=wgt[:, kc, :], rhs=h[:, kc, hsl], start=(kc == 0), stop=(kc == KC - 1)))
                    for kc in range(KC)], reads=[wgt, h], writes=[pg])
                P.group("pe", [
                    (lambda e, kc=kc, pu=pu, wut=wut, hsl=hsl: e.matmul(
                        pu[:], lhsT=wut[:, kc, :], rhs=h[:, kc, hsl], start=(kc == 0), stop=(kc == KC - 1)))
                    for kc in range(KC)], reads=[wut, h], writes=[pu])
                sl = C.rot(C.hb, "hb")
                P.op("act", lambda e, sl=sl, pg=pg: e.activation(out=sl[:], in_=pg[:], func=AF.Silu),
                     reads=[pg], writes=[sl])
                P.op("dve", lambda e, sl=sl, pu=pu, ft=ft, hsl=hsl: e.tensor_tensor(
                    out=act[:, ft, hsl], in0=pu[:], in1=sl[:], op=ALU.mult),
                    reads=[pu, sl], writes=[act])
        pend = None
        for m in range(KC):
            wdt = C.rot(wds, "wd")
            for q in range(4):
                P.dma("pool", wdt, wdt[:, q * 11:(q + 1) * 11, :], wd, wd.h[m, :, q * 11:(q + 1) * 11, :])
            for half in range(2):
                hg = blk * 2 + half
                tsl = slice(hg * 512, (hg + 1) * 512)
                hsl = slice(half * 512, (half + 1) * 512)
                pd = bk[4 + half]
                st = bk[6 + half]
                P.group("pe", [
                    (lambda e, fc=fc, pd=pd, wdt=wdt, hsl=hsl: e.matmul(
                        pd[:], lhsT=wdt[:, fc, :], rhs=act[:, fc, hsl], start=(fc == 0), stop=(fc == FT - 1)))
                    for fc in range(FT)], reads=[wdt, act], writes=[pd])
                ftile = C.rot(C.ft, "ft")
                P.op("act", lambda e, ftile=ftile, pd=pd: e.copy(out=ftile[:], in_=pd[:]),
                     reads=[pd], writes=[ftile])
                sq = C.rot(C.sq, "sq")
                P.op("act", lambda e, sq=sq, ftile=ftile: e.activation(out=sq[:], in_=ftile[:], func=AF.Square),
                     reads=[ftile], writes=[sq])
                if pend is not None:
                    pend()
                pend = (lambda sq=sq, m=m, st=st: P.op("pe", lambda e: e.matmul(
                    st[:], lhsT=C.ones[:], rhs=sq[:], start=(m == 0), stop=(m == KC - 1)),
                    reads=[sq, C.ones], writes=[st]))
                P.dma("act", f_scr[(m, hg)], f_scr[(m, hg)].h[m, :, tsl], ftile, ftile[:])
        if pend is not None:
            pend()
        for half in range(2):
            hg = blk * 2 + half
            tsl = slice(hg * 512, (hg + 1) * 512)
            st = bk[6 + half]
            rstd = C.rstd[half]
            emit_rstd(P, C, st, rstd)
            st2 = bk[4 + half]
            for kc in range(KC):
                ftile = C.rot(C.ft, "ft")
                xt = C.rot(C.xt, "xt")
                P.dma("sp", ftile, ftile[:], f_scr[(kc, hg)], f_scr[(kc, hg)].h[kc, :, tsl])
                P.dma("sp", xt, xt[:], x_in[(kc, hg)], x_in[(kc, hg)].h[kc, :, tsl])
                P.op("dve", lambda e, ftile=ftile, kc=kc, rstd=rstd: e.scalar_tensor_tensor(
                    out=ftile[:], in0=ftile[:], scalar=g[:, jpost * 16 + kc: jpost * 16 + kc + 1],
                    in1=rstd[:], op0=ALU.mult, op1=ALU.mult),
                    reads=[rstd, g], writes=[ftile])
                P.op("dve", lambda e, ftile=ftile, xt=xt: e.scalar_tensor_tensor(
                    out=xt[:], in0=ftile[:], scalar=0.5, in1=xt[:], op0=ALU.mult, op1=ALU.add),
                    reads=[ftile], writes=[xt])
                P.dma("act", x_out[(kc, hg)], x_out[(kc, hg)].h[kc, :, tsl], xt, xt[:],
                      is_output=x_out.get("is_output", False))
                if h_out is not None:
                    sq = C.rot(C.sq, "sq")
                    P.op("act", lambda e, sq=sq, xt=xt: e.activation(out=sq[:], in_=xt[:], func=AF.Square),
                         reads=[xt], writes=[sq])
                    P.op("pe", lambda e, sq=sq, kc=kc, st2=st2: e.matmul(
                        st2[:], lhsT=C.ones[:], rhs=sq[:], start=(kc == 0), stop=(kc == KC - 1)),
                        reads=[sq, C.ones], writes=[st2])
            if h_out is not None:
                emit_rstd(P, C, st2, rstd)
                for kc in range(KC):
                    xt = C.rot(C.xt, "xt")
                    P.dma("sp", xt, xt[:], x_out[(kc, hg)], x_out[(kc, hg)].h[kc, :, tsl])
                    hb = C.rot(C.hb, "hb")
                    P.op("dve", lambda e, xt=xt, kc=kc, rstd=rstd, hb=hb: e.scalar_tensor_tensor(
                        out=hb[:], in0=xt[:], scalar=g[:, jnext * 16 + kc: jnext * 16 + kc + 1],
                        in1=rstd[:], op0=ALU.mult, op1=ALU.mult),
                        reads=[xt, rstd, g], writes=[hb])
                    P.dma("act", h_out[(kc, hg)], h_out[(kc, hg)].h[kc, :, tsl], hb, hb[:],
                          is_output=h_out.get("is_output", False))


def regions(P, name, shape, dt, kind, nk, nh):
    base = P.dram(name, shape, dt, kind)
    d = {}
    for k in range(nk):
        for hh in range(nh):
            d[(k, hh)] = T(base.h, f"{name}_{k}_{hh}")
    d["is_output"] = (kind == "ExternalOutput")
    d["base"] = base
    return d


def ffn_resources(P, ntok, scr_name="f_scr", f_scr=None):
    res = {}
    res["h"] = P.sb([128, KC, 1024], BF16, "h")
    res["hAB"] = [T(res["h"].h, "hA"), T(res["h"].h, "hB")]
    res["act"] = P.sb([128, FT, 1024], BF16, "act")
    res["wgs"] = [P.sb([128, KC, 128], BF16, "wg") for _ in range(3)]
    res["wus"] = [P.sb([128, KC, 128], BF16, "wu") for _ in range(3)]
    res["wds"] = [P.sb([128, FT, 128], BF16, "wd") for _ in range(2)]
    res["f_scr"] = f_scr if f_scr is not None else regions(P, scr_name, [KC, 128, ntok], F32, "Internal", KC, ntok // 512)
    return res


def emit_outproj(P, C, res, ym, wo, x_in, x_out, g, jpost, ntok, sel=None, alt_chunks=0):
    f_scr = res["f_scr"]
    wgs = res["wgs"]
    bk = C.banks
    nch = ntok // 512

    def main(c, yt):
        tsl = slice(c * 512, (c + 1) * 512)
        st = bk[6 + (c % 2)]
        pend = None
        for m in range(KC):
            wt = C.rot(wgs, "wg")
            P.dma("pool", wt, wt[:], wo, wo.h[m])
            pd = bk[4 + (m % 2)]
            P.group("pe", [(lambda e, kc=kc, pd=pd, wt=wt, yt=yt: e.matmul(
                pd[:], lhsT=wt[:, kc, :], rhs=yt[:, kc, :], start=(kc == 0), stop=(kc == KC - 1)))
                for kc in range(KC)], reads=[wt, yt], writes=[pd])
            ftile = C.rot(C.ft, "ft")
            P.op("act", lambda e, ftile=ftile, pd=pd: e.copy(out=ftile[:], in_=pd[:]), reads=[pd], writes=[ftile])
            sq = C.rot(C.sq, "sq")
            P.op("act", lambda e, sq=sq, ftile=ftile: e.activation(out=sq[:], in_=ftile[:], func=AF.Square),
                 reads=[ftile], writes=[sq])
            if pend is not None:
                pend()
            pend = (lambda sq=sq, m=m, st=st: P.op("pe", lambda e: e.matmul(
                st[:], lhsT=C.ones[:], rhs=sq[:], start=(m == 0), stop=(m == KC - 1)),
                reads=[sq, C.ones], writes=[st]))
            P.dma("act", f_scr[(m, c)], f_scr[(m, c)].h[m, :, tsl], ftile, ftile[:])
            yield None
        pend()

    def side(c):
        tsl = slice(c * 512, (c + 1) * 512)
        st = bk[6 + (c % 2)]
        rstd = C.rstd[c % 2]
        emit_rstd(P, C, st, rstd)
        for kc in range(KC):
            ftile = C.rot(C.ft, "ft")
            xt = C.rot(C.xt, "xt")
            P.dma("sp", ftile, ftile[:], f_scr[(kc, c)], f_scr[(kc, c)].h[kc, :, tsl])
            P.dma("sp", xt, xt[:], x_in[(kc, c)], x_in[(kc, c)].h[kc, :, tsl])
            if sel is not None:
                c2 = c + alt_chunks
                xb = C.rot(C.xt, "xt")
                P.dma("sp", xb, xb[:], x_in[(kc, c2)], x_in[(kc, c2)].h[kc, :, c2 * 512:(c2 + 1) * 512])
                P.op("pool", lambda e, xt=xt: e.tensor_scalar(out=xt[:], in0=xt[:], scalar1=sel[:, 0:1], scalar2=None,
                                                              op0=ALU.mult), reads=[sel], writes=[xt])
                P.op("dve", lambda e, xt=xt, xb=xb: e.scalar_tensor_tensor(
                    out=xt[:], in0=xb[:], scalar=sel[:, 1:2], in1=xt[:], op0=ALU.mult, op1=ALU.add),
                    reads=[xb, sel], writes=[xt])
            P.op("dve", lambda e, ftile=ftile, kc=kc, rstd=rstd: e.scalar_tensor_tensor(
                out=ftile[:], in0=ftile[:], scalar=g[:, jpost * 16 + kc: jpost * 16 + kc + 1],
                in1=rstd[:], op0=ALU.mult, op1=ALU.mult), reads=[rstd, g], writes=[ftile])
            P.op("dve", lambda e, ftile=ftile, xt=xt: e.tensor_tensor(
                out=xt[:], in0=ftile[:], in1=xt[:], op=ALU.add), reads=[ftile], writes=[xt])
            P.dma("act", x_out[(kc, c)], x_out[(kc, c)].h[kc, :, tsl], xt, xt[:],
                  is_output=x_out.get("is_output", False))
            yield None

    yts = {0: ym(0)}
    for c in range(nch):
        if c + 1 < nch:
            yts[c + 1] = ym(c + 1)
        sd = side(c - 1) if c > 0 else None
        for _ in main(c, yts[c]):
            if sd is not None:
                try:
                    next(sd)
                    next(sd)
                except StopIteration:
                    sd = None
        if sd is not None:
            for _ in sd:
                pass
    for _ in side(nch - 1):
        pass
    P.barrier()


FLUSH = "FLUSH"


def emit_ffn_pipelined(P, C, res, x_in, x_out, wg, wu, wd, g, jpre, jpost, ntok, h_out=None, jnext=None):
    h = res["h"]
    act = res["act"]
    wgs, wus, wds = res["wgs"], res["wus"], res["wds"]
    f_scr = res["f_scr"]
    bk = C.banks
    nblk = ntok // 1024
    rstdA = C.rstdA

    def stats_mm(st, sq, first, last):
        return lambda: P.op("pe", lambda e: e.matmul(st[:], lhsT=C.ones[:], rhs=sq[:], start=first, stop=last),
                            reads=[sq, C.ones], writes=[st])

    def stageA(blk):
        for half in range(2):
            hg = blk * 2 + half
            tsl = slice(hg * 512, (hg + 1) * 512)
            hsl = slice(half * 512, (half + 1) * 512)
            st = bk[0 + half]
            for kc in range(KC):
                xt = C.rot(C.xt, "xt")
                P.dma("sp", xt, xt[:], x_in[(kc, hg)], x_in[(kc, hg)].h[kc, :, tsl])
                sq = C.rot(C.sqS, "sqS")
                P.op("act", lambda e, sq=sq, xt=xt: e.activation(out=sq[:], in_=xt[:], func=AF.Square),
                     reads=[xt], writes=[sq])
                yield stats_mm(st, sq, kc == 0, kc == KC - 1)
            yield FLUSH
            rstd = rstdA[half]
            emit_rstd(P, C, st, rstd)
            for kc in range(KC):
                xt = C.rot(C.xt, "xt")
                P.dma("sp", xt, xt[:], x_in[(kc, hg)], x_in[(kc, hg)].h[kc, :, tsl])
                P.op("dve", lambda e, xt=xt, kc=kc, rstd=rstd, hsl=hsl: e.scalar_tensor_tensor(
                    out=h[:, kc, hsl], in0=xt[:], scalar=g[:, jpre * 16 + kc: jpre * 16 + kc + 1],
                    in1=rstd[:], op0=ALU.mult, op1=ALU.mult),
                    reads=[xt, rstd, g], writes=[h])
                yield None

    def stageB(blk):
        for ft in range(FT):
            wgt = C.rot(wgs, "wg")
            wut = C.rot(wus, "wu")
            P.dma("pool", wgt, wgt[:], wg, wg.h[ft])
            P.dma("pool", wut, wut[:], wu, wu.h[ft])
            for half in range(2):
                hsl = slice(half * 512, (half + 1) * 512)
                pg = bk[0 + half]
                pu = bk[2 + half]
                P.group("pe", [
                    (lambda e, kc=kc, pg=pg, wgt=wgt, hsl=hsl: e.matmul(
                        pg[:], lhsT=wgt[:, kc, :], rhs=h[:, kc, hsl], start=(kc == 0), stop=(kc == KC - 1)))
                    for kc in range(KC)], reads=[wgt, h], writes=[pg])
                P.group("pe", [
                    (lambda e, kc=kc, pu=pu, wut=wut, hsl=hsl: e.matmul(
                        pu[:], lhsT=wut[:, kc, :], rhs=h[:, kc, hsl], start=(kc == 0), stop=(kc == KC - 1)))
                    for kc in range(KC)], reads=[wut, h], writes=[pu])
                sl = C.rot(C.hb, "hb")
                P.op("act", lambda e, sl=sl, pg=pg: e.activation(out=sl[:], in_=pg[:], func=AF.Silu),
                     reads=[pg], writes=[sl])
                P.op("dve", lambda e, sl=sl, pu=pu, ft=ft, hsl=hsl: e.tensor_tensor(
                    out=act[:, ft, hsl], in0=pu[:], in1=sl[:], op=ALU.mult),
                    reads=[pu, sl], writes=[act])
            yield None

    def stageC(blk):
        pend = None
        for m in range(KC):
            wdt = C.rot(wds, "wd")
            for q in range(4):
                P.dma("pool", wdt, wdt[:, q * 11:(q + 1) * 11, :], wd, wd.h[m, :, q * 11:(q + 1) * 11, :])
            for half in range(2):
                hg = blk * 2 + half
                tsl = slice(hg * 512, (hg + 1) * 512)
                hsl = slice(half * 512, (half + 1) * 512)
                pd = bk[4 + half]
                st = bk[6 + half]
                P.group("pe", [
                    (lambda e, fc=fc, pd=pd, wdt=wdt, hsl=hsl: e.matmul(
                        pd[:], lhsT=wdt[:, fc, :], rhs=act[:, fc, hsl], start=(fc == 0), stop=(fc == FT - 1)))
                    for fc in range(FT)], reads=[wdt, act], writes=[pd])
                ftile = C.rot(C.ft, "ft")
                P.op("act", lambda e, ftile=ftile, pd=pd: e.copy(out=ftile[:], in_=pd[:]),
                     reads=[pd], writes=[ftile])
                sq = C.rot(C.sq, "sq")
                P.op("act", lambda e, sq=sq, ftile=ftile: e.activation(out=sq[:], in_=ftile[:], func=AF.Square),
                     reads=[ftile], writes=[sq])
                if pend is not None:
                    pend()
                pend = stats_mm(st, sq, m == 0, m == KC - 1)
                P.dma("act", f_scr[(m, hg)], f_scr[(m, hg)].h[m, :, tsl], ftile, ftile[:])
                yield None
        pend()
        yield None

    def stageDE(blk):
        for half in range(2):
            hg = blk * 2 + half
            tsl = slice(hg * 512, (hg + 1) * 512)
            st = bk[6 + half]
            rstd = C.rstd[half]
            emit_rstd(P, C, st, rstd)
            st2 = bk[4 + half]
            for kc in range(KC):
                ftile = C.rot(C.ft, "ft")
                xt = C.rot(C.xt, "xt")
                P.dma("sp", ftile, ftile[:], f_scr[(kc, hg)], f_scr[(kc, hg)].h[kc, :, tsl])
                P.dma("sp", xt, xt[:], x_in[(kc, hg)], x_in[(kc, hg)].h[kc, :, tsl])
                P.op("dve", lambda e, ftile=ftile, kc=kc, rstd=rstd: e.scalar_tensor_tensor(
                    out=ftile[:], in0=ftile[:], scalar=g[:, jpost * 16 + kc: jpost * 16 + kc + 1],
                    in1=rstd[:], op0=ALU.mult, op1=ALU.mult),
                    reads=[rstd, g], writes=[ftile])
                P.op("dve", lambda e, ftile=ftile, xt=xt: e.scalar_tensor_tensor(
                    out=xt[:], in0=ftile[:], scalar=0.5, in1=xt[:], op0=ALU.mult, op1=ALU.add),
                    reads=[ftile], writes=[xt])
                P.dma("act", x_out[(kc, hg)], x_out[(kc, hg)].h[kc, :, tsl], xt, xt[:],
                      is_output=x_out.get("is_output", False))
                if h_out is not None:
                    sq = C.rot(C.sqS, "sqS")
                    P.op("act", lambda e, sq=sq, xt=xt: e.activation(out=sq[:], in_=xt[:], func=AF.Square),
                         reads=[xt], writes=[sq])
                    yield stats_mm(st2, sq, kc == 0, kc == KC - 1)
                else:
                    yield None
            if h_out is not None:
                yield FLUSH
                emit_rstd(P, C, st2, rstd)
                for kc in range(KC):
                    xt = C.rot(C.xt, "xt")
                    P.dma("sp", xt, xt[:], x_out[(kc, hg)], x_out[(kc, hg)].h[kc, :, tsl])
                    hb = C.rot(C.hb, "hb")
                    P.op("dve", lambda e, xt=xt, kc=kc, rstd=rstd, hb=hb: e.scalar_tensor_tensor(
                        out=hb[:], in0=xt[:], scalar=g[:, jnext * 16 + kc: jnext * 16 + kc + 1],
                        in1=rstd[:], op0=ALU.mult, op1=ALU.mult),
                        reads=[xt, rstd, g], writes=[hb])
                    P.dma("act", h_out[(kc, hg)], h_out[(kc, hg)].h[kc, :, tsl], hb, hb[:],
                          is_output=h_out.get("is_output", False))
                    yield None

    def run(main, side, per_iter):
        deferred = []
        side_done = side is None

        def side_step():
            nonlocal side_done
            try:
                r = next(side)
            except StopIteration:
                side_done = True
                return
            if r is FLUSH:
                for t in deferred:
                    t()
                deferred.clear()
            elif r is not None:
                deferred.append(r)

        if main is None:
            while not side_done:
                side_step()
                for t in deferred:
                    t()
                deferred.clear()
            return
        for _ in main:
            for t in deferred:
                t()
            deferred.clear()
            if not side_done:
                for _ in range(per_iter):
                    if side_done:
                        break
                    side_step()
        while not side_done:
            side_step()
            for t in deferred:
                t()
            deferred.clear()
        for t in deferred:
            t()
        deferred.clear()

    run(None, stageA(0), 0)
    for blk in range(nblk):
        run(stageB(blk), stageDE(blk - 1) if blk > 0 else None, 2)
        run(stageC(blk), stageA(blk + 1) if blk + 1 < nblk else None, 2)
    run(None, stageDE(nblk - 1), 0)


L = 4096
KC = 16
SCALE = 128 ** -0.5
POOL_WINDOWS = (2, 4, 8, 16)


def emit_proj_fm(P, psb, hT, wt, outs, evac):
    pass


def emit_emix(P, hT, wq, wk, wv, wp, pw_d, ps_d, cst_d, ymT, selw_d, tile_of=lambda j: j, is_output=True):
    nc = P.nc
    base_es = P.es
    cst = P.sb([128, 768], F32, "cst")
    P.dma("sp", cst, cst[:], cst_d, cst_d.h[:, :])
    ident_bf = P.sb([128, 128], BF16, "identbf")
    mask_f = cst
    mask_bf = P.sb([128, 128], BF16, "maskbf")
    P.op("dve", lambda e: e.tensor_copy(out=ident_bf[:], in_=cst[:, 0:128]), reads=[cst], writes=[ident_bf])
    P.op("dve", lambda e: e.tensor_copy(out=mask_bf[:], in_=cst[:, 128:256]), reads=[cst], writes=[mask_bf])
    one_c = P.sb([128, 1], F32, "onec")
    P.op("pool", lambda e: e.memset(one_c[:], 1.0), writes=[one_c])
    pscale = P.sb([128, 2], F32, "pscale")
    P.dma("sp", pscale, pscale[:], ps_d, ps_d.h[:, :])
    selw = P.sb([128, 2, 21], F32, "selw")
    P.dma("sp", selw, selw[:], selw_d, selw_d.h[:, :, :])
    hch = [P.sb([128, KC, 256], BF16, "hch") for _ in range(2)]
    pbank = [P.ps([128, 512], F32, "pbank") for _ in range(2)]
    cnt = {}

    def rot(lst, key):
        i = cnt.get(key, 0)
        cnt[key] = i + 1
        return lst[i % len(lst)]

    def load_h(c):
        t = rot(hch, "hch")
        for kq in range(4):
            P.dma("sp", t, t[:, kq * 4:(kq + 1) * 4, :], hT[(0, c)],
                  hT[(0, c)].h[kq * 4:(kq + 1) * 4, :, c * 256:(c + 1) * 256].rearrange("k p t -> p k t"))
        return t

    with ExitStack() as es:
        P.es = es
        wps = [P.sb([128, KC, 128], BF16, "wp") for _ in range(2)]
        pws = [P.sb([128, 128], BF16, "pw") for _ in range(2)]
        for gi in range(2):
            P.dma("pool", wps[gi], wps[gi][:], wp, wp.h[gi])
            P.dma("pool", pws[gi], pws[gi][:], pw_d, pw_d.h[gi])
        u = [P.sb([128, 16 + L], F32, "upool") for _ in range(2)]
        sa = P.sb([128, 16 + L], F32, "sa")
        sbb = P.sb([128, 16 + L], F32, "sbb")
        pooled = P.sb([128, L], BF16, "pooled")
        yp = [P.sb([128, 512], BF16, "yp") for _ in range(2)]
        for t in u + [sa, sbb]:
            P.op("pool", lambda e, t=t: e.memset(t[:, 0:16], 0.0), writes=[t])
        for c in range(16):
            ht = load_h(c)
            for gi in range(2):
                bk = rot(pbank, "pb")
                P.group("pe", [(lambda e, kc=kc, bk=bk, gi=gi, ht=ht: e.matmul(
                    bk[:, 0:256], lhsT=wps[gi][:, kc, :], rhs=ht[:, kc, :], start=(kc == 0), stop=(kc == KC - 1)))
                    for kc in range(KC)], reads=[wps[gi], ht], writes=[bk])
                P.op("act", lambda e, bk=bk, gi=gi, c=c: e.copy(out=u[gi][:, 16 + c * 256:16 + (c + 1) * 256],
                                                                 in_=bk[:, 0:256]), reads=[bk], writes=[u[gi]])
        sc = P.sb([128, L], F32, "sc")
        acc = P.sb([128, L], F32, "acc")
        for gi in range(2):
            src = u[gi]
            bufs = [sa, sbb]
            for k in range(4):
                sh = 1 << k
                dst = bufs[k % 2]
                P.op("dve", lambda e, dst=dst, src=src, sh=sh: e.tensor_tensor(
                    out=dst[:, 16:16 + L], in0=src[:, 16:16 + L], in1=src[:, 16 - sh:16 + L - sh], op=ALU.add),
                    reads=[src], writes=[dst])
                src = dst
                if k == 0:
                    P.op("dve", lambda e, dst=dst, gi=gi: e.tensor_scalar(
                        out=acc[:], in0=dst[:, 16:16 + L], scalar1=selw[:, gi, 0:1], scalar2=None, op0=ALU.mult),
                        reads=[dst, selw], writes=[acc])
                else:
                    P.op("dve", lambda e, dst=dst, gi=gi, k=k: e.scalar_tensor_tensor(
                        out=acc[:], in0=dst[:, 16:16 + L], scalar=selw[:, gi, k:k + 1], in1=acc[:],
                        op0=ALU.mult, op1=ALU.add), reads=[dst, selw], writes=[acc])
            P.op("dve", lambda e, gi=gi: e.tensor_scalar(
                out=sc[:], in0=acc[:], scalar1=selw[:, gi, 4:5], scalar2=None, op0=ALU.mult),
                reads=[acc, selw], writes=[sc])
            P.op("dve", lambda e, gi=gi: e.tensor_tensor(
                out=sc[:, 0:16], in0=acc[:, 0:16], in1=selw[:, gi, 5:21], op=ALU.mult),
                reads=[acc, selw], writes=[sc])
            P.op("dve", lambda e, gi=gi: e.tensor_tensor(
                out=pooled[:], in0=sc[:], in1=u[gi][:, 16:16 + L], op=ALU.subtract),
                reads=[sc, u[gi]], writes=[pooled])
            for c in range(8):
                bk = rot(pbank, "pb")
                P.op("pe", lambda e, bk=bk, gi=gi, c=c: e.matmul(
                    bk[:], lhsT=pws[gi][:], rhs=pooled[:, c * 512:(c + 1) * 512], start=True, stop=True),
                    reads=[pws[gi], pooled], writes=[bk])
                y = rot(yp, "yp")
                P.op("act", lambda e, y=y, bk=bk, gi=gi: e.activation(
                    out=y[:], in_=bk[:], func=AF.Copy, scale=pscale[:, gi:gi + 1]),
                    reads=[bk, pscale], writes=[y])
                P.dma("act", ymT[(tile_of(gi), c)], ymT[(tile_of(gi), c)].h[tile_of(gi), :, c * 512:(c + 1) * 512], y, y[:], is_output=is_output)
        P.barrier()
    for hp in range(3):
        with ExitStack() as es:
            P.es = es
            wqs = [P.sb([128, KC, 128], BF16, "wq") for _ in range(2)]
            wks = [P.sb([128, KC, 128], BF16, "wk") for _ in range(2)]
            wvs = P.sb([128, KC, 256], BF16, "wv")
            for j in range(2):
                P.dma("pool", wqs[j], wqs[j][:], wq, wq.h[hp * 2 + j])
                P.dma("pool", wks[j], wks[j][:], wk, wk.h[hp * 2 + j])
            P.dma("pool", wvs, wvs[:], wv, wv.h[hp])
            qT = [P.sb([128, L], BF16, "qT") for _ in range(2)]
            kT = [P.sb([128, L], BF16, "kT") for _ in range(2)]
            v = P.sb([128, 32, 256], BF16, "v")
            for c in range(16):
                ht = load_h(c)
                for j in range(2):
                    for (ws, dst) in ((wqs[j], qT[j]), (wks[j], kT[j])):
                        bk = rot(pbank, "pb")
                        P.group("pe", [(lambda e, kc=kc, bk=bk, ws=ws, ht=ht: e.matmul(
                            bk[:, 0:256], lhsT=ws[:, kc, :], rhs=ht[:, kc, :], start=(kc == 0), stop=(kc == KC - 1)))
                            for kc in range(KC)], reads=[ws, ht], writes=[bk])
                        P.op("act", lambda e, bk=bk, dst=dst, c=c: e.copy(
                            out=dst[:, c * 256:(c + 1) * 256], in_=bk[:, 0:256]), reads=[bk], writes=[dst])
                for tb in range(2):
                    bk = rot(pbank, "pb")
                    P.group("pe", [(lambda e, kc=kc, bk=bk, tb=tb, ht=ht: e.matmul(
                        bk[:, 0:256], lhsT=ht[:, kc, tb * 128:(tb + 1) * 128], rhs=wvs[:, kc, :],
                        start=(kc == 0), stop=(kc == KC - 1))) for kc in range(KC)],
                        reads=[wvs, ht], writes=[bk])
                    P.op("dve", lambda e, bk=bk, c=c, tb=tb: e.tensor_copy(
                        out=v[:, c * 2 + tb, :], in_=bk[:, 0:256]), reads=[bk], writes=[v])
            Pps = [P.sb([128, L + 1], F32, "Pp") for _ in range(2)]
            for Pp in Pps:
                P.op("pool", lambda e, Pp=Pp: e.memset(Pp[:, 0:1], 0.0), writes=[Pp])
            Eb = [P.sb([128, L], F32, "E") for _ in range(2)]
            wb = [P.sb([128, L], BF16, "w") for _ in range(2)]
            et = [P.sb([128, 512], F32, "et") for _ in range(4)]
            lt = [P.sb([128, 512], F32, "lt") for _ in range(4)]
            negT = [P.sb([128, 1], F32, "negT") for _ in range(2)]
            wTt = [P.sb([128, 512], BF16, "wT") for _ in range(3)]
            yT = [P.sb([128, 128], BF16, "yT") for _ in range(2)]
            zbank = [P.ps([128, 512], F32, "zbank") for _ in range(2)] + pbank
            tbank = [P.ps([128, 512], BF16, "tbank") for _ in range(2)]
            obank = [P.ps([128, 128], F32, "obank") for _ in range(2)]
            ones512 = cst
            def p1(i, j):
                    jg = 2 + hp * 2 + j
                    Pp = Pps[j]
                    nk = (i + 1) * 128
                    nch = (nk + 511) // 512
                    E = rot(Eb, "E")
                    w = rot(wb, "w")
                    for c in range(nch):
                        k0 = c * 512
                        kn = min(512, nk - k0)
                        zb = rot(zbank, "zb")
                        P.op("pe", lambda e, zb=zb, j=j, i=i, k0=k0, kn=kn: e.matmul(
                            zb[:, 0:kn], lhsT=qT[j][:, i * 128:(i + 1) * 128], rhs=kT[j][:, k0:k0 + kn],
                            start=True, stop=True), reads=[qT[j], kT[j]], writes=[zb])
                        e_t = rot(et, "et")
                        l_t = rot(lt, "lt")
                        P.op("act", lambda e, e_t=e_t, zb=zb, kn=kn: e.activation(
                            out=e_t[:, 0:kn], in_=zb[:, 0:kn], func=AF.Exp, scale=SCALE), reads=[zb], writes=[e_t])
                        P.op("act", lambda e, e_t=e_t, l_t=l_t, kn=kn: e.activation(
                            out=l_t[:, 0:kn], in_=e_t[:, 0:kn], func=AF.Ln, bias=one_c[:, 0:1], scale=1.0),
                            reads=[e_t, one_c], writes=[l_t])
                        if c == nch - 1:
                            P.op("pool", lambda e, l_t=l_t, kn=kn: e.tensor_tensor(
                                out=l_t[:, kn - 128:kn], in0=l_t[:, kn - 128:kn], in1=cst[:, 128:256], op=ALU.mult),
                                reads=[mask_f], writes=[l_t])
                        init = 0.0 if c == 0 else Pp[:, k0:k0 + 1]
                        P.op("dve", lambda e, l_t=l_t, k0=k0, kn=kn, init=init, Pp=Pp: e.tensor_tensor_scan(
                            out=Pp[:, 1 + k0:1 + k0 + kn], data0=cst[:, 256:256 + kn], data1=l_t[:, 0:kn],
                            initial=init, op0=ALU.mult, op1=ALU.add), reads=[l_t, ones512], writes=[Pp])
                        P.op("dve", lambda e, E=E, zb=zb, k0=k0, kn=kn, Pp=Pp: e.scalar_tensor_tensor(
                            out=E[:, k0:k0 + kn], in0=zb[:, 0:kn], scalar=SCALE, in1=Pp[:, k0:k0 + kn],
                            op0=ALU.mult, op1=ALU.add), reads=[zb, Pp], writes=[E])
                    nt = rot(negT, "negT")
                    P.op("dve", lambda e, nt=nt, nk=nk, Pp=Pp: e.tensor_scalar(
                        out=nt[:], in0=Pp[:, nk:nk + 1], scalar1=-1.0, scalar2=None, op0=ALU.mult),
                        reads=[Pp], writes=[nt])
                    return dict(i=i, j=j, jg=jg, nk=nk, nch=nch, E=E, w=w, nt=nt)

            def p2(st_):
                    i, j, jg, nk, nch, E, w, nt = (st_[k_] for k_ in ('i', 'j', 'jg', 'nk', 'nch', 'E', 'w', 'nt'))
                    for c in range(nch):
                        k0 = c * 512
                        kn = min(512, nk - k0)
                        P.op("act", lambda e, w=w, E=E, nt=nt, k0=k0, kn=kn: e.activation(
                            out=w[:, k0:k0 + kn], in_=E[:, k0:k0 + kn], func=AF.Exp, bias=nt[:, 0:1], scale=1.0),
                            reads=[E, nt], writes=[w])
                    P.op("pool", lambda e, w=w, nk=nk: e.tensor_tensor(
                        out=w[:, nk - 128:nk], in0=w[:, nk - 128:nk], in1=mask_bf[:], op=ALU.mult),
                        reads=[mask_bf], writes=[w])
                    ob = rot(obank, "ob")
                    groups = list(range(0, i + 1, 4))

                    def emit_T(s0):
                        ns = min(4, i + 1 - s0)
                        tb_ = rot(tbank, "tb")
                        P.group("pe", [(lambda e, tb_=tb_, w=w, s0=s0, q=q: e.transpose(
                            out=tb_[:, q * 128:(q + 1) * 128], in_=w[:, (s0 + q) * 128:(s0 + q + 1) * 128],
                            identity=ident_bf[:])) for q in range(ns)], reads=[w, ident_bf], writes=[tb_])
                        wT = rot(wTt, "wT")
                        P.op("dve", lambda e, wT=wT, tb_=tb_, ns=ns: e.tensor_copy(
                            out=wT[:, 0:ns * 128], in_=tb_[:, 0:ns * 128]), reads=[tb_], writes=[wT])
                        return (s0, ns, wT)

                    def emit_PV(g_):
                        s0, ns, wT = g_
                        P.group("pe", [(lambda e, ob=ob, wT=wT, s0=s0, q=q, j=j, i=i: e.matmul(
                            ob[:], lhsT=v[:, s0 + q, j * 128:(j + 1) * 128], rhs=wT[:, q * 128:(q + 1) * 128],
                            start=(s0 + q == 0), stop=(s0 + q == i))) for q in range(ns)],
                            reads=[v, wT], writes=[ob])

                    pg_ = None
                    for s0 in groups:
                        g_ = emit_T(s0)
                        if pg_ is not None:
                            emit_PV(pg_)
                        pg_ = g_
                    emit_PV(pg_)
                    y = rot(yT, "yT")
                    P.op("act", lambda e, y=y, ob=ob: e.copy(out=y[:], in_=ob[:]), reads=[ob], writes=[y])
                    P.dma("act", ymT[(tile_of(jg), i)], ymT[(tile_of(jg), i)].h[tile_of(jg), :, i * 128:(i + 1) * 128], y, y[:], is_output=is_output)

            pend_ = None
            for i in range(32):
                for j in range(2):
                    st_ = p1(i, j)
                    if pend_ is not None:
                        p2(pend_)
                    pend_ = st_
            p2(pend_)
            P.barrier()
    P.es = base_es

import math

L = 4096
KC = 16
NGP = 16
PI = math.pi
MAGIC = 12582912.0
TWO_PI_S = 2 * math.pi * (1 - 2e-6)


def emit_ossm(P, hT, wu_d, prm_d, bsm_d, ct_d, dd_d, cst_d, ysT, ys_ap=None, is_output=True):
    cnt = {}

    def rot(lst, key):
        i = cnt.get(key, 0)
        cnt[key] = i + 1
        return lst[i % len(lst)]

    cst = P.sb([128, 1152], F32, "cst")
    P.dma("sp", cst, cst[:], cst_d, cst_d.h[:, :])
    ident = cst
    prm = P.sb([128, 3, 16], F32, "prm")
    P.dma("sp", prm, prm[:], prm_d, prm_d.h[:, :, :])
    bsm = P.sb([128, 2, 16, 32], F32, "bsm")
    P.dma("sp", bsm, bsm[:], bsm_d, bsm_d.h[:, :, :, :])
    ctf = P.sb([128, 2, 16, 32], F32, "ctf")
    P.dma("sp", ctf, ctf[:], ct_d, ct_d.h[:, :, :, :])
    ddf = P.sb([32, 16, 32], F32, "ddf")
    P.dma("sp", ddf, ddf[:], dd_d, dd_d.h[:, :, :])
    wus = P.sb([128, NGP, KC, 32], BF16, "wus")
    for gp in range(NGP):
        P.dma("pool", wus, wus[:, gp], wu_d, wu_d.h[gp])
    negpi = P.sb([128, 1], F32, "negpi")
    P.op("pool", lambda e: e.memset(negpi[:], -PI), writes=[negpi])

    def small(name):
        return P.sb([128, 16], F32, name)

    dt, adt, th, dec, sa, ca, sn, cs = [small(n) for n in ["dt", "adt", "th", "dec", "sa", "ca", "sn", "cs"]]
    lre, lim, nr, den, f_re, f_im, tA, tB = [small(n) for n in ["lre", "lim", "nr", "den", "fre", "fim", "tA", "tB"]]
    a_re = lambda: prm[:, 0, :]
    a_im = lambda: prm[:, 1, :]
    P.op("act", lambda e: e.activation(out=dt[:], in_=prm[:, 2, :], func=AF.Exp), reads=[prm], writes=[dt])
    P.op("dve", lambda e: e.tensor_tensor(out=adt[:], in0=a_re(), in1=dt[:], op=ALU.mult), reads=[prm, dt], writes=[adt])
    P.op("dve", lambda e: e.tensor_tensor(out=th[:], in0=a_im(), in1=dt[:], op=ALU.mult), reads=[prm, dt], writes=[th])
    P.op("act", lambda e: e.activation(out=dec[:], in_=adt[:], func=AF.Exp), reads=[adt], writes=[dec])
    thn = small("thn")
    P.op("dve", lambda e: e.tensor_scalar(out=thn[:], in0=th[:], scalar1=1.0 / (2 * PI), scalar2=None, op0=ALU.mult),
         reads=[th], writes=[thn])

    def emit_sincos(src, dst_sin, dst_cos, mk):
        for (off, dst) in ((0.0, dst_sin), (0.25, dst_cos)):
            a2 = mk()
            k_ = mk()
            P.op("dve", lambda e, a2=a2, off=off: e.tensor_scalar(out=a2[:], in0=src[:], scalar1=off, scalar2=None, op0=ALU.add),
                 reads=[src], writes=[a2])
            P.op("dve", lambda e, a2=a2, k_=k_: e.tensor_scalar(out=k_[:], in0=a2[:], scalar1=MAGIC, scalar2=MAGIC,
                                                                 op0=ALU.add, op1=ALU.subtract), reads=[a2], writes=[k_])
            P.op("dve", lambda e, a2=a2, k_=k_: e.tensor_tensor(out=a2[:], in0=a2[:], in1=k_[:], op=ALU.subtract),
                 reads=[k_], writes=[a2])
            P.op("act", lambda e, a2=a2, dst=dst: e.activation(out=dst[:], in_=a2[:], func=AF.Sin, scale=TWO_PI_S),
                 reads=[a2], writes=[dst])

    emit_sincos(thn, sn, cs, lambda: small("sc_tmp"))
    P.op("dve", lambda e: e.tensor_tensor(out=lre[:], in0=dec[:], in1=cs[:], op=ALU.mult), reads=[dec, cs], writes=[lre])
    P.op("dve", lambda e: e.tensor_tensor(out=lim[:], in0=dec[:], in1=sn[:], op=ALU.mult), reads=[dec, sn], writes=[lim])
    P.op("dve", lambda e: e.tensor_scalar(out=nr[:], in0=lre[:], scalar1=-1.0, scalar2=None, op0=ALU.add), reads=[lre], writes=[nr])
    P.op("dve", lambda e: e.tensor_tensor(out=tA[:], in0=a_re(), in1=a_re(), op=ALU.mult), reads=[prm], writes=[tA])
    P.op("dve", lambda e: e.tensor_tensor(out=tB[:], in0=a_im(), in1=a_im(), op=ALU.mult), reads=[prm], writes=[tB])
    P.op("dve", lambda e: e.tensor_tensor(out=den[:], in0=tA[:], in1=tB[:], op=ALU.add), reads=[tA, tB], writes=[den])
    P.op("dve", lambda e: e.reciprocal(out=den[:], in_=den[:]), reads=[], writes=[den])
    P.op("dve", lambda e: e.tensor_tensor(out=tA[:], in0=nr[:], in1=a_re(), op=ALU.mult), reads=[nr, prm], writes=[tA])
    P.op("dve", lambda e: e.tensor_tensor(out=tB[:], in0=lim[:], in1=a_im(), op=ALU.mult), reads=[lim, prm], writes=[tB])
    P.op("dve", lambda e: e.tensor_tensor(out=tA[:], in0=tA[:], in1=tB[:], op=ALU.add), reads=[tB], writes=[tA])
    P.op("dve", lambda e: e.tensor_tensor(out=f_re[:], in0=tA[:], in1=den[:], op=ALU.mult), reads=[tA, den], writes=[f_re])
    P.op("dve", lambda e: e.tensor_tensor(out=tA[:], in0=lim[:], in1=a_re(), op=ALU.mult), reads=[lim, prm], writes=[tA])
    P.op("dve", lambda e: e.tensor_tensor(out=tB[:], in0=nr[:], in1=a_im(), op=ALU.mult), reads=[nr, prm], writes=[tB])
    P.op("dve", lambda e: e.tensor_tensor(out=tA[:], in0=tA[:], in1=tB[:], op=ALU.subtract), reads=[tB], writes=[tA])
    P.op("dve", lambda e: e.tensor_tensor(out=f_im[:], in0=tA[:], in1=den[:], op=ALU.mult), reads=[tA, den], writes=[f_im])

    BT = [P.sb([32, NGP, 128], BF16, "BTre"), P.sb([32, NGP, 128], BF16, "BTim")]
    xt_ = [P.sb([128, 32], F32, "xt_") for _ in range(2)]
    xo = [P.sb([128, 32], F32, "xo") for _ in range(2)]
    tps = [P.ps([32, 128], F32, "tps") for _ in range(1)]
    for gp in range(NGP):
        for ri in range(2):
            t_ = rot(xt_, "xt_")
            o_ = rot(xo, "xo")
            if ri == 0:
                P.op("dve", lambda e, t_=t_, gp=gp: e.tensor_scalar(
                    out=t_[:], in0=bsm[:, 1, gp, :], scalar1=f_im[:, gp:gp + 1], scalar2=None, op0=ALU.mult),
                    reads=[bsm, f_im], writes=[t_])
                P.op("dve", lambda e, t_=t_, o_=o_, gp=gp: e.scalar_tensor_tensor(
                    out=o_[:], in0=bsm[:, 0, gp, :], scalar=f_re[:, gp:gp + 1], in1=t_[:],
                    op0=ALU.mult, op1=ALU.subtract), reads=[bsm, f_re, t_], writes=[o_])
            else:
                P.op("dve", lambda e, t_=t_, gp=gp: e.tensor_scalar(
                    out=t_[:], in0=bsm[:, 0, gp, :], scalar1=f_im[:, gp:gp + 1], scalar2=None, op0=ALU.mult),
                    reads=[bsm, f_im], writes=[t_])
                P.op("dve", lambda e, t_=t_, o_=o_, gp=gp: e.scalar_tensor_tensor(
                    out=o_[:], in0=bsm[:, 1, gp, :], scalar=f_re[:, gp:gp + 1], in1=t_[:],
                    op0=ALU.mult, op1=ALU.add), reads=[bsm, f_re, t_], writes=[o_])
            tp = rot(tps, "tps")
            P.op("pe", lambda e, tp=tp, o_=o_: e.transpose(out=tp[:], in_=o_[:], identity=cst[:, 0:128]),
                 reads=[o_, cst], writes=[tp])
            P.op("act", lambda e, tp=tp, ri=ri, gp=gp: e.copy(out=BT[ri][:, gp, :], in_=tp[:]),
                 reads=[tp], writes=[BT[ri]])
    ctb = [P.sb([128, NGP, 32], BF16, "ctre"), P.sb([128, NGP, 32], BF16, "ctimn")]
    P.op("dve", lambda e: e.tensor_copy(out=ctb[0][:], in_=ctf[:, 0]), reads=[ctf], writes=[ctb[0]])
    P.op("dve", lambda e: e.tensor_scalar(out=ctb[1][:], in0=ctf[:, 1], scalar1=-1.0, scalar2=None, op0=ALU.mult),
         reads=[ctf], writes=[ctb[1]])
    ddb = P.sb([32, NGP, 32], BF16, "ddb")
    P.op("dve", lambda e: e.tensor_copy(out=ddb[:], in_=ddf[:]), reads=[ddf], writes=[ddb])

    cosT = [P.sb([128, 512], F32, "cosT") for _ in range(NGP)]
    sinT = [P.sb([128, 512], F32, "sinT") for _ in range(NGP)]
    decT = [P.sb([128, 512], F32, "decT") for _ in range(2)]
    ang = [P.sb([128, 512], F32, "ang") for _ in range(2)]
    arg = [P.sb([128, 512], F32, "arg") for _ in range(4)]
    for gp in range(NGP):
        a_ = rot(ang, "ang")
        P.op("dve", lambda e, a_=a_, gp=gp: e.tensor_scalar(
            out=a_[:], in0=cst[:, 128:640], scalar1=thn[:, gp:gp + 1], scalar2=None, op0=ALU.mult),
            reads=[cst, thn], writes=[a_])
        emit_sincos(a_, sinT[gp], cosT[gp], lambda: rot(arg, "arg"))

    carry = [[P.sb([128, 1], F32, f"car{ri}") for ri in range(2)] for _ in range(NGP)]
    for gp in range(NGP):
        for ri in range(2):
            P.op("pool", lambda e, gp=gp, ri=ri: e.memset(carry[gp][ri][:], 0.0), writes=[carry[gp][ri]])
    hch = [P.sb([128, KC, 512], BF16, "hch") for _ in range(2)]
    pu_b = [P.ps([32, 512], F32, "pu") for _ in range(2)]
    pr_b = [P.ps([128, 512], F32, "pr") for _ in range(2)]
    pi_b = [P.ps([128, 512], F32, "pi") for _ in range(2)]
    py_b = [P.ps([32, 512], F32, "py") for _ in range(1)]
    ugs = [P.sb([32, 512], BF16, "ug") for _ in range(2)]
    tt = [P.sb([128, 512], F32, "tt") for _ in range(8)]
    mm = [P.sb([128, 512], F32, "mm") for _ in range(4)]
    ww = [P.sb([128, 512], F32, "ww") for _ in range(4)]
    sf = [P.sb([128, 512], F32, "sf") for _ in range(4)]
    sbf = [P.sb([128, 512], BF16, "sbf") for _ in range(4)]
    ybs = [P.sb([32, 512], BF16, "yb") for _ in range(2)]
    pend_ = None
    for tc in range(L // 512):
        ht = rot(hch, "hch")
        for kq in range(4):
            P.dma("sp", ht, ht[:, kq * 4:(kq + 1) * 4, :], hT,
                  hT.h[kq * 4:(kq + 1) * 4, :, tc * 512:(tc + 1) * 512].rearrange("k p t -> p k t"))
        def p1(tc, gp, ht):
                pu = rot(pu_b, "pu")
                P.group("pe", [(lambda e, kc=kc, pu=pu, gp=gp, ht=ht: e.matmul(
                    pu[:], lhsT=wus[:, gp, kc, :], rhs=ht[:, kc, :], start=(kc == 0), stop=(kc == KC - 1)))
                    for kc in range(KC)], reads=[wus, ht], writes=[pu])
                ug = rot(ugs, "ug")
                P.op("act", lambda e, ug=ug, pu=pu: e.copy(out=ug[:], in_=pu[:]), reads=[pu], writes=[ug])
                pr = rot(pr_b, "pr")
                pi_ = rot(pi_b, "pi")
                P.op("pe", lambda e, pr=pr, ug=ug, gp=gp: e.matmul(pr[:], lhsT=BT[0][:, gp, :], rhs=ug[:], start=True, stop=True),
                     reads=[BT[0], ug], writes=[pr])
                P.op("pe", lambda e, pi_=pi_, ug=ug, gp=gp: e.matmul(pi_[:], lhsT=BT[1][:, gp, :], rhs=ug[:], start=True, stop=True),
                     reads=[BT[1], ug], writes=[pi_])
                cT, sT, dT = cosT[gp], sinT[gp], rot(decT, "decT")
                P.op("act", lambda e, gp=gp, dT=dT: e.activation(
                    out=dT[:], in_=cst[:, 640:1152], func=AF.Copy, scale=dec[:, gp:gp + 1]),
                    reads=[cst, dec], writes=[dT])
                t1, t2, t3, t4 = [rot(tt, "tt") for _ in range(4)]
                P.op("dve", lambda e, t1=t1, pr=pr, cT=cT: e.tensor_tensor(out=t1[:], in0=pr[:], in1=cT[:], op=ALU.mult),
                     reads=[pr, cT], writes=[t1])
                P.op("dve", lambda e, t2=t2, pi_=pi_, sT=sT: e.tensor_tensor(out=t2[:], in0=pi_[:], in1=sT[:], op=ALU.mult),
                     reads=[pi_, sT], writes=[t2])
                P.op("dve", lambda e, t3=t3, pi_=pi_, cT=cT: e.tensor_tensor(out=t3[:], in0=pi_[:], in1=cT[:], op=ALU.mult),
                     reads=[pi_, cT], writes=[t3])
                P.op("dve", lambda e, t4=t4, pr=pr, sT=sT: e.tensor_tensor(out=t4[:], in0=pr[:], in1=sT[:], op=ALU.mult),
                     reads=[pr, sT], writes=[t4])
                m_re, m_im = rot(mm, "mm"), rot(mm, "mm")
                P.op("pool", lambda e, m_re=m_re, t1=t1, t2=t2: e.tensor_tensor(out=m_re[:], in0=t1[:], in1=t2[:], op=ALU.add),
                     reads=[t1, t2], writes=[m_re])
                P.op("pool", lambda e, m_im=m_im, t3=t3, t4=t4: e.tensor_tensor(out=m_im[:], in0=t3[:], in1=t4[:], op=ALU.subtract),
                     reads=[t3, t4], writes=[m_im])
                w_re, w_im = rot(ww, "ww"), rot(ww, "ww")
                for (w_, m_, ri) in ((w_re, m_re, 0), (w_im, m_im, 1)):
                    P.op("dve", lambda e, w_=w_, m_=m_, ri=ri, gp=gp, dT=dT: e.tensor_tensor_scan(
                        out=w_[:], data0=dT[:], data1=m_[:], initial=carry[gp][ri][:, 0:1], op0=ALU.mult, op1=ALU.add),
                        reads=[dT, m_, carry[gp][ri]], writes=[w_])
                a1, a2, a3, a4 = [rot(tt, "tt") for _ in range(4)]
                P.op("pool", lambda e, a1=a1, w_re=w_re, cT=cT: e.tensor_tensor(out=a1[:], in0=w_re[:], in1=cT[:], op=ALU.mult),
                     reads=[w_re, cT], writes=[a1])
                P.op("pool", lambda e, a2=a2, w_im=w_im, sT=sT: e.tensor_tensor(out=a2[:], in0=w_im[:], in1=sT[:], op=ALU.mult),
                     reads=[w_im, sT], writes=[a2])
                P.op("pool", lambda e, a3=a3, w_re=w_re, sT=sT: e.tensor_tensor(out=a3[:], in0=w_re[:], in1=sT[:], op=ALU.mult),
                     reads=[w_re, sT], writes=[a3])
                P.op("pool", lambda e, a4=a4, w_im=w_im, cT=cT: e.tensor_tensor(out=a4[:], in0=w_im[:], in1=cT[:], op=ALU.mult),
                     reads=[w_im, cT], writes=[a4])
                s_re, s_im = rot(sf, "sf"), rot(sf, "sf")
                P.op("dve", lambda e, s_re=s_re, a1=a1, a2=a2: e.tensor_tensor(out=s_re[:], in0=a1[:], in1=a2[:], op=ALU.subtract),
                     reads=[a1, a2], writes=[s_re])
                P.op("dve", lambda e, s_im=s_im, a3=a3, a4=a4: e.tensor_tensor(out=s_im[:], in0=a3[:], in1=a4[:], op=ALU.add),
                     reads=[a3, a4], writes=[s_im])
                sb_re, sb_im = rot(sbf, "sbf"), rot(sbf, "sbf")
                for (sb_, s_, ri) in ((sb_re, s_re, 0), (sb_im, s_im, 1)):
                    P.op("act", lambda e, sb_=sb_, s_=s_: e.copy(out=sb_[:], in_=s_[:]), reads=[s_], writes=[sb_])
                    P.op("act", lambda e, s_=s_, gp=gp, ri=ri: e.copy(out=carry[gp][ri][:], in_=s_[:, 511:512]),
                         reads=[s_], writes=[carry[gp][ri]])
                return dict(tc=tc, gp=gp, ug=ug, sb_re=sb_re, sb_im=sb_im)

        def p2(st_):
                tc, gp, ug, sb_re, sb_im = (st_[k_] for k_ in ('tc', 'gp', 'ug', 'sb_re', 'sb_im'))
                py = rot(py_b, "py")
                P.group("pe", [
                    (lambda e, py=py, sb_re=sb_re, gp=gp: e.matmul(py[:], lhsT=ctb[0][:, gp, :], rhs=sb_re[:], start=True, stop=False)),
                    (lambda e, py=py, sb_im=sb_im, gp=gp: e.matmul(py[:], lhsT=ctb[1][:, gp, :], rhs=sb_im[:], start=False, stop=False)),
                    (lambda e, py=py, ug=ug, gp=gp: e.matmul(py[:], lhsT=ddb[:, gp, :], rhs=ug[:], start=False, stop=True)),
                ], reads=[ctb[0], ctb[1], ddb, sb_re, sb_im, ug], writes=[py])
                yb = rot(ybs, "yb")
                P.op("act", lambda e, yb=yb, py=py: e.activation(out=yb[:], in_=py[:], func=AF.Gelu), reads=[py], writes=[yb])
                P.dma("act", ysT[(gp, tc)], (ys_ap(gp, tc) if ys_ap else ysT[(gp, tc)].h[gp, :, tc * 512:(tc + 1) * 512]), yb, yb[:], is_output=is_output)

        for gp in range(NGP):
            st_ = p1(tc, gp, ht)
            if pend_ is not None:
                p2(pend_)
            pend_ = st_
    p2(pend_)


KC = 16
EPS = 1e-6


def emit_omix2(P, hT, yss, wzu_d, wzv_d, glw_d, glb_d, sgn_d, wsT_d, sgb_d, mle_d, ymx, ntok, sel=None, alt_off=0):
    cnt = {}

    def rot(lst, key):
        i = cnt.get(key, 0)
        cnt[key] = i + 1
        return lst[i % len(lst)]

    wzu = [P.sb([128, KC, 128], BF16, "wzu") for _ in range(8)]
    wzv = [P.sb([128, KC, 512], BF16, "wzv") for _ in range(2)]
    glw = [P.sb([128, 8, 128], BF16, "glw") for _ in range(8)]
    for m in range(8):
        P.dma("pool", wzu[m], wzu[m][:], wzu_d, wzu_d.h[m])
        P.dma("pool", glw[m], glw[m][:], glw_d, glw_d.h[m])
    for hf in range(2):
        for q in range(4):
            P.dma("pool", wzv[hf], wzv[hf][:, q * 4:(q + 1) * 4, :], wzv_d, wzv_d.h[hf, :, q * 4:(q + 1) * 4, :])
    glb = P.sb([128, 8], F32, "glb")
    P.dma("sp", glb, glb[:], glb_d, glb_d.h[:, :])
    sgn = P.sb([128, 1024], F32, "sgn")
    P.dma("sp", sgn, sgn[:], sgn_d, sgn_d.h[:, :])
    wsf = P.sb([128, 8, 128], F32, "wsf")
    P.dma("sp", wsf, wsf[:], wsT_d, wsT_d.h[:, :, :])
    sgb = P.sb([128, 8, 128], F32, "sgb")
    P.dma("sp", sgb, sgb[:], sgb_d, sgb_d.h[:, :, :])
    mle = P.sb([128, 128], F32, "mle")
    P.dma("sp", mle, mle[:], mle_d, mle_d.h[:, :])
    wsb = P.sb([128, 8, 128], BF16, "wsb")
    for hd in range(8):
        P.op("dve", lambda e, hd=hd: e.tensor_tensor(out=wsb[:, hd, :], in0=wsf[:, hd, :], in1=mle[:], op=ALU.mult),
             reads=[wsf, mle], writes=[wsb])
    epst = P.sb([128, 1], F32, "epst")
    P.op("pool", lambda e: e.memset(epst[:], EPS), writes=[epst])

    hch = [P.sb([128, KC, 512], BF16, "hch") for _ in range(2)]
    ych = [P.sb([128, 8, 512], BF16, "ych") for _ in range(2)]
    uT = P.sb([128, 8, 512], F32, "uT")
    vg = [P.sb([128, 1024], F32, "vg") for _ in range(2)]
    sqv = P.sb([128, 1024], F32, "sqv")
    ss = [P.sb([128, 1], F32, "ss") for _ in range(2)]
    rs = [P.sb([128, 1], F32, "rs") for _ in range(2)]
    vtm = [P.sb([128, 1024], BF16, "vtm") for _ in range(2)]
    sig = [P.sb([128, 512], F32, "sig") for _ in range(2)]
    tmp4 = [P.sb([128, 4, 128], F32, "tmp4") for _ in range(2)]
    yo = [P.sb([128, 512], BF16, "yo") for _ in range(3)]
    yo4 = [P.sb([128, 4, 512], BF16, "yo4") for _ in range(2)]
    bank = [P.ps([128, 512], F32, "bk") for _ in range(4)]
    bank4 = [P.ps([128, 4, 128], F32, "bk4") for _ in range(2)]

    for c in range(ntok // 512):
        tsl = slice(c * 512, (c + 1) * 512)
        ht = rot(hch, "hch")
        for kq in range(4):
            P.dma("sp", ht, ht[:, kq * 4:(kq + 1) * 4, :], hT,
                  hT.h[kq * 4:(kq + 1) * 4, :, tsl].rearrange("k p t -> p k t"))
        yt = rot(ych, "ych")
        for kq in range(2):
            P.dma("sp", yt, yt[:, kq * 4:(kq + 1) * 4, :], yss,
                  yss.h[kq * 4:(kq + 1) * 4, :, tsl].rearrange("k p t -> p k t"))
        if sel is not None:
            asl = slice(alt_off + c * 512, alt_off + (c + 1) * 512)
            hb_ = rot(hch, "hch")
            for kq in range(4):
                P.dma("sp", hb_, hb_[:, kq * 4:(kq + 1) * 4, :], hT,
                      hT.h[kq * 4:(kq + 1) * 4, :, asl].rearrange("k p t -> p k t"))
            yb_ = rot(ych, "ych")
            for kq in range(2):
                P.dma("sp", yb_, yb_[:, kq * 4:(kq + 1) * 4, :], yss,
                      yss.h[kq * 4:(kq + 1) * 4, :, asl].rearrange("k p t -> p k t"))
            for (a_, b_) in ((ht, hb_), (yt, yb_)):
                P.op("pool", lambda e, a_=a_: e.tensor_scalar(out=a_[:], in0=a_[:], scalar1=sel[:, 0:1], scalar2=None,
                                                              op0=ALU.mult), reads=[sel], writes=[a_])
                P.op("dve", lambda e, a_=a_, b_=b_: e.scalar_tensor_tensor(
                    out=a_[:], in0=b_[:], scalar=sel[:, 1:2], in1=a_[:], op0=ALU.mult, op1=ALU.add),
                    reads=[b_, sel], writes=[a_])
        for m in range(8):
            bk = rot(bank, "bk")
            P.group("pe", [(lambda e, kc=kc, bk=bk, m=m, yt=yt: e.matmul(
                bk[:], lhsT=glw[m][:, kc, :], rhs=yt[:, kc, :], start=(kc == 0), stop=(kc == 7)))
                for kc in range(8)], reads=[glw[m], yt], writes=[bk])
            sg = rot(sig, "sig")
            P.op("act", lambda e, sg=sg, bk=bk, m=m: e.activation(
                out=sg[:], in_=bk[:], func=AF.Sigmoid, bias=glb[:, m:m + 1], scale=1.0),
                reads=[bk, glb], writes=[sg])
            y_ = rot(yo, "yo")
            P.op("dve", lambda e, y_=y_, sg=sg, yt=yt, m=m: e.tensor_tensor(
                out=y_[:], in0=sg[:], in1=yt[:, m, :], op=ALU.mult), reads=[sg, yt], writes=[y_])
            P.dma("act", ymx[(m, c)], ymx[(m, c)].h[m, :, tsl], y_, y_[:])
        for m in range(8):
            bk = rot(bank, "bk")
            P.group("pe", [(lambda e, kc=kc, bk=bk, m=m, ht=ht: e.matmul(
                bk[:], lhsT=wzu[m][:, kc, :], rhs=ht[:, kc, :], start=(kc == 0), stop=(kc == KC - 1)))
                for kc in range(KC)], reads=[wzu[m], ht], writes=[bk])
            P.op("act", lambda e, bk=bk, m=m: e.activation(out=uT[:, m, :], in_=bk[:], func=AF.Gelu),
                 reads=[bk], writes=[uT])
        y4 = [rot(yo4, "yo4") for _ in range(2)]
        for tb in range(4):
            vg_ = rot(vg, "vg")
            for hf in range(2):
                bk = rot(bank, "bk")
                P.group("pe", [(lambda e, kc=kc, bk=bk, hf=hf, ht=ht, tb=tb: e.matmul(
                    bk[:], lhsT=ht[:, kc, tb * 128:(tb + 1) * 128], rhs=wzv[hf][:, kc, :],
                    start=(kc == 0), stop=(kc == KC - 1))) for kc in range(KC)],
                    reads=[wzv[hf], ht], writes=[bk])
                P.op("act", lambda e, bk=bk, vg_=vg_, hf=hf: e.activation(
                    out=vg_[:, hf * 512:(hf + 1) * 512], in_=bk[:], func=AF.Gelu), reads=[bk], writes=[vg_])
            ss_ = rot(ss, "ss")
            rs_ = rot(rs, "rs")
            P.op("dve", lambda e, vg_=vg_: e.tensor_tensor(out=sqv[:], in0=vg_[:], in1=vg_[:], op=ALU.mult),
                 reads=[vg_], writes=[sqv])
            P.op("dve", lambda e, ss_=ss_: e.reduce_sum(out=ss_[:], in_=sqv[:], axis=AX.X), reads=[sqv], writes=[ss_])
            P.op("act", lambda e, ss_=ss_, rs_=rs_: e.activation(
                out=rs_[:], in_=ss_[:], func=AF.Sqrt, bias=epst[:, 0:1], scale=1.0 / 1024),
                reads=[ss_, epst], writes=[rs_])
            P.op("dve", lambda e, rs_=rs_: e.reciprocal(out=rs_[:], in_=rs_[:]), reads=[], writes=[rs_])
            vt = rot(vtm, "vtm")
            P.op("dve", lambda e, vt=vt, vg_=vg_, rs_=rs_: e.scalar_tensor_tensor(
                out=vt[:], in0=vg_[:], scalar=rs_[:, 0:1], in1=sgn[:], op0=ALU.mult, op1=ALU.mult),
                reads=[vg_, rs_, sgn], writes=[vt])
            for j in range(2):
                b4 = rot(bank4, "bk4")
                P.group("pe", [(lambda e, b4=b4, vt=vt, j=j, q=q: e.matmul(
                    b4[:, q, :], lhsT=vt[:, (4 * j + q) * 128:(4 * j + q + 1) * 128], rhs=wsb[:, 4 * j + q, :],
                    start=True, stop=True)) for q in range(4)], reads=[vt, wsb], writes=[b4])
                t4 = rot(tmp4, "tmp4")
                P.op("dve", lambda e, t4=t4, b4=b4, j=j: e.tensor_tensor(
                    out=t4[:], in0=b4[:], in1=sgb[:, 4 * j:4 * j + 4, :], op=ALU.add), reads=[b4, sgb], writes=[t4])
                P.op("dve", lambda e, t4=t4, j=j, tb=tb, y4=y4: e.tensor_tensor(
                    out=y4[j][:, :, tb * 128:(tb + 1) * 128], in0=t4[:],
                    in1=uT[:, 4 * j:4 * j + 4, tb * 128:(tb + 1) * 128], op=ALU.mult),
                    reads=[t4, uT], writes=[y4[j]])
        for j in range(2):
            for q in range(4):
                m = 8 + 4 * j + q
                P.dma("act", ymx[(m, c)], ymx[(m, c)].h[m, :, tsl], y4[j], y4[j][:, q, :])

NCORE = 8
NT = 2048
BF = ml_dtypes.bfloat16


def _bass():
    return bass.Bass("TRN2", target_bir_lowering=False)


def _ffn_w_inputs(P, sfx):
    wg = P.dram("wg" + sfx, [FT, 128, KC, 128], F32, "ExternalInput")
    wu = P.dram("wu" + sfx, [FT, 128, KC, 128], F32, "ExternalInput")
    wd = P.dram("wd" + sfx, [KC, 128, FT, 128], F32, "ExternalInput")
    return wg, wu, wd


def _load_g(P, name):
    gd = P.dram(name, [128, 96], F32, "ExternalInput")
    g = P.sb([128, 96], F32, name)
    P.dma("sp", g, g[:], gd, gd.h[:, :])
    return g


LSEQ = 4096
NACT = 8
LT = 2048


def _whole(P, name, shape, dt, kind):
    base = P.dram(name, shape, dt, kind)

    class _D(dict):
        def __missing__(self, k):
            return base
    d = _D()
    d["is_output"] = (kind == "ExternalOutput")
    d["base"] = base
    return d


def _ym_loader(P, res, ymd):
    def ym(c):
        t = res["hAB"][c % 2]
        off = (c % 2) * 512
        for kq in range(4):
            P.dma("sp", t, t.h[:, kq * 4:(kq + 1) * 4, off:off + 512], ymd,
                  ymd.h[kq * 4:(kq + 1) * 4, :, c * 512:(c + 1) * 512].rearrange("k p t -> p k t"))
        return _View(t, off)
    return ym


class _View:
    def __init__(self, t, off):
        self.t = t
        self.off = off
        self.w = t.w
        self.r = t.r
        self.name = t.name

    def __getitem__(self, idx):
        a, b, c = idx
        assert c == slice(None)
        return self.t.h[a, b, self.off:self.off + 512]


STOP_AFTER = 99
SKIP = set()


def _on(k):
    return STOP_AFTER >= k and k not in SKIP


def build_FUSED():
    nc = _bass()
    NH = LSEQ // 512
    with ExitStack() as es:
        P = Prog(nc, es)
        base = P.es
        x_in = regions(P, "xT", [KC, 128, LSEQ], F32, "ExternalInput", KC, NH)
        xo = regions(P, "xoT", [KC, 128, LT], F32, "ExternalOutput", KC, LT // 512)
        xs = [regions(P, f"x{i}s", [KC, 128, LSEQ], F32, "Internal", KC, NH) for i in range(1, 5)]
        x1, x2, x3, x4 = xs
        x5 = regions(P, "x5s", [KC, 128, LT], F32, "Internal", KC, LT // 512)
        seld = P.dram("selh", [128, 2], F32, "ExternalInput")
        h1 = regions(P, "h1s", [KC, 128, LSEQ], BF16, "Internal", KC, NH)
        h2 = regions(P, "h2s", [KC, 128, LSEQ], BF16, "Internal", KC, NH)
        ymT = regions(P, "ymTs", [KC, 128, LSEQ], BF16, "Internal", KC, 32)
        ymx = regions(P, "ymxs", [KC, 128, LT], BF16, "Internal", KC, LT // 512)
        ysd = regions(P, "yss", [8, 128, LSEQ], BF16, "Internal", 32, NH)
        wf = [_ffn_w_inputs(P, str(i)) for i in range(4)]
        gd = [P.dram(f"g{i}", [128, 96], F32, "ExternalInput") for i in range(2)]
        em = []
        for hh in range(2):
            s = f"_{hh}"
            em.append(dict(
                wq=P.dram("wq" + s, [6, 128, KC, 128], F32, "ExternalInput"),
                wk=P.dram("wk" + s, [6, 128, KC, 128], F32, "ExternalInput"),
                wv=P.dram("wv" + s, [3, 128, KC, 256], F32, "ExternalInput"),
                wp=P.dram("wp" + s, [2, 128, KC, 128], F32, "ExternalInput"),
                pw=P.dram("pw" + s, [2, 128, 128], F32, "ExternalInput"),
                psc=P.dram("psc" + s, [128, 2], F32, "ExternalInput"),
                selw=P.dram("selw" + s, [128, 2, 21], F32, "ExternalInput")))
        cst = P.dram("cst", [128, 768], F32, "ExternalInput")
        wo0 = P.dram("wo0", [KC, 128, KC, 128], F32, "ExternalInput")
        wo1 = P.dram("wo1", [KC, 128, KC, 128], F32, "ExternalInput")
        om = []
        for hh in range(2):
            s = f"_{hh}"
            om.append(dict(
                wu=P.dram("swu" + s, [NGP, 128, KC, 32], F32, "ExternalInput"),
                prm=P.dram("prm" + s, [128, 3, 16], F32, "ExternalInput"),
                bsm=P.dram("bsm" + s, [128, 2, 16, 32], F32, "ExternalInput"),
                ct=P.dram("ct" + s, [128, 2, 16, 32], F32, "ExternalInput"),
                dd=P.dram("dd" + s, [32, 16, 32], F32, "ExternalInput")))
        cst2 = P.dram("cst2", [128, 1152], F32, "ExternalInput")
        o2 = dict(
            wzu=P.dram("wzu", [8, 128, KC, 128], F32, "ExternalInput"),
            wzv=P.dram("wzv", [2, 128, KC, 512], F32, "ExternalInput"),
            glw=P.dram("glw", [8, 128, 8, 128], F32, "ExternalInput"),
            glb=P.dram("glb", [128, 8], F32, "ExternalInput"),
            sgn=P.dram("sgn", [128, 1024], F32, "ExternalInput"),
            wsT=P.dram("wsT", [128, 8, 128], F32, "ExternalInput"),
            sgb=P.dram("sgb", [128, 8, 128], F32, "ExternalInput"),
            mle=P.dram("mle", [128, 128], F32, "ExternalInput"))
        fscr = [regions(P, f"fscr{i}", [KC, 128, LSEQ], F32, "Internal", KC, NH) for i in range(2)]
        fscr.append(regions(P, "fscr2", [KC, 128, LT], F32, "Internal", KC, LT // 512))

        def load_g(i):
            g = P.sb([128, 96], F32, f"g{i}")
            P.dma("sp", g, g[:], gd[i], gd[i].h[:, :])
            return g

        def ffn_res(fs):
            res = ffn_resources(P, LSEQ, scr_name=None, f_scr=fs)
            return res

        with ExitStack() as sc:
            P.es = sc
            C = Common(P)
            g0 = load_g(0)
            res = ffn_res(fscr[0])
            emit_ffn_pipelined(P, C, res, x_in, x1, wf[0][0], wf[0][1], wf[0][2], g0, 0, 1, LSEQ, h_out=h1, jnext=2)
            P.barrier()
        for hh in range(2 if _on(2) else 0):
            with ExitStack() as sc:
                P.es = sc
                e_ = em[hh]
                emit_emix(P, {(0, c): h1["base"] for c in range(16)}, e_["wq"], e_["wk"], e_["wv"], e_["wp"],
                          e_["pw"], e_["psc"], cst, ymT, e_["selw"],
                          tile_of=(lambda j, hh=hh: (2 * hh + j) if j < 2 else (4 + 6 * hh + j - 2)),
                          is_output=False)
                P.barrier()
        for _ in range(1 if _on(3) else 0):
          with ExitStack() as sc:
            P.es = sc
            C = Common(P)
            g0 = load_g(0)
            g1 = load_g(1)
            res = ffn_res(fscr[1])
            emit_outproj(P, C, res, _ym_loader(P, res, ymT["base"]), wo0, x1, x2, g0, 3, LSEQ)
            emit_ffn_pipelined(P, C, res, x2, x3, wf[1][0], wf[1][1], wf[1][2], g0, 4, 5, LSEQ)
            emit_ffn_pipelined(P, C, res, x3, x4, wf[2][0], wf[2][1], wf[2][2], g1, 0, 1, LSEQ, h_out=h2, jnext=2)
            P.barrier()
        for hh in range(2 if _on(4) else 0):
            with ExitStack() as sc:
                P.es = sc
                o_ = om[hh]

                def ys_ap(gp, tc, hh=hh):
                    gg = 16 * hh + gp
                    return ysd["base"].h[gg // 4, (gg % 4) * 32:(gg % 4) * 32 + 32, tc * 512:(tc + 1) * 512]
                ysT = {(gp, tc): ysd[(16 * hh + gp, tc)] for gp in range(NGP) for tc in range(NH)}
                emit_ossm(P, h2["base"], o_["wu"], o_["prm"], o_["bsm"], o_["ct"], o_["dd"], cst2, ysT,
                          ys_ap=ys_ap, is_output=False)
                P.barrier()
        for _ in range(1 if _on(5) else 0):
          with ExitStack() as sc:
            P.es = sc
            selt = P.sb([128, 2], F32, "selt")
            P.dma("sp", selt, selt[:], seld, seld.h[:, :])
            emit_omix2(P, h2["base"], ysd["base"], o2["wzu"], o2["wzv"], o2["glw"], o2["glb"], o2["sgn"],
                       o2["wsT"], o2["sgb"], o2["mle"], ymx, LT, sel=selt, alt_off=LT)
            P.barrier()
        for _ in range(1 if _on(5) else 0):
          with ExitStack() as sc:
            P.es = sc
            C = Common(P)
            g1 = load_g(1)
            res = ffn_resources(P, LT, scr_name=None, f_scr=fscr[2])
            selt = P.sb([128, 2], F32, "selt")
            P.dma("sp", selt, selt[:], seld, seld.h[:, :])

            def ym(c):
                t = res["hAB"][c % 2]
                off = (c % 2) * 512
                for kc in range(KC):
                    P.dma("sp", t, t.h[:, kc, off:off + 512], ymx[(kc, c)], ymx[(kc, c)].h[kc, :, c * 512:(c + 1) * 512])
                return _View(t, off)
            emit_outproj(P, C, res, ym, wo1, x4, x5, g1, 3, LT, sel=selt, alt_chunks=LT // 512)
            emit_ffn_pipelined(P, C, res, x5, xo, wf[3][0], wf[3][1], wf[3][2], g1, 4, 5, LT)
            P.barrier()
        P.es = base
        P.finish()
        P.emit()
    return nc


def _colT(cols):
    K = cols.shape[0] // 128
    n = cols.shape[1] // 128
    return np.ascontiguousarray(cols.reshape(K, 128, n, 128).transpose(2, 1, 0, 3))


def _ffn_layout(w_gate, w_up, w_down, sfx):
    return {"wg" + sfx: _colT(w_gate), "wu" + sfx: _colT(w_up),
            "wd" + sfx: np.ascontiguousarray(w_down.reshape(FT, 128, KC, 128).transpose(2, 1, 0, 3))}


def _g_layout(gn):
    return np.ascontiguousarray(gn.reshape(6, KC, 128).transpose(2, 0, 1).reshape(128, 96))


def _fm(a):
    return np.ascontiguousarray(a.T).reshape(a.shape[1] // 128, 128, a.shape[0])


def _emix_inputs(w_in, pool_w, pool_scale, hh):
    heads = range(6 * hh, 6 * hh + 6)
    qc = np.concatenate([w_in[:, 512 + h * 128: 512 + (h + 1) * 128] for h in heads], 1)
    kc = np.concatenate([w_in[:, 512 + 1536 + h * 128: 512 + 1536 + (h + 1) * 128] for h in heads], 1)
    vc = np.concatenate([w_in[:, 512 + 3072 + h * 128: 512 + 3072 + (h + 1) * 128] for h in heads], 1)
    pc = w_in[:, hh * 256:(hh + 1) * 256]
    wv = np.ascontiguousarray(vc.reshape(KC, 128, 3, 256).transpose(2, 1, 0, 3))
    pw = np.ascontiguousarray(pool_w[2 * hh:2 * hh + 2])
    psc = np.ascontiguousarray(pool_scale[hh * 256:(hh + 1) * 256].reshape(2, 128).T)
    selw = np.zeros((128, 2, 21), np.float32)
    for gi in range(2):
        k = 2 * hh + gi
        w = POOL_WINDOWS[k]
        selw[:, gi, k] = 1.0
        selw[:, gi, 4] = 1.0 / w
        selw[:, gi, 5:21] = (1.0 / np.minimum(np.arange(1, 17), w))[None, :]
    cst = np.zeros((128, 768), np.float32)
    cst[:, 0:128] = np.eye(128)
    cst[:, 128:256] = np.tril(np.ones((128, 128)), -1)
    cst[:, 256:768] = 1.0
    return {"wq": _colT(qc), "wk": _colT(kc), "wv": wv, "wp": _colT(pc), "pw": pw, "psc": psc,
            "selw": selw, "cst": cst}


def _ossm_inputs(w_in, a_re, a_im, log_dt, b_re, b_im, c_re, c_im, d, hh):
    G0 = 32 * hh
    wu = w_in[:, G0 * 16:(G0 + 32) * 16]
    wu = np.ascontiguousarray(wu.reshape(KC, 128, NGP, 32).transpose(2, 1, 0, 3))
    prm = np.zeros((128, 3, 16), np.float32)
    bsm = np.zeros((128, 2, 16, 32), np.float32)
    ct = np.zeros((128, 2, 16, 32), np.float32)
    dd = np.zeros((32, 16, 32), np.float32)
    for gp in range(NGP):
        for gl in range(2):
            g = G0 + 2 * gp + gl
            sl = slice(gl * 64, (gl + 1) * 64)
            prm[sl, 0, gp] = a_re[g]
            prm[sl, 1, gp] = a_im[g]
            prm[sl, 2, gp] = log_dt[g]
            bsm[sl, 0, gp, gl * 16:(gl + 1) * 16] = b_re[g]
            bsm[sl, 1, gp, gl * 16:(gl + 1) * 16] = b_im[g]
            ct[sl, 0, gp, gl * 16:(gl + 1) * 16] = c_re[g].T
            ct[sl, 1, gp, gl * 16:(gl + 1) * 16] = c_im[g].T
            idx = np.arange(16)
            dd[gl * 16 + idx, gp, gl * 16 + idx] = d[g * 16:(g + 1) * 16]
    c2 = np.zeros((128, 1152), np.float32)
    c2[:, 0:128] = np.eye(128)
    c2[:, 128:640] = np.arange(1, 513, dtype=np.float32)[None, :]
    c2[:, 640:1152] = 1.0
    return {"wu": wu, "prm": prm, "bsm": bsm, "ct": ct, "dd": dd, "cst2": c2}


def _omix2_inputs(w_in, glu_w, glu_b, sgu_norm_g, sgu_w, sgu_b):
    zu = w_in[:, 1024:2048]
    zv = w_in[:, 2048:3072]
    wzv = np.ascontiguousarray(zv.reshape(KC, 128, 2, 512).transpose(2, 1, 0, 3))
    glw = _colT(glu_w)
    glb = np.ascontiguousarray(glu_b.reshape(8, 128).T)
    sgn = np.ascontiguousarray(np.broadcast_to(sgu_norm_g[None, :], (128, 1024)))
    wsT = np.ascontiguousarray(sgu_w.transpose(2, 0, 1))
    sgb = np.ascontiguousarray(np.broadcast_to(sgu_b[None, :, :], (128, 8, 128)))
    mle = np.triu(np.ones((128, 128), np.float32))
    return {"wzu": _colT(zu), "wzv": wzv, "glw": glw, "glb": glb, "sgn": sgn, "wsT": wsT, "sgb": sgb, "mle": mle}


_PROGS = {}


def _make_inputs(x, norm_g, ffn_w_gate, ffn_w_up, ffn_w_down, ev_w_in, ev_pool_w, ev_pool_scale, ev_w_out,
           od_w_in, od_ssm_a_re, od_ssm_a_im, od_ssm_log_dt, od_ssm_b_re, od_ssm_b_im, od_ssm_c_re,
           od_ssm_c_im, od_ssm_d, od_glu_w, od_glu_b, od_sgu_norm_g, od_sgu_w, od_sgu_b, od_w_out):
    f = lambda a: np.asarray(a, dtype=np.float32)
    x = f(x)
    norm_g, ffn_w_gate, ffn_w_up, ffn_w_down = f(norm_g), f(ffn_w_gate), f(ffn_w_up), f(ffn_w_down)
    B, Lq, Dm = x.shape
    xts = [_fm(x[b]) for b in range(B)]
    shared = {}
    for i, (l, j) in enumerate(((0, 0), (0, 1), (1, 0), (1, 1))):
        shared.update(_ffn_layout(ffn_w_gate[l, j], ffn_w_up[l, j], ffn_w_down[l, j], str(i)))
    shared["g0"] = _g_layout(norm_g[0])
    shared["g1"] = _g_layout(norm_g[1])
    for hh in range(2):
        e_ = _emix_inputs(f(ev_w_in[0]), f(ev_pool_w[0]), f(ev_pool_scale[0]), hh)
        shared["cst"] = e_.pop("cst")
        for k, v in e_.items():
            shared[f"{k}_{hh}"] = v
        o_ = _ossm_inputs(f(od_w_in[0]), f(od_ssm_a_re[0]), f(od_ssm_a_im[0]), f(od_ssm_log_dt[0]),
                          f(od_ssm_b_re[0]), f(od_ssm_b_im[0]), f(od_ssm_c_re[0]), f(od_ssm_c_im[0]),
                          f(od_ssm_d[0]), hh)
        shared["cst2"] = o_.pop("cst2")
        shared[f"swu_{hh}"] = o_.pop("wu")
        for k, v in o_.items():
            shared[f"{k}_{hh}"] = v
    shared["wo0"] = _colT(f(ev_w_out[0]))
    shared["wo1"] = _colT(f(od_w_out[0]))
    shared.update(_omix2_inputs(f(od_w_in[0]), f(od_glu_w[0]), f(od_glu_b[0]), f(od_sgu_norm_g[0]),
                                f(od_sgu_w[0]), f(od_sgu_b[0])))
    ims = []
    for c in range(NACT):
        b, th = c // 2, c % 2
        selh = np.zeros((128, 2), np.float32)
        selh[:, th] = 1.0
        ims.append(dict(shared, xT=xts[b], selh=selh))
    return ims


def kernel(**inputs):
    x = np.asarray(inputs["x"])
    B, Lq, Dm = x.shape
    ims = _make_inputs(**inputs)
    if "F" not in _PROGS:
        _PROGS["F"] = build_FUSED()
    r = run_bass_kernel_spmd(_PROGS["F"], ims, core_ids=list(range(NACT)))
    out = np.empty((B, Lq, Dm), np.float32)
    for c in range(NACT):
        b, th = c // 2, c % 2
        out[b, th * LT:(th + 1) * LT] = r.results[c]["xoT"].reshape(Dm, LT).T
    return out
```
